# Optimizing a Trainium2 kernel written in Bass

```python
import math
import jax
import jax.numpy as jnp
from jax import lax
import numpy as np

D_MODEL = 2048
BATCH = 2
SEQ = 8192
DEPTH = 4

GRID_W = 64
CTX_LEN = 256

HY_W = 1024
HY_ORDER = 2
HY_EMB = 33
HY_BANDS = (HY_EMB - 1) // 2
HY_HID = 64
HY_FAST_DECAY = 0.3
HY_SLOW_DECAY = 1.5
HY_TARGET = 1e-2

MLA_HEADS = 8
MLA_NOPE = 128
MLA_ROPE = 64
MLA_V = 128
MLA_Q_RANK = 512
MLA_KV_RANK = 256
MLA_SCALE = (MLA_NOPE + MLA_ROPE) ** -0.5
Q_BLOCK = 128
ROPE_BASE = 10000.0

O_HY = 3 * HY_W
O_Q = O_HY + MLA_Q_RANK
O_KV = O_Q + MLA_KV_RANK
IN_W = O_KV + MLA_ROPE

S5_GC = 16
S5_NG = D_MODEL // S5_GC
S5_P = 64
S5_GB = 16
S5_NBLK = S5_NG // S5_GB

PEER_HEADS = 8
PEER_NK = 128
PEER_N = PEER_NK * PEER_NK
PEER_DK = 256
PEER_TOPK = 16
PEER_BLOCK = 128

N_EVEN = (DEPTH + 1) // 2
N_ODD = DEPTH // 2
DN_ALPHA = (2.0 * DEPTH) ** 0.25
DN_BETA = (8.0 * DEPTH) ** -0.25
LN_EPS = 1e-5
RMS_EPS = 1e-6
F32 = jnp.float32

kernel_name = 'hybrid_hyena_mla_s5_peer_dit'


def layer_norm(x, g, b):
    xf = x.astype(F32)
    xc = xf - jnp.mean(xf, axis=-1, keepdims=True)
    var = jnp.mean(xc * xc, axis=-1, keepdims=True)
    return (xc * lax.rsqrt(var + LN_EPS) * g + b).astype(x.dtype)


def rms_norm(x, g):
    xf = x.astype(F32)
    return (xf * lax.rsqrt(jnp.mean(xf * xf, axis=-1, keepdims=True) + RMS_EPS) * g).astype(x.dtype)


def axial_rope(L):
    rows = L // GRID_W
    row = jnp.repeat(jnp.arange(rows), GRID_W).astype(F32)
    col = jnp.tile(jnp.arange(GRID_W), rows).astype(F32)
    half = MLA_ROPE // 2
    inv = ROPE_BASE ** (-jnp.arange(0, half, 2, dtype=F32) / half)
    ar, ac = row[:, None] * inv, col[:, None] * inv
    ang = jnp.concatenate([ar, ar, ac, ac], axis=-1)
    return jnp.cos(ang), jnp.sin(ang)


def apply_axial_rope(x, cos, sin):
    xs = x.reshape(x.shape[:-1] + (2, 2, MLA_ROPE // 4))
    rot = jnp.concatenate([-xs[..., 1:, :], xs[..., :1, :]], axis=-2).reshape(x.shape)
    return (x * cos + rot * sin).astype(x.dtype)


def short_conv3(z, w, b):
    L = z.shape[1]
    zp = jnp.pad(z, ((0, 0), (1, 1), (0, 0)))
    return zp[:, :L] * w[0] + zp[:, 1:L + 1] * w[1] + zp[:, 2:] * w[2] + b


def hyena_pos_features(L):
    t = jnp.linspace(0.0, 1.0, L, dtype=F32)[:, None]
    w = (2.0 * math.pi / L) * jnp.arange(L, dtype=F32)[:, None]
    f = jnp.linspace(1e-4, HY_BANDS - 1, HY_BANDS, dtype=F32)[None, :]
    return jnp.concatenate([t, jnp.cos(f * w), -jnp.sin(f * w)], axis=-1)


def hyena_filter_spectrum(L, w1, b1, w2, b2, w3, b3, w4, freq):
    act = lambda a: jnp.sin(freq * a)
    hdn = act(hyena_pos_features(L) @ w1 + b1)
    hdn = act(hdn @ w2 + b2)
    hdn = act(hdn @ w3 + b3)
    h = (hdn @ w4).astype(F32).reshape(L, 2, HY_ORDER, HY_W)
    deltas = jnp.abs(jnp.linspace(math.log(HY_TARGET) / HY_SLOW_DECAY, math.log(HY_TARGET) / HY_FAST_DECAY,
                                  HY_ORDER * HY_W, dtype=F32)).reshape(HY_ORDER, HY_W)
    t = jnp.linspace(0.0, 1.0, L, dtype=F32)[:, None, None, None]
    h = h * jnp.exp(-t * deltas)
    h = h / jnp.sum(jnp.abs(h), axis=(0, 1), keepdims=True)
    k = jnp.concatenate([h[:, 0], jnp.zeros((1, HY_ORDER, HY_W), F32), h[1:, 1][::-1]], axis=0)
    return jnp.fft.rfft(k, axis=0)


def fft_long_conv(u, k_f):
    L = u.shape[1]
    u_f = jnp.fft.rfft(u.astype(F32), n=2 * L, axis=1)
    return jnp.fft.irfft(u_f * k_f[None], n=2 * L, axis=1)[:, :L]


def hyena_mixer(z, conv_w, conv_b, w1, b1, w2, b2, w3, b3, w4, freq, bias):
    L = z.shape[1]
    zc = short_conv3(z, conv_w, conv_b)
    x1, x2, v = jnp.split(zc, 3, axis=-1)
    k_f = hyena_filter_spectrum(L, w1, b1, w2, b2, w3, b3, w4, freq)
    y = v.astype(F32)
    for o, gate in enumerate((x1, x2)):
        y = gate * (fft_long_conv(y, k_f[:, o]) + y * bias[o])
    return y.astype(z.dtype)


def mla_heads(z, q_norm, w_uq, kv_norm, w_ukv):
    Bn, L, _ = z.shape
    q = (rms_norm(z[..., O_HY:O_Q], q_norm) @ w_uq).reshape(Bn, L, MLA_HEADS, MLA_NOPE + MLA_ROPE)
    kv = (rms_norm(z[..., O_Q:O_KV], kv_norm) @ w_ukv).reshape(Bn, L, MLA_HEADS, MLA_NOPE + MLA_V)
    return q[..., :MLA_NOPE], q[..., MLA_NOPE:], kv[..., :MLA_NOPE], z[..., O_KV:], kv[..., MLA_NOPE:]


def mla_attend(qn, qr, kn, kr, v):
    s = jnp.einsum('bqhd,bkhd->bhqk', qn, kn) + jnp.einsum('bqhr,bkr->bhqk', qr, kr)
    p = jax.nn.softmax(s.astype(F32) * MLA_SCALE, axis=-1).astype(v.dtype)
    return jnp.einsum('bhqk,bkhd->bqhd', p, v)


def even_mixer(in_ctx, in_lat, cos, sin, need_ctx, w_in, hy, mla, w_o):
    z_ctx = in_ctx @ w_in
    z_lat = in_lat @ w_in
    cqn, cqr, ckn, ckr, cv = mla_heads(z_ctx, *mla)
    lqn, lqr, lkn, lkr, lv = mla_heads(z_lat, *mla)
    lqr = apply_axial_rope(lqr, cos[:, None, :], sin[:, None, :])
    lkr = apply_axial_rope(lkr, cos, sin)
    kn = jnp.concatenate([ckn, lkn], axis=1)
    kr = jnp.concatenate([ckr, lkr], axis=1)
    vv = jnp.concatenate([cv, lv], axis=1)
    Bn, L = in_lat.shape[:2]
    nb = L // Q_BLOCK
    blocks = lambda t: t.reshape((Bn, nb, Q_BLOCK) + t.shape[2:]).swapaxes(0, 1)
    att = lax.map(lambda q: mla_attend(q[0], q[1], kn, kr, vv), (blocks(lqn), blocks(lqr)))
    att = att.swapaxes(0, 1).reshape(Bn, L, MLA_HEADS * MLA_V)
    y_lat = jnp.concatenate([hyena_mixer(z_lat[..., :O_HY], *hy), att.astype(z_lat.dtype)], axis=-1) @ w_o
    y_ctx = None
    if need_ctx:
        att_c = mla_attend(cqn, cqr, ckn, ckr, cv).reshape(Bn, in_ctx.shape[1], MLA_HEADS * MLA_V)
        y_ctx = jnp.concatenate([hyena_mixer(z_ctx[..., :O_HY], *hy), att_c.astype(z_ctx.dtype)], axis=-1) @ w_o
    return y_ctx, y_lat


def _scan_op(e1, e2):
    a1, b1 = e1
    a2, b2 = e2
    return a1 * a2, a2 * b1 + b2


def linear_scan(a_bar, bu, s0):
    bu = bu.at[:, 0].add(a_bar * s0)
    a = jnp.broadcast_to(a_bar, bu.shape)
    _, s = lax.associative_scan(_scan_op, (a, bu), axis=1)
    return s


def _rev(s, d):
    return s[:, ::-1] if d == 1 else s


def s5_mixer(in_ctx, in_lat, need_ctx, lam_re, lam_im, log_dt, b_re, b_im, c_re, c_im, d_skip, w_o, w_g):
    def to_blocks(h):
        Bn, L, _ = h.shape
        return h.astype(F32).reshape(Bn, L, S5_NBLK, S5_GB, S5_GC).transpose(2, 0, 1, 3, 4)

    def from_blocks(y):
        nbk, Bn, L = y.shape[:3]
        return y.transpose(1, 2, 0, 3, 4).reshape(Bn, L, D_MODEL)

    def par_blocks(a):
        a = a.astype(F32)
        return a.reshape((2, S5_NBLK, S5_GB) + a.shape[2:]).swapaxes(0, 1)

    lam = lax.complex(par_blocks(lam_re), par_blocks(lam_im))
    dt = jnp.exp(par_blocks(log_dt))
    bmat = lax.complex(par_blocks(b_re), par_blocks(b_im))
    cmat = lax.complex(par_blocks(c_re), par_blocks(c_im))

    def block_fn(args):
        u_ctx, u_lat, lm, dtb, bm, cm = args
        a_bar = jnp.exp(lm * dtb[..., None])
        b_bar = ((a_bar - 1.0) / lm)[..., None] * bm
        zero = jnp.zeros((u_ctx.shape[0], S5_GB, S5_P), jnp.complex64)
        outs_ctx, outs_lat = [], []
        for d in range(2):
            bu_ctx = jnp.einsum('blgc,gpc->blgp', _rev(u_ctx, d).astype(jnp.complex64), b_bar[d])
            bu_lat = jnp.einsum('blgc,gpc->blgp', _rev(u_lat, d).astype(jnp.complex64), b_bar[d])
            s_ctx = linear_scan(a_bar[d], bu_ctx, zero)
            s_lat = linear_scan(a_bar[d], bu_lat, s_ctx[:, -1])
            outs_lat.append(_rev(jnp.einsum('blgp,gcp->blgc', s_lat, cm[d]).real, d))
            if need_ctx:
                outs_ctx.append(_rev(jnp.einsum('blgp,gcp->blgc', s_ctx, cm[d]).real, d))
        y_lat = outs_lat[0] + outs_lat[1]
        if need_ctx:
            return y_lat, outs_ctx[0] + outs_ctx[1]
        return (y_lat,)

    outs = lax.map(block_fn, (to_blocks(in_ctx), to_blocks(in_lat), lam, dt, bmat, cmat))

    def finish(yb, h):
        y = from_blocks(yb) + d_skip * h.astype(F32)
        y = jax.nn.gelu(y).astype(h.dtype)
        return (y @ w_o) * jax.nn.sigmoid(y @ w_g)

    y_lat = finish(outs[0], in_lat)
    y_ctx = finish(outs[1], in_ctx) if need_ctx else None
    return y_ctx, y_lat


def peer_ffn(h, w_q, keys, u_tab, v_tab):
    Bn, L, D = h.shape
    hb = h.reshape(Bn * L // PEER_BLOCK, PEER_BLOCK, D)

    def block_fn(xb):
        q = (xb @ w_q).reshape(PEER_BLOCK, PEER_HEADS, 2, PEER_DK // 2)
        s = jnp.einsum('thsk,hsnk->thsn', q, keys).astype(F32)
        top_s, top_i = lax.top_k(s, PEER_TOPK)
        cand = top_s[:, :, 0, :, None] + top_s[:, :, 1, None, :]
        cand_s, cand_j = lax.top_k(cand.reshape(PEER_BLOCK, PEER_HEADS, PEER_TOPK * PEER_TOPK), PEER_TOPK)
        idx = (jnp.take_along_axis(top_i[:, :, 0], cand_j // PEER_TOPK, axis=-1) * PEER_NK
               + jnp.take_along_axis(top_i[:, :, 1], cand_j % PEER_TOPK, axis=-1))
        g = jax.nn.softmax(cand_s, axis=-1)
        act = jax.nn.gelu(jnp.einsum('thkd,td->thk', u_tab[idx], xb).astype(F32))
        return jnp.einsum('thk,thkd->td', (g * act).astype(xb.dtype), v_tab[idx])

    return lax.map(block_fn, hb).reshape(Bn, L, D)


def setup_inputs(seed: int = 0) -> dict:
    key = jax.random.key(seed)
    ks = list(jax.random.split(key, 48))

    def nrm(shape, scale):
        return jax.random.normal(ks.pop(), shape, F32) * scale

    def gain(shape):
        return 1.0 + nrm(shape, 0.02)

    D = D_MODEL
    return {
        'x': nrm((BATCH, SEQ, D), 1.0),
        'c': nrm((BATCH, D), 1.0),
        'ctx': nrm((BATCH, CTX_LEN, D), 1.0),
        'c_ctx': nrm((D,), 1.0),
        'mod_w': nrm((DEPTH, D, 6 * D), 0.5 * D ** -0.5),
        'mod_b': nrm((DEPTH, 6 * D), 0.02),
        'ln_mix_g': gain((DEPTH, D)),
        'ln_mix_b': nrm((DEPTH, D), 0.02),
        'ln_ffn_g': gain((DEPTH, D)),
        'ln_ffn_b': nrm((DEPTH, D), 0.02),
        'ev_w_in': nrm((N_EVEN, D, IN_W), D ** -0.5),
        'ev_conv_w': nrm((N_EVEN, 3, O_HY), 0.5),
        'ev_conv_b': nrm((N_EVEN, O_HY), 0.02),
        'hy_w1': nrm((N_EVEN, HY_EMB, HY_HID), HY_EMB ** -0.5),
        'hy_b1': nrm((N_EVEN, HY_HID), 0.02),
        'hy_w2': nrm((N_EVEN, HY_HID, HY_HID), HY_HID ** -0.5),
        'hy_b2': nrm((N_EVEN, HY_HID), 0.02),
        'hy_w3': nrm((N_EVEN, HY_HID, HY_HID), HY_HID ** -0.5),
        'hy_b3': nrm((N_EVEN, HY_HID), 0.02),
        'hy_w4': nrm((N_EVEN, HY_HID, 2 * HY_ORDER * HY_W), HY_HID ** -0.5),
        'hy_freq': gain((N_EVEN, HY_HID)),
        'hy_bias': nrm((N_EVEN, HY_ORDER, HY_W), 0.1),
        'mla_q_norm': gain((N_EVEN, MLA_Q_RANK)),
        'mla_w_uq': nrm((N_EVEN, MLA_Q_RANK, MLA_HEADS * (MLA_NOPE + MLA_ROPE)), MLA_Q_RANK ** -0.5),
        'mla_kv_norm': gain((N_EVEN, MLA_KV_RANK)),
        'mla_w_ukv': nrm((N_EVEN, MLA_KV_RANK, MLA_HEADS * (MLA_NOPE + MLA_V)), MLA_KV_RANK ** -0.5),
        'ev_w_o': nrm((N_EVEN, D, D), DN_BETA * D ** -0.5),
        's5_lam_re': -0.5 + nrm((N_ODD, 2, S5_NG, S5_P), 0.01),
        's5_lam_im': math.pi * jnp.arange(S5_P, dtype=F32) + nrm((N_ODD, 2, S5_NG, S5_P), 0.01),
        's5_log_dt': jax.random.uniform(ks.pop(), (N_ODD, 2, S5_NG), F32, math.log(1e-3), math.log(1e-1)),
        's5_b_re': nrm((N_ODD, 2, S5_NG, S5_P, S5_GC), (2 * S5_GC) ** -0.5),
        's5_b_im': nrm((N_ODD, 2, S5_NG, S5_P, S5_GC), (2 * S5_GC) ** -0.5),
        's5_c_re': nrm((N_ODD, 2, S5_NG, S5_GC, S5_P), S5_P ** -0.5),
        's5_c_im': nrm((N_ODD, 2, S5_NG, S5_GC, S5_P), S5_P ** -0.5),
        's5_d': nrm((N_ODD, D), 1.0),
        'od_w_o': nrm((N_ODD, D, D), DN_BETA * D ** -0.5),
        'od_w_g': nrm((N_ODD, D, D), D ** -0.5),
        'peer_w_q': nrm((DEPTH, D, PEER_HEADS * PEER_DK), D ** -0.5),
        'peer_keys': nrm((DEPTH, PEER_HEADS, 2, PEER_NK, PEER_DK // 2), (PEER_DK // 2) ** -0.5),
        'peer_u': nrm((DEPTH, PEER_N, D), D ** -0.5),
        'peer_v': nrm((DEPTH, PEER_N, D), DN_BETA * PEER_HEADS ** -0.5),
    }


def reference(x, c, ctx, c_ctx, mod_w, mod_b, ln_mix_g, ln_mix_b, ln_ffn_g, ln_ffn_b,
              ev_w_in, ev_conv_w, ev_conv_b, hy_w1, hy_b1, hy_w2, hy_b2, hy_w3, hy_b3, hy_w4,
              hy_freq, hy_bias, mla_q_norm, mla_w_uq, mla_kv_norm, mla_w_ukv, ev_w_o,
              s5_lam_re, s5_lam_im, s5_log_dt, s5_b_re, s5_b_im, s5_c_re, s5_c_im, s5_d,
              od_w_o, od_w_g, peer_w_q, peer_keys, peer_u, peer_v):
    cos, sin = axial_rope(x.shape[1])
    act_lat = jax.nn.silu(c)
    act_ctx = jax.nn.silu(c_ctx)
    h_lat, h_ctx = x, ctx
    for layer in range(DEPTH):
        need_ctx = layer < DEPTH - 1
        m_lat = jnp.split((act_lat @ mod_w[layer] + mod_b[layer])[:, None, :], 6, axis=-1)
        m_ctx = jnp.split((act_ctx @ mod_w[layer] + mod_b[layer])[None, None, :], 6, axis=-1)
        in_lat = h_lat * (1.0 + m_lat[1]) + m_lat[0]
        in_ctx = h_ctx * (1.0 + m_ctx[1]) + m_ctx[0]
        i = layer // 2
        if layer % 2 == 0:
            hy = (ev_conv_w[i], ev_conv_b[i], hy_w1[i], hy_b1[i], hy_w2[i], hy_b2[i], hy_w3[i], hy_b3[i],
                  hy_w4[i], hy_freq[i], hy_bias[i])
            mla = (mla_q_norm[i], mla_w_uq[i], mla_kv_norm[i], mla_w_ukv[i])
            y_ctx, y_lat = even_mixer(in_ctx, in_lat, cos, sin, need_ctx, ev_w_in[i], hy, mla, ev_w_o[i])
        else:
            y_ctx, y_lat = s5_mixer(in_ctx, in_lat, need_ctx, s5_lam_re[i], s5_lam_im[i], s5_log_dt[i],
                                    s5_b_re[i], s5_b_im[i], s5_c_re[i], s5_c_im[i], s5_d[i],
                                    od_w_o[i], od_w_g[i])
        h_lat = layer_norm(DN_ALPHA * h_lat + m_lat[2] * y_lat, ln_mix_g[layer], ln_mix_b[layer])
        f_lat = peer_ffn(h_lat * (1.0 + m_lat[4]) + m_lat[3], peer_w_q[layer], peer_keys[layer],
                         peer_u[layer], peer_v[layer])
        h_lat = layer_norm(DN_ALPHA * h_lat + m_lat[5] * f_lat, ln_ffn_g[layer], ln_ffn_b[layer])
        if need_ctx:
            h_ctx = layer_norm(DN_ALPHA * h_ctx + m_ctx[2] * y_ctx, ln_mix_g[layer], ln_mix_b[layer])
            f_ctx = peer_ffn(h_ctx * (1.0 + m_ctx[4]) + m_ctx[3], peer_w_q[layer], peer_keys[layer],
                             peer_u[layer], peer_v[layer])
            h_ctx = layer_norm(DN_ALPHA * h_ctx + m_ctx[5] * f_ctx, ln_ffn_g[layer], ln_ffn_b[layer])
    return h_lat
```

```python
import math
from contextlib import ExitStack

import numpy as np
import concourse.bass as bass
import concourse.mybir as mybir
from concourse.bass_utils import run_bass_kernel_spmd

F32 = mybir.dt.float32
I32 = mybir.dt.int32
U32 = mybir.dt.uint32
ALU = mybir.AluOpType
AF = mybir.ActivationFunctionType
AX = mybir.AxisListType

NCORES = 8
VERBOSE = False


class Prog:
    ENGS = ("sp", "act", "pe", "dve", "pool")
    NDMA = 12

    def __init__(self, nc, stack):
        self.nc = nc
        self.stack = stack
        self.ops = {e: [] for e in self.ENGS}
        self.cnt = {e: 0 for e in self.ENGS}
        self.sem = {e: stack.enter_context(nc.semaphore("sem_" + e)) for e in self.ENGS}
        self.known = {e: {} for e in self.ENGS}
        self.dma_sems = {}
        self.dma_n = {}
        for q in ("sp", "act", "pool"):
            self.dma_sems[q] = [stack.enter_context(nc.semaphore("dma_%s_%d" % (q, i))) for i in range(self.NDMA)]
            self.dma_n[q] = 0
        self.dma_uses = {}
        self.access = {}
        self.semobj = {}
        self.out_tokens = []
        self._uid = 0
        self.pstack = stack
        self.consts = {}

    def phase(self):
        prog = self

        class _Ph:
            def __enter__(self_):
                import time as _t
                self_.t0 = _t.time()
                prog.pstack = ExitStack()
                prog.pstack.__enter__()
                prog.consts = {}
                return prog

            def __exit__(self_, et, ev, tb):
                import time as _t
                t1 = _t.time()
                nops = sum(len(v) for v in prog.ops.values())
                if et is None:
                    prog.finish()
                if VERBOSE:
                    print("phase: %d ops, record %.1fs, emit %.1fs, max access list %d" % (
                        nops, t1 - self_.t0, _t.time() - t1, max([len(v) for v in prog.access.values()] + [0])), flush=True)
                prog.pstack.__exit__(et, ev, tb)
                prog.pstack = prog.stack
                prog.ops = {e: [] for e in prog.ENGS}
                prog.access = {}
                return False
        return _Ph()

    def sb(self, name, shape, dtype=F32):
        self._uid += 1
        return self.pstack.enter_context(self.nc.sbuf_tensor("%s_%d" % (name, self._uid), list(shape), dtype))

    def ps(self, name, shape=(128, 512), dtype=F32):
        self._uid += 1
        return self.pstack.enter_context(self.nc.psum_tensor("%s_%d" % (name, self._uid), list(shape), dtype))

    @staticmethod
    def _rng(ap):
        t = ap.tensor
        name = t.name
        pat = ap.ap
        off = ap.offset
        if not isinstance(off, int):
            return name, 0, 1 << 60
        space = str(t.space)
        if "DRAM" in space.upper() or "Dram" in space or "dram" in space:
            lo = hi = off
            for st, n in pat:
                if st >= 0:
                    hi += st * (n - 1)
                else:
                    lo += st * (n - 1)
            return name, lo, hi + 1
        pstep = pat[0][0]
        if pstep <= 0 or "PSUM" in space.upper():
            return name, 0, 1 << 60
        lo = hi = off % pstep
        for st, n in pat[1:]:
            if st >= 0:
                hi += st * (n - 1)
            else:
                lo += st * (n - 1)
        return name, lo, hi + 1

    def _wait_token(self, eng, tok, waits):
        sname, val = tok
        if eng == "pe" and sname == "Epe":
            return
        if self.known[eng].get(sname, 0) >= val:
            return
        self.known[eng][sname] = val
        waits.append((self.semobj[sname], val))

    def _deps(self, eng, reads, writes):
        waits = []
        for ap in reads:
            name, lo, hi = self._rng(ap)
            for ent in self.access.get(name, ()):
                if ent[2] and ent[0] < hi and lo < ent[1]:
                    self._wait_token(eng, ent[3], waits)
        for ap in writes:
            name, lo, hi = self._rng(ap)
            for ent in self.access.get(name, ()):
                if ent[0] < hi and lo < ent[1]:
                    self._wait_token(eng, ent[3], waits)
        return waits

    def _record(self, reads, writes, tok):
        for ap in reads:
            name, lo, hi = self._rng(ap)
            lst = self.access.setdefault(name, [])
            lst[:] = [e for e in lst if not (not e[2] and e[0] == lo and e[1] == hi and e[3][0] == tok[0])]
            lst.append([lo, hi, False, tok])
        for ap in writes:
            name, lo, hi = self._rng(ap)
            lst = self.access.setdefault(name, [])
            lst[:] = [e for e in lst if not (lo <= e[0] and e[1] <= hi)]
            lst.append([lo, hi, True, tok])

    def op(self, eng, fn, reads=(), writes=()):
        waits = self._deps(eng, reads, writes)
        self.cnt[eng] += 1
        sname = "E" + eng
        self.semobj[sname] = self.sem[eng]
        tok = (sname, self.cnt[eng])
        self._record(reads, writes, tok)
        self.ops[eng].append((waits, fn, (self.sem[eng], 1)))
        return tok

    def dma(self, q, out, in_, fn=None, is_output=False, slow=False):
        if slow and fn is None:
            fn = lambda e, out=out, in_=in_: e.dma_start(out=out, in_=in_, allow_slow_non_contiguous=True)
        slot = self.dma_n[q] % self.NDMA
        self.dma_n[q] += 1
        sem = self.dma_sems[q][slot]
        sname = "D%s%d" % (q, slot)
        self.semobj[sname] = sem
        uses = self.dma_uses.get(sname, 0)
        waits = self._deps(q, [in_], [out])
        if uses > 0:
            self._wait_token(q, (sname, 16 * uses), waits)
        self.dma_uses[sname] = uses + 1
        tok = (sname, 16 * (uses + 1))
        self._record([in_], [out], tok)
        if fn is None:
            fn = lambda e, out=out, in_=in_: e.dma_start(out=out, in_=in_)
        self.ops[q].append((waits, fn, (sem, 16)))
        if is_output:
            self.out_tokens.append(tok)
        return tok

    def allgather(self, out, in_, inc=1):
        if not hasattr(self, "cc_sem"):
            self.cc_sem = self.stack.enter_context(self.nc.semaphore("cc_sem"))
            self.cc_n = 0
            self.semobj["CC"] = self.cc_sem
        waits = self._deps("pool", [in_], [out])
        self.cc_n += inc
        tok = ("CC", self.cc_n)
        self._record([in_], [out], tok)
        fn = lambda e: e.collective_compute("AllGather", ALU.bypass, replica_groups=[list(range(NCORES))],
                                            ins=[in_], outs=[out])
        self.ops["pool"].append((waits, fn, (self.cc_sem, inc)))
        return tok

    def finish(self):
        final = []
        for e in self.ENGS:
            if self.cnt[e] > 0:
                final.append(("E" + e, self.cnt[e]))
        for sname, uses in self.dma_uses.items():
            final.append((sname, 16 * uses))
        if hasattr(self, "cc_sem") and self.cc_n > 0:
            final.append(("CC", self.cc_n))
        for e in self.ENGS:
            waits = []
            for tok in final:
                if tok[0] == "E" + e and e != "pe":
                    pass
                self._wait_token_force(e, tok, waits)
            self.ops[e].append((waits, None, None))
        nc = self.nc
        with nc.Block() as block:
            def run(eng_name):
                def body(e):
                    for waits, fn, inc in self.ops[eng_name]:
                        for s, v in waits:
                            e.wait_ge(s, v)
                        if fn is not None:
                            ins = fn(e)
                            ins.then_inc(inc[0], inc[1])
                return body
            block.sync(run("sp"))
            block.scalar(run("act"))
            block.tensor(run("pe"))
            block.vector(run("dve"))
            block.gpsimd(run("pool"))

    def _wait_token_force(self, eng, tok, waits):
        sname, val = tok
        if self.known[eng].get(sname, 0) >= val:
            return
        self.known[eng][sname] = val
        waits.append((self.semobj[sname], val))

    def mm(self, out, lhsT, rhs, start=True, stop=True):
        return self.op("pe", lambda e: e.matmul(out, lhsT, rhs, start=start, stop=stop),
                       reads=[lhsT, rhs] + ([] if start else [out]), writes=[out])

    def transpose(self, out, in_, ident):
        return self.op("pe", lambda e: e.transpose(out, in_, ident), reads=[in_, ident], writes=[out])

    def act(self, out, in_, func, bias=None, scale=None, accum_out=None):
        reads = [in_]
        kw = {}
        if bias is not None:
            kw["bias"] = bias
            if not isinstance(bias, (int, float)):
                reads.append(bias)
        if scale is not None:
            kw["scale"] = scale
            if not isinstance(scale, (int, float)):
                reads.append(scale)
        writes = [out]
        if accum_out is not None:
            kw["accum_out"] = accum_out
            writes.append(accum_out)
        return self.op("act", lambda e: e.activation(out, in_, func, **kw), reads=reads, writes=writes)

    def tt(self, eng, out, in0, in1, op):
        return self.op(eng, lambda e: e.tensor_tensor(out, in0, in1, op), reads=[in0, in1], writes=[out])

    def ts(self, eng, out, in0, s1, op0, s2=None, op1=None, accum_out=None):
        reads = [in0]
        if not isinstance(s1, (int, float)):
            reads.append(s1)
        if s2 is not None and not isinstance(s2, (int, float)):
            reads.append(s2)
        writes = [out]
        kw = {}
        if op1 is not None:
            kw["op1"] = op1
        if accum_out is not None:
            kw["accum_out"] = accum_out
            writes.append(accum_out)
        return self.op(eng, lambda e: e.tensor_scalar(out, in0, s1, s2, op0, **kw), reads=reads, writes=writes)

    def stt(self, eng, out, in0, scalar, in1, op0, op1, accum_out=None):
        reads = [in0, in1]
        if not isinstance(scalar, (int, float)):
            reads.append(scalar)
        writes = [out]
        kw = {}
        if accum_out is not None:
            kw["accum_out"] = accum_out
            writes.append(accum_out)
        return self.op(eng, lambda e: e.scalar_tensor_tensor(out, in0, scalar, in1, op0, op1, **kw),
                       reads=reads, writes=writes)

    def copy(self, eng, out, in_):
        if eng == "act":
            return self.op("act", lambda e: e.copy(out, in_), reads=[in_], writes=[out])
        return self.op(eng, lambda e: e.tensor_copy(out, in_), reads=[in_], writes=[out])

    def memset(self, eng, ap, val):
        return self.op(eng, lambda e: e.memset(ap, val), writes=[ap])

    def recip(self, out, in_):
        return self.op("dve", lambda e: e.reciprocal(out, in_), reads=[in_], writes=[out])


D = 2048
KD = D // 128
BATCH = 2
SEQ = 8192
CTX = 256
DEPTH = 4
TOK = 2112
TILES = [(0, 512, 0), (512, 512, 0), (1024, 512, 0), (1536, 512, 0), (2048, 64, 1)]
HY_W = 1024
IN_W = 3904
DN_ALPHA = (2.0 * DEPTH) ** 0.25
LN_EPS = 1e-5
RMS_EPS = 1e-6
MLA_SCALE = (128 + 64) ** -0.5


def new_nc():
    return bass.Bass("TRN2", target_bir_lowering=False)


def run_spmd(nc, in_maps):
    res = run_bass_kernel_spmd(nc, in_maps, core_ids=list(range(len(in_maps))))
    return res.results


def wl_layout(W):
    K, N = W.shape
    assert K % 128 == 0 and N % 128 == 0
    return np.ascontiguousarray(W.reshape(K // 128, 128, N // 128, 128).transpose(2, 1, 0, 3))


class Ctx:
    def __init__(self, P, ident_ap=None):
        self.P = P
        self.ones = P.sb("ones", [128, 128])
        P.memset("pool", self.ones[:], 1.0)
        self.wbufs = [P.sb("wbuf", [128, KD, 128]) for _ in range(3)]
        self.wi = 0
        self.psums = [P.ps("ps") for _ in range(4)]
        self.pi = 0
        self.qi = 0

    def wbuf(self):
        self.wi += 1
        return self.wbufs[self.wi % len(self.wbufs)]

    def psum(self):
        self.pi += 1
        return self.psums[self.pi % len(self.psums)]

    def q(self):
        self.qi += 1
        return ("sp", "act")[self.qi % 2]


def dense_chunk(P, C, w_ap, nk, x, TT, m=128):
    wb = C.wbuf()
    P.dma(C.q(), wb[:, 0:nk, 0:m], w_ap)
    ps = C.psum()
    for k in range(nk):
        P.mm(ps[0:m, 0:TT], wb[:, k, 0:m], x[:, k, 0:TT], start=(k == 0), stop=(k == nk - 1))
    return ps


def col_stats(P, C, x, nk, TT, scale, sq_tmp, center=False):
    ps = C.psum()
    for k in range(nk):
        P.act(sq_tmp[:, k, 0:TT], x[:, k, 0:TT], AF.Square)
    for k in range(nk):
        P.mm(ps[:, 0:TT], C.ones[:], sq_tmp[:, k, 0:TT], start=(k == 0), stop=(k == nk - 1))
    return ps


def rstd_from(P, out, ps, TT, scale, eps):
    P.act(out[:, 0:TT], ps[:, 0:TT], AF.Sqrt, bias=eps_ap(P, eps), scale=scale)
    P.recip(out[:, 0:TT], out[:, 0:TT])


def eps_ap(P, val):
    key = ("eps", val)
    if key not in P.consts:
        t = P.sb("eps", [128, 1])
        P.memset("pool", t[:], val)
        P.consts[key] = t
    return P.consts[key][:]


def build_mod():
    nc = new_nc()
    NCH = 48
    cT = nc.dram_tensor("cT", [128, KD, 3], F32, kind="ExternalInput").ap()
    w = nc.dram_tensor("w", [NCH, 128, KD, 128], F32, kind="ExternalInput").ap()
    b = nc.dram_tensor("b", [128, NCH], F32, kind="ExternalInput").ap()
    o = nc.dram_tensor("o", [128, NCH, 3], F32, kind="ExternalOutput").ap()
    with ExitStack() as st:
        P = Prog(nc, st)
        C = Ctx(P)
        ct = P.sb("ct", [128, KD, 3])
        sg = P.sb("sg", [128, KD, 3])
        bt = P.sb("bt", [128, NCH])
        ot = P.sb("ot", [128, NCH, 3])
        P.dma("sp", ct[:], cT)
        P.dma("act", bt[:], b)
        P.act(sg[:], ct[:], AF.Sigmoid)
        P.tt("dve", ct[:], ct[:], sg[:], ALU.mult)
        for cc in range(NCH):
            ps = dense_chunk(P, C, w[cc], KD, ct, 3)
            P.ts("dve", ot[:, cc, :], ps[:, 0:3], bt[:, cc:cc + 1], ALU.add)
        P.dma("sp", o, ot[:], is_output=True)
        P.finish()
    return nc


def run_mod(c, c_ctx, mod_w, mod_b):
    nc = build_mod()
    cT = np.stack([c[0], c[1], c_ctx], axis=-1).reshape(KD, 128, 3).transpose(1, 0, 2)
    cT = np.ascontiguousarray(cT)
    wall = mod_w.transpose(1, 0, 2).reshape(D, DEPTH * 6 * D)
    ball = mod_b.reshape(DEPTH * 6 * D)
    in_maps = []
    for core in range(NCORES):
        cols = slice(core * 6144, (core + 1) * 6144)
        in_maps.append({"cT": cT, "w": wl_layout(wall[:, cols]),
                        "b": np.ascontiguousarray(ball[cols].reshape(48, 128).T)})
    res = run_spmd(nc, in_maps)
    m = np.concatenate([r["o"].transpose(1, 0, 2).reshape(6144, 3) for r in res], axis=0)
    return np.ascontiguousarray(m.reshape(DEPTH, 6 * D, 3).transpose(0, 2, 1))


def mod_for_core(m_layer, core):
    b = core // 4
    mm_ = np.stack([m_layer[b], m_layer[2]], axis=-1)
    return np.ascontiguousarray(mm_.reshape(96, 128, 2).transpose(1, 0, 2))


class Stage:
    def __init__(self, P, n=3, shape=(128, 512)):
        self.bufs = [P.sb("stg", list(shape)) for _ in range(n)]
        self.i = 0
        self.e = 0

    def get(self):
        self.i += 1
        return self.bufs[self.i % len(self.bufs)]

    def eng(self):
        self.e += 1
        return ("dve", "act")[self.e % 2]


def load_mod(P, mod_ap):
    mt = P.sb("modt", [128, 96, 2])
    P.dma("sp", mt[:], mod_ap)
    return mt


def modulate(P, out, x, mt, shift_idx, scale_idx, sel, TT, tmp1):
    P.ts("dve", tmp1[:, 0:KD], mt[:, scale_idx * KD:(scale_idx + 1) * KD, sel], 1.0, ALU.add)
    for k in range(KD):
        eng = "dve" if k % 2 == 0 else "pool"
        P.ts(eng, out[:, k, 0:TT], x[:, k, 0:TT], tmp1[:, k:k + 1], ALU.mult,
             mt[:, shift_idx * KD + k, sel:sel + 1], ALU.add)


def rope_apply(P, C, S, out_dram, x_ps, m0, TT, cosT, sinT, rT, xr, t0):
    P.copy("dve", xr[0:64, 0:TT], x_ps)
    ps2 = C.psum()
    P.mm(ps2[0:64, 0:TT], rT[0:64, 0:64], xr[0:64, 0:TT])
    st = S.get()
    P.tt("dve", st[0:64, 0:TT], xr[0:64, 0:TT], cosT[0:64, t0:t0 + TT], ALU.mult)
    P.tt("dve", xr[0:64, 0:TT], ps2[0:64, 0:TT], sinT[0:64, t0:t0 + TT], ALU.mult)
    P.tt("pool", st[0:64, 0:TT], st[0:64, 0:TT], xr[0:64, 0:TT], ALU.add)
    P.dma(C.q(), out_dram, st[0:64, 0:TT], is_output=True)


def build_t1():
    nc = new_nc()
    dt = lambda name, shape, kind="ExternalInput": nc.dram_tensor(name, list(shape), F32, kind=kind).ap()
    hT = dt("hT", [D, TOK])
    mod = dt("mod", [128, 96, 2])
    w_in = dt("w_in", [31, 128, KD, 128])
    qg = dt("qg", [128, 4])
    kvg = dt("kvg", [128, 2])
    wqn = dt("wqn", [8, 128, 4, 128])
    wqr = dt("wqr", [8, 128, 4, 64])
    wkv = dt("wkv", [16, 128, 2, 128])
    cos_d = dt("cosT", [64, TOK])
    sin_d = dt("sinT", [64, TOK])
    rT_d = dt("rT", [64, 64])
    zhy = dt("zhy", [3072, TOK], "ExternalOutput")
    qT = dt("qT", [8, 192, TOK], "ExternalOutput")
    kvT = dt("kvT", [16, 128, TOK], "ExternalOutput")
    krT = dt("krT", [64, TOK], "ExternalOutput")
    with ExitStack() as stck:
        P = Prog(nc, stck)
        C = Ctx(P)
        S = Stage(P)
        mt = load_mod(P, mod)
        cosT = P.sb("cosT", [64, TOK]); sinT = P.sb("sinT", [64, TOK]); rT = P.sb("rT", [64, 64])
        qgt = P.sb("qgt", [128, 4]); kvgt = P.sb("kvgt", [128, 2])
        P.dma("sp", cosT[:], cos_d); P.dma("act", sinT[:], sin_d); P.dma("sp", rT[:], rT_d)
        P.dma("act", qgt[:], qg); P.dma("sp", kvgt[:], kvg)
        x = P.sb("x", [128, KD, 512])
        xin = P.sb("xin", [128, KD, 512])
        zl = P.sb("zl", [128, 6, 512])
        sq = P.sb("sq", [128, 6, 512])
        zn = P.sb("zn", [128, 6, 512])
        rs = P.sb("rs", [128, 512])
        xr = P.sb("xr", [64, 512])
        tmp1 = P.sb("tmp1", [128, KD])
        hT3 = hT.rearrange("(k p) t -> p k t", p=128)
        for (t0, TT, sel) in TILES:
            P.dma("sp", x[:, :, 0:TT], hT3[:, :, t0:t0 + TT])
            modulate(P, xin, x, mt, 0, 1, sel, TT, tmp1)
            for cc in range(31):
                m = 128 if cc < 30 else 64
                ps = dense_chunk(P, C, w_in[cc][:, :, 0:m], KD, xin, TT, m=m)
                if cc < 24:
                    st = S.get()
                    P.copy(S.eng(), st[:, 0:TT], ps[:, 0:TT])
                    P.dma(C.q(), zhy[cc * 128:(cc + 1) * 128, t0:t0 + TT], st[:, 0:TT], is_output=True)
                elif cc < 30:
                    P.copy(S.eng(), zl[:, cc - 24, 0:TT], ps[:, 0:TT])
                else:
                    rope_apply(P, C, S, krT[:, t0:t0 + TT], ps[0:64, 0:TT], 0, TT, cosT, sinT, rT, xr, t0)
            for (k0, nk, gt) in ((0, 4, qgt), (4, 2, kvgt)):
                pss = col_stats(P, C, zl[:, k0:k0 + nk, :], nk, TT, None, sq)
                rstd_from(P, rs, pss, TT, 1.0 / (nk * 128), RMS_EPS)
                for k in range(nk):
                    P.stt("dve", zn[:, k0 + k, 0:TT], zl[:, k0 + k, 0:TT], gt[:, k:k + 1], rs[:, 0:TT],
                          ALU.mult, ALU.mult)
            for h in range(8):
                ps = dense_chunk(P, C, wqn[h], 4, zn[:, 0:4, :], TT)
                st = S.get()
                P.copy(S.eng(), st[:, 0:TT], ps[:, 0:TT])
                P.dma(C.q(), qT[h, 0:128, t0:t0 + TT], st[:, 0:TT], is_output=True)
                ps = dense_chunk(P, C, wqr[h], 4, zn[:, 0:4, :], TT, m=64)
                rope_apply(P, C, S, qT[h, 128:192, t0:t0 + TT], ps[0:64, 0:TT], 0, TT, cosT, sinT, rT, xr, t0)
            for cc in range(16):
                ps = dense_chunk(P, C, wkv[cc], 2, zn[:, 4:6, :], TT)
                st = S.get()
                P.copy(S.eng(), st[:, 0:TT], ps[:, 0:TT])
                P.dma(C.q(), kvT[cc, :, t0:t0 + TT], st[:, 0:TT], is_output=True)
        P.finish()
    return nc


def rope_tables():
    rows = SEQ // 64
    row = np.repeat(np.arange(rows), 64).astype(np.float32)
    col = np.tile(np.arange(64), rows).astype(np.float32)
    half = 32
    inv = (10000.0 ** (-np.arange(0, half, 2, dtype=np.float32) / half)).astype(np.float32)
    ar, ac = row[:, None] * inv, col[:, None] * inv
    ang = np.concatenate([ar, ar, ac, ac], axis=-1)
    return np.cos(ang).astype(np.float32), np.sin(ang).astype(np.float32)


def rope_rT():
    R = np.zeros((64, 64), np.float32)
    for a in range(2):
        for f in range(16):
            R[a * 32 + f, a * 32 + 16 + f] = -1.0
            R[a * 32 + 16 + f, a * 32 + f] = 1.0
    return np.ascontiguousarray(R.T)


def core_tokens_T(h_lat, h_ctx, core):
    b, j = core // 4, core % 4
    return np.ascontiguousarray(np.concatenate([h_lat[b, j * 2048:(j + 1) * 2048], h_ctx[b, j * 64:(j + 1) * 64]], 0).T)


def t1_weight_maps(w_in, q_norm, w_uq, kv_norm, w_ukv):
    w_in_p = np.concatenate([w_in, np.zeros((D, 64), np.float32)], axis=1)
    uq = w_uq.reshape(512, 8, 192)
    wqn = np.stack([wl_layout(np.ascontiguousarray(uq[:, h, :128]))[0] for h in range(8)])
    wqr = np.stack([np.ascontiguousarray(uq[:, h, 128:].reshape(4, 128, 64).transpose(1, 0, 2)) for h in range(8)])
    return {"w_in": wl_layout(w_in_p), "qg": np.ascontiguousarray(q_norm.reshape(4, 128).T),
            "kvg": np.ascontiguousarray(kv_norm.reshape(2, 128).T), "wqn": wqn, "wqr": wqr,
            "wkv": wl_layout(w_ukv), "rT": rope_rT()}


def rope_core(cos, sin, core):
    j = core % 4
    c = np.concatenate([cos[j * 2048:(j + 1) * 2048], np.ones((64, 64), np.float32)], 0).T
    s = np.concatenate([sin[j * 2048:(j + 1) * 2048], np.zeros((64, 64), np.float32)], 0).T
    return np.ascontiguousarray(c), np.ascontiguousarray(s)


NTOK = CTX + SEQ
NKC = NTOK // 128


def build_attn():
    nc = new_nc()
    dt = lambda name, shape, kind="ExternalInput": nc.dram_tensor(name, list(shape), F32, kind=kind).ap()
    QT = dt("QT", [2, 192, NTOK])
    KT = dt("KT", [2, 192, NTOK])
    Vd = dt("V", [2, NTOK, 128])
    att = dt("att", [2, NTOK, 128], "ExternalOutput")
    with ExitStack() as stck:
        P = Prog(nc, stck)
        ktn = P.sb("ktn", [128, NTOK]); ktr = P.sb("ktr", [64, NTOK])
        va = P.sb("va", [128, NKC, 129])
        qn = [P.sb("qn", [128, 512]) for _ in range(2)]
        qr = [P.sb("qr", [64, 512]) for _ in range(2)]
        pts = [P.sb("pt", [128, 512]) for _ in range(3)]
        ps_s = [P.ps("ps_s") for _ in range(3)]
        acc = [P.ps("acc") for _ in range(4)]
        rc = P.sb("rc", [128, 4])
        ost = [P.sb("ost", [128, 4, 128]) for _ in range(2)]
        P.memset("pool", va[:, :, 128:129], 1.0)
        blocks = [(0, 256, 2)] + [(CTX + i * 512, 512, NKC) for i in range(SEQ // 512)]
        bi = 0
        ci = 0
        for h in range(2):
            P.dma("sp", ktn[:], KT[h, 0:128, :])
            P.dma("act", ktr[:], KT[h, 128:192, :])
            P.dma("sp", va[:, :, 0:128], Vd[h].rearrange("(c p) d -> p c d", p=128))
            for (q0, TQ, nkc) in blocks:
                bi += 1
                qnb, qrb, ob = qn[bi % 2], qr[bi % 2], ost[bi % 2]
                P.dma("sp", qnb[:, 0:TQ], QT[h, 0:128, q0:q0 + TQ])
                P.dma("act", qrb[:, 0:TQ], QT[h, 128:192, q0:q0 + TQ])
                nsub = TQ // 128

                def s_mm(kc):
                    ps = ps_s[(ci + kc) % 3]
                    P.mm(ps[:, 0:TQ], ktn[:, kc * 128:(kc + 1) * 128], qnb[:, 0:TQ], start=True, stop=False)
                    P.mm(ps[:, 0:TQ], ktr[:, kc * 128:(kc + 1) * 128], qrb[:, 0:TQ], start=False, stop=True)

                s_mm(0)
                for kc in range(nkc):
                    if kc + 1 < nkc:
                        s_mm(kc + 1)
                    ps = ps_s[(ci + kc) % 3]
                    pt = pts[(ci + kc) % 3]
                    P.act(pt[:, 0:TQ], ps[:, 0:TQ], AF.Exp, scale=MLA_SCALE)
                    for s in range(nsub):
                        P.mm(acc[s][:, 0:129], pt[:, s * 128:(s + 1) * 128], va[:, kc, :],
                             start=(kc == 0), stop=(kc == nkc - 1))
                ci += nkc
                for s in range(nsub):
                    P.recip(rc[:, s:s + 1], acc[s][:, 128:129])
                    P.ts("dve", ob[:, s, :], acc[s][:, 0:128], rc[:, s:s + 1], ALU.mult)
                P.dma("sp", att[h, q0:q0 + TQ, :].rearrange("(s p) d -> p s d", p=128), ob[:, 0:nsub, :],
                      is_output=True)
        P.finish()
    return nc


TT2 = 256
TILES2 = [(i * TT2, TT2, 0) for i in range(2048 // TT2)] + [(2048, 64, 1)]
GELU_C = 2.0 * math.sqrt(2.0 / math.pi)


def gelu_tanh(P, out, x, tmp, eng="dve"):
    P.tt(eng, tmp, x, x, ALU.mult)
    P.ts(eng, tmp, tmp, 0.044715, ALU.mult, 1.0, ALU.add)
    P.tt(eng, tmp, tmp, x, ALU.mult)
    P.act(tmp, tmp, AF.Sigmoid, scale=GELU_C)
    P.tt(eng, out, x, tmp, ALU.mult)


def layer_norm_T(P, C, u, TT, g_ap, b_ap, sq, rs):
    ps = C.psum()
    for k in range(KD):
        P.mm(ps[:, 0:TT], C.ones[:], u[:, k, 0:TT], start=(k == 0), stop=(k == KD - 1))
    P.act(rs[:, 0:TT], ps[:, 0:TT], AF.Copy, scale=-1.0 / D)
    for k in range(KD):
        eng = "dve" if k % 2 == 0 else "pool"
        P.tt(eng, u[:, k, 0:TT], u[:, k, 0:TT], rs[:, 0:TT], ALU.add)
    pss = col_stats(P, C, u, KD, TT, None, sq)
    rstd_from(P, rs, pss, TT, 1.0 / D, LN_EPS)
    for k in range(KD):
        P.tt("dve", u[:, k, 0:TT], u[:, k, 0:TT], rs[:, 0:TT], ALU.mult)
        P.act(u[:, k, 0:TT], u[:, k, 0:TT], AF.Identity, bias=b_ap[:, k:k + 1], scale=g_ap[:, k:k + 1])


class Peer:
    def __init__(self, P, C, keysT_d, ident_d):
        self.P, self.C = P, C
        self.keysT = P.sb("keysT", [128, 16, 128])
        self.ident = P.sb("ident", [128, 128])
        P.dma("sp", self.keysT[:], keysT_d)
        P.dma("act", self.ident[:], ident_d)
        io_i = P.sb("io_i", [128, 256], I32)
        self.iota = P.sb("iota", [128, 256])
        P.op("pool", lambda e: e.iota(io_i[:], [[1, 256]], base=0, channel_multiplier=0), writes=[io_i[:]])
        P.copy("dve", self.iota[:], io_i[:])
        self.qT = P.sb("pq", [128, 16, 128])
        self.sc = P.sb("psc", [128, 16, 128])
        self.sc2 = P.sb("psc2", [128, 128])
        self.top = P.sb("ptop", [128, 16, 16])
        self.ti = P.sb("pti", [128, 16, 16], U32)
        self.tif = P.sb("ptif", [128, 16, 16])
        self.cand = P.sb("pcand", [128, 8, 256])
        self.cand2 = P.sb("pcand2", [128, 256])
        self.fidx = P.sb("pfidx", [128, 8, 256])
        self.c16 = P.sb("pc16", [128, 8, 16])
        self.cj = P.sb("pcj", [128, 8, 16], U32)
        self.cjf = P.sb("pcjf", [128, 8, 16])
        self.junk = P.sb("pjunk", [128, 256])
        self.isel = P.sb("pisel", [128, 128])
        self.isel32 = P.sb("pisel32", [128, 128], I32)
        self.gw = P.sb("pgw", [128, 8, 16])
        self.nmax = P.sb("pnmax", [128, 8])
        self.ssum = P.sb("pssum", [128, 8])
        self.apre = P.sb("papre", [128, 128])
        self.atmp = P.sb("patmp", [128, 128])
        self.wgt = P.sb("pwgt", [128, 128])
        self.xtok = P.sb("pxtok", [128, D])
        self.gbuf = [P.sb("pg", [128, D]) for _ in range(3)]
        self.gi = 0
        self.accs = [P.sb("pacc", [128, D]) for _ in range(2)]

    def gather(self, tab_d, col):
        P = self.P
        self.gi += 1
        buf = self.gbuf[self.gi % 3]
        idx_ap = self.isel32[:, col:col + 1]
        P.dma("pool", buf[:], idx_ap,
              fn=lambda e, buf=buf, idx_ap=idx_ap: e.indirect_dma_start(
                  out=buf[:], out_offset=None, in_=tab_d,
                  in_offset=bass.IndirectOffsetOnAxis(ap=idx_ap, axis=0)))
        return buf

    def subtile(self, xin, s0, wq_d, u_d, v_d, fT_out):
        P, C = self.P, self.C
        xs = xin[:, :, s0:s0 + 128]
        for cc in range(16):
            ps = dense_chunk(P, C, wq_d[cc], KD, xs, 128)
            P.copy("act", self.qT[:, cc, :], ps[:, 0:128])
        for g4 in range(4):
            ps = C.psum()
            for j in range(4):
                k = g4 * 4 + j
                P.transpose(ps[:, j * 128:(j + 1) * 128], xs[:, k, :], self.ident[:])
            P.copy("act", self.xtok[:, g4 * 512:(g4 + 1) * 512], ps[:])
        for g4 in range(4):
            ps = C.psum()
            for j in range(4):
                hs = g4 * 4 + j
                P.mm(ps[:, j * 128:(j + 1) * 128], self.qT[:, hs, :], self.keysT[:, hs, :])
            P.copy("dve", self.sc[:, g4 * 4:(g4 + 1) * 4, :], ps[:].rearrange("p (a b) -> p a b", a=4))
        V = lambda fn, r, w: P.op("dve", fn, reads=r, writes=w)
        for hs in range(16):
            sc, sc2, top, ti = self.sc[:, hs, :], self.sc2[:], self.top[:, hs, :], self.ti[:, hs, :]
            V(lambda e, a=top[:, 0:8], b=sc: e.max(a, b), [sc], [top[:, 0:8]])
            V(lambda e, a=ti[:, 0:8], b=top[:, 0:8], c=sc: e.max_index(a, b, c), [top[:, 0:8], sc], [ti[:, 0:8]])
            V(lambda e, a=sc2, b=top[:, 0:8], c=sc: e.match_replace(a, b, c, -1e30), [top[:, 0:8], sc], [sc2])
            V(lambda e, a=top[:, 8:16], b=sc2: e.max(a, b), [sc2], [top[:, 8:16]])
            V(lambda e, a=ti[:, 8:16], b=top[:, 8:16], c=sc2: e.max_index(a, b, c), [top[:, 8:16], sc2], [ti[:, 8:16]])
        P.copy("dve", self.tif[:], self.ti[:])
        for h in range(8):
            c3 = self.cand[:, h, :].rearrange("p (a b) -> p a b", a=16)
            f3 = self.fidx[:, h, :].rearrange("p (a b) -> p a b", a=16)
            t0b = self.top[:, 2 * h, :].unsqueeze(2).to_broadcast([128, 16, 16])
            t1b = self.top[:, 2 * h + 1, :].unsqueeze(1).to_broadcast([128, 16, 16])
            P.tt("dve", c3, t0b, t1b, ALU.add)
            i0b = self.tif[:, 2 * h, :].unsqueeze(2).to_broadcast([128, 16, 16])
            i1b = self.tif[:, 2 * h + 1, :].unsqueeze(1).to_broadcast([128, 16, 16])
            P.stt("dve", f3, i0b, 128.0, i1b, ALU.mult, ALU.add)
            cd, cd2, c16, cj = self.cand[:, h, :], self.cand2[:], self.c16[:, h, :], self.cj[:, h, :]
            V(lambda e, a=c16[:, 0:8], b=cd: e.max(a, b), [cd], [c16[:, 0:8]])
            V(lambda e, a=cj[:, 0:8], b=c16[:, 0:8], c=cd: e.max_index(a, b, c), [c16[:, 0:8], cd], [cj[:, 0:8]])
            V(lambda e, a=cd2, b=c16[:, 0:8], c=cd: e.match_replace(a, b, c, -1e30), [c16[:, 0:8], cd], [cd2])
            V(lambda e, a=c16[:, 8:16], b=cd2: e.max(a, b), [cd2], [c16[:, 8:16]])
            V(lambda e, a=cj[:, 8:16], b=c16[:, 8:16], c=cd2: e.max_index(a, b, c), [c16[:, 8:16], cd2], [cj[:, 8:16]])
        P.copy("dve", self.cjf[:], self.cj[:])
        for h in range(8):
            for k in range(16):
                P.stt("dve", self.junk[:], self.iota[:], self.cjf[:, h, k:k + 1], self.fidx[:, h, :],
                      ALU.is_equal, ALU.mult, accum_out=self.isel[:, h * 16 + k:h * 16 + k + 1])
        P.ts("dve", self.isel[:], self.isel[:], 16383.0, ALU.min, 0.0, ALU.max)
        P.copy("dve", self.isel32[:], self.isel[:])
        P.ts("dve", self.nmax[:], self.c16[:, :, 0], -1.0, ALU.mult)
        for h in range(8):
            P.act(self.gw[:, h, :], self.c16[:, h, :], AF.Exp, bias=self.nmax[:, h:h + 1],
                  accum_out=self.ssum[:, h:h + 1])
        P.recip(self.ssum[:], self.ssum[:])
        P.tt("dve", self.gw[:], self.gw[:], self.ssum[:].unsqueeze(2).to_broadcast([128, 8, 16]), ALU.mult)
        if getattr(self, "stop_after_topk", False):
            return
        for hk in range(getattr(self, "n_gather", 128)):
            buf = self.gather(u_d, hk)
            P.stt("dve", buf[:], buf[:], 1.0, self.xtok[:], ALU.mult, ALU.mult,
                  accum_out=self.apre[:, hk:hk + 1])
        gelu_tanh(P, self.wgt[:], self.apre[:], self.atmp[:])
        P.tt("dve", self.wgt[:], self.wgt[:], self.gw[:].rearrange("p a b -> p (a b)"), ALU.mult)
        for hk in range(getattr(self, "n_gather", 128)):
            buf = self.gather(v_d, hk)
            j = hk % 2
            eng = "dve"
            acc = self.accs[j]
            if hk < 2:
                P.ts(eng, acc[:], buf[:], self.wgt[:, hk:hk + 1], ALU.mult)
            else:
                P.stt(eng, acc[:], buf[:], self.wgt[:, hk:hk + 1], acc[:], ALU.mult, ALU.add)
        P.tt("pool", self.accs[0][:], self.accs[0][:], self.accs[1][:], ALU.add)
        for g4 in range(4):
            ps = C.psum()
            for j in range(4):
                k = g4 * 4 + j
                P.transpose(ps[:, j * 128:(j + 1) * 128], self.accs[0][:, k * 128:(k + 1) * 128], self.ident[:])
            for j in range(4):
                fT_out(g4 * 4 + j, ps[:, j * 128:(j + 1) * 128])


def tiles_for(nt_ctx, nt_lat, TT):
    out = []
    t = 0
    while t < nt_ctx:
        n = min(TT, nt_ctx - t)
        out.append((t, n, 1))
        t += n
    while t < nt_ctx + nt_lat:
        n = min(TT, nt_ctx + nt_lat - t)
        out.append((t, n, 0))
        t += n
    return out


def load_vec16(P, q, ap16):
    t = P.sb("v16", [128, KD])
    P.dma(q, t[:], ap16)
    return t


def post_phase(P, odd, h_d, mix_d, mod_d, modn_d, w_o_d, w_g_d, dskip_d, lng_d, lnb_d, lfg_d, lfb_d,
               wq_d, keysT_d, ident_d, u_d, v_d, s5in_d, nt_ctx, nt_lat):
    C = Ctx(P)
    S = Stage(P, n=2, shape=(128, TT2))
    mt = load_mod(P, mod_d)
    mtn = load_mod(P, modn_d) if modn_d is not None else None
    lng, lnb = load_vec16(P, "sp", lng_d), load_vec16(P, "act", lnb_d)
    lfg, lfb = load_vec16(P, "sp", lfg_d), load_vec16(P, "act", lfb_d)
    dsk = load_vec16(P, "sp", dskip_d) if odd else None
    pe = Peer(P, C, keysT_d, ident_d)
    x = P.sb("x", [128, KD, TT2])
    mx = P.sb("mx", [128, KD, TT2])
    u = P.sb("u", [128, KD, TT2])
    sq = P.sb("sq", [128, KD, TT2])
    rs = P.sb("rs", [128, TT2])
    tmp1 = P.sb("tmp1", [128, KD])
    tch = P.sb("tch", [128, TT2])
    tch2 = P.sb("tch2", [128, TT2])
    h3 = h_d.rearrange("(k p) t -> p k t", p=128)
    m3 = mix_d.rearrange("(k p) t -> p k t", p=128)
    s3 = s5in_d.rearrange("(k p) t -> p k t", p=128) if s5in_d is not None else None
    for (t0, TT, sel) in tiles_for(nt_ctx, nt_lat, TT2):
        P.dma("sp", x[:, :, 0:TT], h3[:, :, t0:t0 + TT])
        P.dma("act", mx[:, :, 0:TT], m3[:, :, t0:t0 + TT])
        m2 = lambda k: mt[:, 2 * KD + k, sel:sel + 1]
        m5 = lambda k: mt[:, 5 * KD + k, sel:sel + 1]
        if odd:
            modulate(P, sq, x, mt, 0, 1, sel, TT, tmp1)
            for k in range(KD):
                P.stt("dve", mx[:, k, 0:TT], sq[:, k, 0:TT], dsk[:, k:k + 1], mx[:, k, 0:TT], ALU.mult, ALU.add)
            for k in range(KD):
                gelu_tanh(P, mx[:, k, 0:TT], mx[:, k, 0:TT], sq[:, k, 0:TT], eng=("dve", "pool")[k % 2])
        for cc in range(KD):
            ps = dense_chunk(P, C, w_o_d[cc], KD, mx, TT)
            if odd:
                ps2 = dense_chunk(P, C, w_g_d[cc], KD, mx, TT)
                P.act(tch2[:, 0:TT], ps2[:, 0:TT], AF.Sigmoid)
                P.stt("dve", tch[:, 0:TT], ps[:, 0:TT], m2(cc), tch2[:, 0:TT], ALU.mult, ALU.mult)
            else:
                P.act(tch[:, 0:TT], ps[:, 0:TT], AF.Copy, scale=m2(cc))
            P.stt("dve", u[:, cc, 0:TT], x[:, cc, 0:TT], DN_ALPHA, tch[:, 0:TT], ALU.mult, ALU.add)
        layer_norm_T(P, C, u, TT, lng, lnb, sq, rs)
        modulate(P, mx, u, mt, 3, 4, sel, TT, tmp1)
        for s0 in range(0, TT, 128):
            def f_out(k, ps, s0=s0):
                P.act(tch[:, 0:128], ps, AF.Copy, scale=m5(k))
                P.stt("dve", u[:, k, s0:s0 + 128], u[:, k, s0:s0 + 128], DN_ALPHA, tch[:, 0:128], ALU.mult, ALU.add)
            pe.subtile(mx, s0, wq_d, u_d, v_d, f_out)
        layer_norm_T(P, C, u, TT, lfg, lfb, sq, rs)
        P.dma("sp", h3[:, :, t0:t0 + TT], u[:, :, 0:TT], is_output=True)
        if s3 is not None:
            modulate(P, sq, u, mtn, 0, 1, sel, TT, tmp1)
            P.dma("act", s3[:, :, t0:t0 + TT], sq[:, :, 0:TT], is_output=True)


S5_T = 512
TWO_PI_LO = 6.28318


def s5_pack_params(lam_re, lam_im, log_dt, b_re, b_im, c_re, c_im):
    ldt = np.repeat(log_dt[:, :, None], 64, axis=2)
    cols = [lam_re[..., None], lam_im[..., None], ldt[..., None], b_re, b_im,
            c_re.transpose(0, 1, 3, 2), c_im.transpose(0, 1, 3, 2)]
    pk = np.concatenate(cols, axis=-1)
    return np.ascontiguousarray(pk.reshape(2, 64, 128, 67))


def s5_phase(P, u_d, y_d, prm_d, ident_d, nt_ctx, nt_lat, n_ct=16):
    NT = nt_ctx + nt_lat
    T = S5_T
    ident = P.sb("ident", [128, 128])
    P.dma("sp", ident[:], ident_d)
    io_i = P.sb("io_i", [128, T + 1], I32)
    iota = P.sb("iota", [128, T + 1])
    P.op("pool", lambda e: e.iota(io_i[:], [[1, T + 1]], base=0, channel_multiplier=0), writes=[io_i[:]])
    P.copy("dve", iota[:], io_i[:])
    ones = P.sb("ones", [128, T])
    P.memset("pool", ones[:], 1.0)
    u = P.sb("u", [128, NT])
    Y0 = P.sb("Y0", [128, NT])
    cosT = [[P.sb("cosT", [128, T + 1]) for d in range(2)] for q in range(4)]
    sinT = [[P.sb("sinT", [128, T + 1]) for d in range(2)] for q in range(4)]
    Bre = [[P.sb("Bre", [128, 128]) for d in range(2)] for q in range(4)]
    Bim = [[P.sb("Bim", [128, 128]) for d in range(2)] for q in range(4)]
    Cre = [[P.sb("Cre", [128, 128]) for d in range(2)] for q in range(4)]
    Cni = [[P.sb("Cni", [128, 128]) for d in range(2)] for q in range(4)]
    rv = [[P.sb("rv", [128, 1]) for d in range(2)] for q in range(4)]
    rb = [[P.sb("rb", [128, T]) for d in range(2)] for q in range(4)]
    init = [[P.sb("init", [128, 2]) for d in range(2)] for q in range(4)]
    prm = P.sb("prm", [128, 67])
    sc = P.sb("sc", [128, 16])
    bb = P.sb("bb", [128, 32])
    bp = P.sb("bp", [128, 2, 128])
    tw = P.sb("tw", [128, T + 1]); tw2 = P.sb("tw2", [128, T + 1]); twi = P.sb("twi", [128, T + 1], I32)
    ps_t = P.ps("ps_t")
    ps_b = [[P.ps("ps_br"), P.ps("ps_bi")] for _ in range(2)]
    ps_y = [P.ps("ps_y") for _ in range(2)]
    W = [dict((n, P.sb(n, [128, T])) for n in ("br", "bi", "t1", "t2", "t3", "t4", "zr", "zi", "sr", "si", "xr", "xi"))
         for _ in range(2)]
    wi = 0
    yi = 0

    def wrap_sin(out, fr):
        P.ts("dve", tw2[:], fr, 0.5, ALU.is_gt)
        P.tt("dve", fr, fr, tw2[:], ALU.subtract)
        P.ts("dve", tw2[:], fr, -0.5, ALU.is_lt)
        P.tt("dve", fr, fr, tw2[:], ALU.add)
        P.act(out, fr, AF.Sin, scale=TWO_PI_LO)

    for ct in range(n_ct):
        P.dma("sp", u[:], u_d[ct * 128:(ct + 1) * 128, :])
        for q in range(4):
            st = ct * 4 + q
            for d in range(2):
                P.dma("act", prm[:], prm_d[d, st])
                lr, li, ld = prm[:, 0:1], prm[:, 1:2], prm[:, 2:3]
                c = lambda i: sc[:, i:i + 1]
                P.act(c(0), ld, AF.Exp)
                P.act(rv[q][d][:], lr, AF.Exp, scale=c(0))
                P.ts("dve", c(1), li, c(0), ALU.mult, 1.0 / (2.0 * math.pi), ALU.mult)
                P.ts("dve", rb[q][d][:], ones[:], rv[q][d][:], ALU.mult)
                P.ts("dve", tw[:], iota[:], c(1), ALU.mult)
                P.copy("dve", twi[:], tw[:])
                P.copy("dve", tw2[:], twi[:])
                P.tt("dve", tw[:], tw[:], tw2[:], ALU.subtract)
                P.ts("dve", cosT[q][d][:], tw[:], 0.25, ALU.add)
                wrap_sin(sinT[q][d][:], tw[:])
                P.copy("dve", tw[:], cosT[q][d][:])
                wrap_sin(cosT[q][d][:], tw[:])
                cos1, sin1 = cosT[q][d][:, 1:2], sinT[q][d][:, 1:2]
                P.ts("dve", c(2), cos1, rv[q][d][:], ALU.mult, -1.0, ALU.add)
                P.ts("dve", c(3), sin1, rv[q][d][:], ALU.mult)
                P.ts("dve", c(4), lr, lr, ALU.mult)
                P.stt("dve", c(4), li, li, c(4), ALU.mult, ALU.add)
                P.recip(c(5), c(4))
                P.ts("dve", c(6), c(2), lr, ALU.mult)
                P.stt("dve", c(6), c(3), li, c(6), ALU.mult, ALU.add)
                P.ts("dve", c(7), c(6), c(5), ALU.mult)
                P.ts("dve", c(8), c(2), li, ALU.mult)
                P.stt("dve", c(8), c(3), lr, c(8), ALU.mult, ALU.subtract)
                P.ts("dve", c(9), c(8), c(5), ALU.mult)
                b_re, b_im, cr, ci = prm[:, 3:19], prm[:, 19:35], prm[:, 35:51], prm[:, 51:67]
                P.ts("dve", bb[:, 0:16], b_im, c(9), ALU.mult)
                P.stt("dve", bb[:, 0:16], b_re, c(7), bb[:, 0:16], ALU.mult, ALU.subtract)
                P.ts("dve", bb[:, 16:32], b_re, c(9), ALU.mult)
                P.stt("dve", bb[:, 16:32], b_im, c(7), bb[:, 16:32], ALU.mult, ALU.add)
                P.memset("pool", bp[:], 0.0)
                P.memset("pool", Cre[q][d][:], 0.0)
                P.memset("pool", Cni[q][d][:], 0.0)
                for gl in range(2):
                    pr = slice(gl * 64, gl * 64 + 64)
                    cs = slice(32 * q + 16 * gl, 32 * q + 16 * gl + 16)
                    P.copy("dve", bp[pr, 0, cs], bb[pr, 0:16])
                    P.copy("dve", bp[pr, 1, cs], bb[pr, 16:32])
                    P.copy("dve", Cre[q][d][pr, cs], cr[pr, :])
                    P.ts("dve", Cni[q][d][pr, cs], ci[pr, :], -1.0, ALU.mult)
                P.transpose(ps_t[:, 0:128], bp[:, 0, :], ident[:])
                P.transpose(ps_t[:, 128:256], bp[:, 1, :], ident[:])
                P.copy("act", Bre[q][d][:], ps_t[:, 0:128])
                P.copy("act", Bim[q][d][:], ps_t[:, 128:256])
                P.memset("pool", init[q][d][:], 0.0)
        for d in range(2):
            chunks = [(0, nt_ctx)] if nt_ctx else []
            lat = [(nt_ctx + i, min(T, nt_lat - i)) for i in range(0, nt_lat, T)]
            chunks = chunks + (lat if d == 0 else lat[::-1])
            for (a, Tc) in chunks:
                yi += 1
                psy = ps_y[yi % 2]
                for q in range(4):
                    wi += 1
                    Wq = W[wi % 2]
                    pbr, pbi = ps_b[wi % 2]
                    cs_, sn_ = cosT[q][d], sinT[q][d]
                    P.mm(pbr[:, 0:Tc], Bre[q][d][:], u[:, a:a + Tc])
                    P.mm(pbi[:, 0:Tc], Bim[q][d][:], u[:, a:a + Tc])
                    if d == 0:
                        P.copy("act", Wq["br"][:, 0:Tc], pbr[:, 0:Tc])
                        P.copy("act", Wq["bi"][:, 0:Tc], pbi[:, 0:Tc])
                    else:
                        P.copy("act", Wq["br"][:, 0:Tc], pbr[:, 0:Tc][:, ::-1])
                        P.copy("act", Wq["bi"][:, 0:Tc], pbi[:, 0:Tc][:, ::-1])
                    X = lambda n: Wq[n][:, 0:Tc]
                    P.tt("pool", X("t1"), cs_[:, 0:Tc], X("br"), ALU.mult)
                    P.tt("pool", X("t2"), sn_[:, 0:Tc], X("bi"), ALU.mult)
                    P.tt("pool", X("xr"), X("t1"), X("t2"), ALU.add)
                    P.tt("pool", X("t3"), cs_[:, 0:Tc], X("bi"), ALU.mult)
                    P.tt("pool", X("t4"), sn_[:, 0:Tc], X("br"), ALU.mult)
                    P.tt("dve", X("xi"), X("t3"), X("t4"), ALU.subtract)
                    i0 = init[q][d]
                    P.op("dve", lambda e, o=X("zr"), a0=rb[q][d][:, 0:Tc], a1=X("xr"), ini=i0[:, 0:1]:
                         e.tensor_tensor_scan(o, a0, a1, ini, ALU.mult, ALU.add),
                         reads=[rb[q][d][:, 0:Tc], X("xr"), i0[:, 0:1]], writes=[X("zr")])
                    P.op("dve", lambda e, o=X("zi"), a0=rb[q][d][:, 0:Tc], a1=X("xi"), ini=i0[:, 1:2]:
                         e.tensor_tensor_scan(o, a0, a1, ini, ALU.mult, ALU.add),
                         reads=[rb[q][d][:, 0:Tc], X("xi"), i0[:, 1:2]], writes=[X("zi")])
                    zr_l, zi_l = Wq["zr"][:, Tc - 1:Tc], Wq["zi"][:, Tc - 1:Tc]
                    ecr, eci = cs_[:, Tc:Tc + 1], sn_[:, Tc:Tc + 1]
                    P.ts("dve", sc[:, 10:11], zi_l, eci, ALU.mult)
                    P.stt("dve", i0[:, 0:1], zr_l, ecr, sc[:, 10:11], ALU.mult, ALU.subtract)
                    P.ts("dve", sc[:, 11:12], zi_l, ecr, ALU.mult)
                    P.stt("dve", i0[:, 1:2], zr_l, eci, sc[:, 11:12], ALU.mult, ALU.add)
                    P.tt("pool", X("t1"), cs_[:, 0:Tc], X("zr"), ALU.mult)
                    P.tt("pool", X("t2"), sn_[:, 0:Tc], X("zi"), ALU.mult)
                    P.tt("dve", X("t3"), sn_[:, 0:Tc], X("zr"), ALU.mult)
                    P.tt("dve", X("t4"), cs_[:, 0:Tc], X("zi"), ALU.mult)
                    so_r = X("sr") if d == 0 else X("sr")[:, ::-1]
                    so_i = X("si") if d == 0 else X("si")[:, ::-1]
                    P.tt("pool", so_r, X("t1"), X("t2"), ALU.subtract)
                    P.tt("dve", so_i, X("t3"), X("t4"), ALU.add)
                    P.mm(psy[:, 0:Tc], Cre[q][d][:], X("sr"), start=(q == 0), stop=False)
                    P.mm(psy[:, 0:Tc], Cni[q][d][:], X("si"), start=False, stop=(q == 3))
                if d == 0:
                    P.copy("act", Y0[:, a:a + Tc], psy[:, 0:Tc])
                else:
                    P.tt("dve", Y0[:, a:a + Tc], Y0[:, a:a + Tc], psy[:, 0:Tc], ALU.add)
        P.dma("sp", y_d[ct * 128:(ct + 1) * 128, :], Y0[:], is_output=True)


HY_G = 16
FFT_N = 2 * SEQ


def hyena_consts():
    a = np.arange(128)
    ang = 2.0 * np.pi * np.outer(a, a) / 128.0
    Cm, Sm = np.cos(ang), np.sin(ang)
    tw = 2.0 * np.pi * np.outer(a, a) / FFT_N
    L = SEQ
    n = np.arange(FFT_N)
    pos = np.where(n < L, n, np.where(n == L, 0, FFT_N - n))

    def feats(Lx, p):
        t = np.linspace(0.0, 1.0, Lx, dtype=np.float32)[:, None]
        w = ((2.0 * math.pi / Lx) * np.arange(Lx, dtype=np.float32))[:, None]
        f = np.linspace(1e-4, 15, 16, dtype=np.float32)[None, :]
        ft = np.concatenate([t, np.cos(f * w), -np.sin(f * w)], axis=-1).astype(np.float32)
        return ft[p], t[p, 0]
    f_lat, t_lat = feats(L, pos)
    perm = (np.arange(128)[None, :] * 128 + np.arange(128)[:, None]).reshape(-1)
    f_ctx, t_ctx = feats(CTX, np.arange(CTX))
    deltas = np.abs(np.linspace(math.log(1e-2) / 1.5, math.log(1e-2) / 0.3, 2 * HY_W, dtype=np.float32))
    c = lambda x: np.ascontiguousarray(x, dtype=np.float32)
    return {
        "hc_FA": c(np.concatenate([Cm, -Sm], 1)), "hc_CS": c(np.concatenate([Cm, Sm], 1)),
        "hc_NSC": c(np.concatenate([-Sm, Cm], 1)), "hc_Cm": c(Cm), "hc_Sm": c(Sm), "hc_NSm": c(-Sm),
        "hc_TWr": c(np.cos(tw)), "hc_TWi": c(-np.sin(tw)),
        "hc_featL": c(f_lat[perm].T), "hc_featC": c(f_ctx.T),
        "hc_tposL": c(t_lat.reshape(128, 128)), "hc_tposC": c(np.tile(t_ctx[None, :], (128, 1))),
        "hc_ndelL": c(np.tile(-deltas[None, :], (128, 1))), "hc_ndelC": c((-deltas).reshape(16, 128).T),
    }


HC_SHAPES = {"hc_FA": [128, 256], "hc_CS": [128, 256], "hc_NSC": [128, 256], "hc_Cm": [128, 128],
             "hc_Sm": [128, 128], "hc_NSm": [128, 128], "hc_TWr": [128, 128], "hc_TWi": [128, 128],
             "hc_featL": [33, FFT_N], "hc_featC": [33, CTX], "hc_tposL": [128, 128], "hc_tposC": [128, CTX],
             "hc_ndelL": [128, 2 * HY_W], "hc_ndelC": [128, 16]}


def hyena_mlp(P, feat_d, npos, hdn_d, w1_d, w23_d, b_d, fq_d):
    w1 = P.sb("mw1", [33, 64]); w23 = P.sb("mw23", [64, 2, 64]); b = P.sb("mb", [64, 3]); fq = P.sb("mfq", [64, 1])
    P.dma("sp", w1[:], w1_d); P.dma("act", w23[:], w23_d.rearrange("l k m -> k l m"))
    P.dma("sp", b[:], b_d); P.dma("act", fq[:], fq_d)
    fq2 = P.sb("mfq2", [64, 1])
    P.ts("dve", fq2[:], fq[:], 1.0 / (2.0 * math.pi), ALU.mult)
    ft = [P.sb("mft", [33, 512]) for _ in range(2)]
    hb = [P.sb("mhb", [64, 512]) for _ in range(3)]
    tw = P.sb("mtw", [64, 512]); tm = P.sb("mtm", [64, 512])
    pss = [P.ps("mps") for _ in range(2)]
    for i, p0 in enumerate(range(0, npos, 512)):
        n = min(512, npos - p0)
        f = ft[i % 2]
        P.dma(("sp", "act")[i % 2], f[:, 0:n], feat_d[:, p0:p0 + n])
        cur = f[0:33, 0:n]
        for layer in range(3):
            ps = pss[(i * 3 + layer) % 2]
            lhsT = w1[:] if layer == 0 else w23[:, layer - 1, :]
            P.mm(ps[0:64, 0:n], lhsT, cur)
            P.ts("dve", tw[:, 0:n], ps[0:64, 0:n], b[:, layer:layer + 1], ALU.add, fq2[:], ALU.mult)
            for _ in range(2):
                P.ts("dve", tm[:, 0:n], tw[:, 0:n], 0.5, ALU.is_gt)
                P.tt("dve", tw[:, 0:n], tw[:, 0:n], tm[:, 0:n], ALU.subtract)
                P.ts("dve", tm[:, 0:n], tw[:, 0:n], -0.5, ALU.is_lt)
                P.tt("dve", tw[:, 0:n], tw[:, 0:n], tm[:, 0:n], ALU.add)
            h = hb[layer]
            P.act(h[:, 0:n], tw[:, 0:n], AF.Sin, scale=TWO_PI_LO)
            cur = h[:, 0:n]
        P.dma("sp", hdn_d[:, p0:p0 + n], cur, is_output=True)


class HyFFT:
    def __init__(self, P, cd):
        self.P = P
        ld = lambda name: self._ld(cd[name], HC_SHAPES[name])
        self.FA, self.CS, self.NSC = ld("hc_FA"), ld("hc_CS"), ld("hc_NSC")
        self.Cm, self.Sm, self.NSm = ld("hc_Cm"), ld("hc_Sm"), ld("hc_NSm")
        self.TWr, self.TWi = ld("hc_TWr"), ld("hc_TWi")
        self.psA = [P.ps("fpA") for _ in range(2)]
        self.psB = [P.ps("fpB") for _ in range(4)]
        self.ia = 0
        self.ib = 0
        self.t = [P.sb("ft", [128, 512]) for _ in range(4)]
        G = HY_G
        self.F = [P.sb("fF", [128, G, 128]) for _ in range(4)]
        self.K = [P.sb("fK", [128, G, 128]) for _ in range(2)]

    def _ld(self, d, shape):
        t = self.P.sb("hc", shape)
        self.P.dma("sp", t[:], d)
        return t

    def cmul_from_psum(self, ps_re, ps_im, br, bi, out_re, out_im, conj=False):
        P = self.P
        t1, t2, t3, t4 = [x[:, 0:1] for x in self.t]
        shp = ps_re.shape
        n = 1
        for s in shp[1:]:
            n *= s
        v = lambda x: x[:, 0:n].rearrange("p (a b) -> p a b", a=shp[1]) if len(shp) == 3 else x[:, 0:n]
        t1, t2, t3, t4 = [v(x) for x in self.t]
        P.tt("dve", t1, ps_re, br, ALU.mult)
        P.tt("dve", t2, ps_im, bi, ALU.mult)
        P.tt("pool", out_re, t1, t2, ALU.add if conj else ALU.subtract)
        P.tt("dve", t3, ps_re, bi, ALU.mult)
        P.tt("dve", t4, ps_im, br, ALU.mult)
        P.tt("pool", out_im, t4, t3, ALU.subtract if conj else ALU.add)

    def fwd(self, x, npart, dst_re, dst_im, kmul=None):
        P = self.P
        G = HY_G
        Ar, Ai = self.F[0], self.F[1]
        for c0 in range(0, G, 2):
            self.ia += 1
            ps = self.psA[self.ia % 2]
            for j in range(2):
                P.mm(ps[:, j * 256:(j + 1) * 256], x[0:npart, c0 + j, :], self.FA[0:npart, :])
            p4 = ps[:].rearrange("p (c r k) -> p c r k", c=2, r=2)
            twr = self.TWr[:].unsqueeze(1).to_broadcast([128, 2, 128])
            twi = self.TWi[:].unsqueeze(1).to_broadcast([128, 2, 128])
            self.cmul_from_psum(p4[:, :, 0, :], p4[:, :, 1, :], twr, twi, Ar[:, c0:c0 + 2, :], Ai[:, c0:c0 + 2, :])
        for c0 in range(0, G, 4):
            self.ib += 1
            pr, pi = self.psB[(2 * self.ib) % 4], self.psB[(2 * self.ib + 1) % 4]
            ar = Ar[:, c0:c0 + 4, :].rearrange("p c k -> p (c k)")
            ai = Ai[:, c0:c0 + 4, :].rearrange("p c k -> p (c k)")
            P.mm(pr[:], self.Cm[:], ar, start=True, stop=False)
            P.mm(pr[:], self.Sm[:], ai, start=False, stop=True)
            P.mm(pi[:], self.Cm[:], ai, start=True, stop=False)
            P.mm(pi[:], self.NSm[:], ar, start=False, stop=True)
            pr3 = pr[:].rearrange("p (c k) -> p c k", c=4)
            pi3 = pi[:].rearrange("p (c k) -> p c k", c=4)
            if kmul is None:
                P.copy("act", dst_re[:, c0:c0 + 4, :], pr3)
                P.copy("act", dst_im[:, c0:c0 + 4, :], pi3)
            else:
                self.cmul_from_psum(pr3, pi3, kmul[0][:, c0:c0 + 4, :], kmul[1][:, c0:c0 + 4, :],
                                    dst_re[:, c0:c0 + 4, :], dst_im[:, c0:c0 + 4, :])

    def inv(self, Yr, Yi, consume):
        P = self.P
        G = HY_G
        Br, Bi = self.F[0], self.F[1]
        for c0 in range(0, G, 2):
            self.ia += 1
            ps = self.psA[self.ia % 2]
            for j in range(2):
                o = ps[:, j * 256:(j + 1) * 256]
                P.mm(o, Yr[:, c0 + j, :], self.CS[:], start=True, stop=False)
                P.mm(o, Yi[:, c0 + j, :], self.NSC[:], start=False, stop=True)
            p4 = ps[:].rearrange("p (c r k) -> p c r k", c=2, r=2)
            twr = self.TWr[:].unsqueeze(1).to_broadcast([128, 2, 128])
            twi = self.TWi[:].unsqueeze(1).to_broadcast([128, 2, 128])
            self.cmul_from_psum(p4[:, :, 0, :], p4[:, :, 1, :], twr, twi, Br[:, c0:c0 + 2, :], Bi[:, c0:c0 + 2, :],
                                conj=True)
        for c0 in range(0, G, 4):
            self.ib += 1
            pr = self.psB[self.ib % 4]
            br = Br[:, c0:c0 + 4, :].rearrange("p c k -> p (c k)")
            bi = Bi[:, c0:c0 + 4, :].rearrange("p c k -> p (c k)")
            P.mm(pr[0:64, :], self.Cm[:, 0:64], br, start=True, stop=False)
            P.mm(pr[0:64, :], self.NSm[:, 0:64], bi, start=False, stop=True)
            consume(c0, pr[0:64, :].rearrange("p (c k) -> p c k", c=4))


def hyena_lat(P, zhy_d, mix_d, hdn_d, w4_d, convw_d, convb_d, hbias_d, cd, t_off, n_groups=HY_W // HY_G):
    G = HY_G
    ff = HyFFT(P, cd)
    ones = P.sb("hones", [128, 128]); P.memset("pool", ones[:], 1.0)
    tpos = P.sb("htpos", [128, 128]); P.dma("sp", tpos[:], cd["hc_tposL"])
    ndel = P.sb("hndel", [128, 2 * HY_W]); P.dma("act", ndel[:], cd["hc_ndelL"])
    w4 = P.sb("hw4", [64, 4096]); P.dma("sp", w4[:], w4_d)
    w4v = w4[:].rearrange("p (d o c) -> p d o c", d=2, o=2)
    T = [P.sb("hT", [128, G, 128]) for _ in range(3)]
    sh = [P.sb("hsh", [64, G, 128]) for _ in range(2)]
    hd = [P.sb("hhd", [64, 2048]) for _ in range(2)]
    cw = P.sb("hcw", [64, 3, 3, G]); cb = P.sb("hcb", [64, 3, G]); hbv = P.sb("hhb", [64, 2, G])
    red = P.sb("hred", [128, G]); rinv = P.sb("hrinv", [128, G])
    psK = [P.ps("hpsK") for _ in range(2)]
    dec = P.sb("hdec", [128, 16, G])
    L = SEQ
    for g in range(n_groups):
        c0 = g * G
        for j in range(3):
            for s in range(3):
                P.dma(("sp", "act")[(j + s) % 2], cw[:, j, s, :],
                      convw_d[s:s + 1, j * HY_W + c0:j * HY_W + c0 + G].to_broadcast([64, G]))
            P.dma("sp", cb[:, j, :], convb_d[j * HY_W + c0:j * HY_W + c0 + G].unsqueeze(0).to_broadcast([64, G]))
        for o in range(2):
            P.dma("act", hbv[:, o, :], hbias_d[o:o + 1, c0:c0 + G].to_broadcast([64, G]))
        for j in range(3):
            row0 = j * HY_W + c0
            src = lambda s: zhy_d[row0:row0 + G, t_off + s:t_off + s + L].rearrange("c (a b) -> a c b", b=128)
            dst = T[j]
            P.dma("sp", dst[0:64, :, :], src(0))
            P.dma("act", sh[0][:, :, 1:128], zhy_d[row0:row0 + G, t_off:t_off + L].rearrange("c (a b) -> a c b", b=128)[:, :, 0:127])
            P.dma("act", sh[0][1:64, :, 0:1], zhy_d[row0:row0 + G, t_off:t_off + L].rearrange("c (a b) -> a c b", b=128)[0:63, :, 127:128], slow=True)
            P.memset("pool", sh[0][0:1, :, 0:1], 0.0)
            P.dma("sp", sh[1][:, :, 0:127], zhy_d[row0:row0 + G, t_off:t_off + L].rearrange("c (a b) -> a c b", b=128)[:, :, 1:128])
            P.memset("pool", sh[1][0:64, :, 127:128], 0.0)
            P.dma("sp", sh[1][0:63, :, 127:128], zhy_d[row0:row0 + G, t_off:t_off + L].rearrange("c (a b) -> a c b", b=128)[1:64, :, 0:1], slow=True)
            bc = lambda a: a.unsqueeze(2).to_broadcast([64, G, 128])
            P.tt("dve", dst[0:64], dst[0:64], bc(cw[:, j, 1, :]), ALU.mult)
            P.tt("pool", sh[0][:], sh[0][:], bc(cw[:, j, 0, :]), ALU.mult)
            P.tt("dve", dst[0:64], dst[0:64], sh[0][:], ALU.add)
            P.tt("pool", sh[1][:], sh[1][:], bc(cw[:, j, 2, :]), ALU.mult)
            P.tt("dve", dst[0:64], dst[0:64], sh[1][:], ALU.add)
            P.tt("dve", dst[0:64], dst[0:64], bc(cb[:, j, :]), ALU.add)
        for o in range(2):
            kk = ff.F[2]
            for n2b in range(0, 128, 16):
                hb_ = hd[(n2b // 16) % 2]
                P.dma(("sp", "act")[(n2b // 16) % 2], hb_[:], hdn_d[:, n2b * 128:(n2b + 16) * 128])
                ps, psb = psK[0], psK[1]
                for j in range(16):
                    P.mm(ps[:, j * G:(j + 1) * G], hb_[:, j * 128:(j + 1) * 128], w4v[:, 0, o, c0:c0 + G])
                    P.mm(psb[:, j * G:(j + 1) * G], hb_[:, j * 128:(j + 1) * 128], w4v[:, 1, o, c0:c0 + G])
                P.tt("dve", dec[:], tpos[:, n2b:n2b + 16].unsqueeze(2).to_broadcast([128, 16, G]),
                     ndel[:, o * HY_W + c0:o * HY_W + c0 + G].unsqueeze(1).to_broadcast([128, 16, G]), ALU.mult)
                P.act(dec[:], dec[:], AF.Exp)
                P.tt("dve", kk[0:64, :, n2b:n2b + 16].rearrange("p c n -> p n c"),
                     ps[0:64, 0:16 * G].rearrange("p (n c) -> p n c", n=16), dec[0:64], ALU.mult)
                P.tt("dve", kk[64:128, :, n2b:n2b + 16].rearrange("p c n -> p n c"),
                     psb[64:128, 0:16 * G].rearrange("p (n c) -> p n c", n=16), dec[64:128], ALU.mult)
            P.op("dve", lambda e, o_=red[:], i_=kk[:]: e.tensor_reduce(o_, i_, AX.X, ALU.add, apply_absolute_value=True),
                 reads=[kk[:]], writes=[red[:]])
            psn = psK[0]
            P.mm(psn[:, 0:G], ones[:], red[:])
            P.recip(rinv[:], psn[:, 0:G])
            P.tt("dve", kk[:], kk[:], rinv[:].unsqueeze(2).to_broadcast([128, G, 128]), ALU.mult)
            P.memset("pool", kk[64:65, :, 0:1], 0.0)
            ff.fwd(kk, 128, ff.K[0], ff.K[1])
            y = T[2]
            gate = T[o]
            ff.fwd(y, 64, ff.F[2], ff.F[3], kmul=(ff.K[0], ff.K[1]))

            def consume(cc, ps3, o=o, y=y, gate=gate):
                t = ff.t[0][0:64, :].rearrange("p (c k) -> p c k", c=4)
                P.tt("pool", t, y[0:64, cc:cc + 4, :], hbv[:, o, cc:cc + 4].unsqueeze(2).to_broadcast([64, 4, 128]), ALU.mult)
                P.stt("dve", t, ps3, 1.0 / FFT_N, t, ALU.mult, ALU.add)
                P.tt("dve", y[0:64, cc:cc + 4, :], t, gate[0:64, cc:cc + 4, :], ALU.mult)
            ff.inv(ff.F[2], ff.F[3], consume)
        P.dma("sp", mix_d[c0:c0 + G, t_off:t_off + L].rearrange("c (a b) -> a c b", b=128), T[2][0:64, :, :], is_output=True)


def hyena_ctx(P, zhy_d, mix_d, hdn_d, w4_d, convw_d, convb_d, hbias_d, cd, n_ct=8):
    Lc = CTX
    w4 = P.sb("cw4", [64, 4096]); P.dma("sp", w4[:], w4_d)
    w4v = w4[:].rearrange("p (d o c) -> p d o c", d=2, o=2)
    hdn = P.sb("chdn", [64, Lc]); P.dma("act", hdn[:], hdn_d)
    tpos = P.sb("ctpos", [128, Lc]); P.dma("sp", tpos[:], cd["hc_tposC"])
    ndel = P.sb("cndel", [128, 16]); P.dma("act", ndel[:], cd["hc_ndelC"])
    cw = P.sb("ccw", [128, 24, 3]); cb = P.sb("ccb", [128, 24]); hbv = P.sb("chb", [128, 2, 8])
    P.dma("sp", cw[:], convw_d); P.dma("act", cb[:], convb_d); P.dma("sp", hbv[:], hbias_d)
    zp = P.sb("czp", [128, Lc + 2])
    X = [P.sb("cX", [128, Lc]) for _ in range(3)]
    hf = [P.sb("chf", [128, Lc]) for _ in range(2)]
    dec = P.sb("cdec", [128, Lc])
    acc = [P.sb("cacc", [128, Lc]) for _ in range(2)]
    rd = P.sb("crd", [128, 4])
    ps = [P.ps("cps") for _ in range(2)]
    P.memset("pool", zp[:], 0.0)
    for ct in range(n_ct):
        for j in range(3):
            k = j * 8 + ct
            P.dma("sp", zp[:, 1:Lc + 1], zhy_d[j * HY_W + ct * 128:j * HY_W + (ct + 1) * 128, 0:Lc])
            P.ts("dve", X[j][:], zp[:, 1:Lc + 1], cw[:, k, 1:2], ALU.mult, cb[:, k:k + 1], ALU.add)
            P.stt("dve", X[j][:], zp[:, 0:Lc], cw[:, k, 0:1], X[j][:], ALU.mult, ALU.add)
            P.stt("dve", X[j][:], zp[:, 2:Lc + 2], cw[:, k, 2:3], X[j][:], ALU.mult, ALU.add)
        y = X[2]
        for o in range(2):
            P.act(dec[:], tpos[:], AF.Exp, scale=ndel[:, o * 8 + ct:o * 8 + ct + 1])
            for d in range(2):
                P.mm(ps[d][:, 0:Lc], w4v[:, d, o, ct * 128:(ct + 1) * 128], hdn[:])
                P.tt("dve", hf[d][:], ps[d][:, 0:Lc], dec[:], ALU.mult)
                P.op("dve", lambda e, o_=rd[:, d:d + 1], i_=hf[d][:]: e.tensor_reduce(o_, i_, AX.X, ALU.add, apply_absolute_value=True),
                     reads=[hf[d][:]], writes=[rd[:, d:d + 1]])
            P.tt("dve", rd[:, 2:3], rd[:, 0:1], rd[:, 1:2], ALU.add)
            P.recip(rd[:, 3:4], rd[:, 2:3])
            for d in range(2):
                P.ts("dve", hf[d][:], hf[d][:], rd[:, 3:4], ALU.mult)
            P.ts("dve", acc[0][:], y[:], hf[0][:, 0:1], ALU.mult)
            P.memset("pool", acc[1][:], 0.0)
            for tau in range(1, Lc):
                n = Lc - tau
                P.stt("dve", acc[0][:, tau:Lc], y[:, 0:n], hf[0][:, tau:tau + 1], acc[0][:, tau:Lc], ALU.mult, ALU.add)
                P.stt("dve", acc[1][:, 0:n], y[:, tau:Lc], hf[1][:, tau:tau + 1], acc[1][:, 0:n], ALU.mult, ALU.add)
            P.tt("pool", acc[0][:], acc[0][:], acc[1][:], ALU.add)
            P.stt("dve", acc[0][:], y[:], hbv[:, o, ct:ct + 1], acc[0][:], ALU.mult, ALU.add)
            P.tt("dve", y[:], acc[0][:], X[o][:], ALU.mult)
        P.dma("sp", mix_d[ct * 128:(ct + 1) * 128, 0:Lc], y[:], is_output=True)


def mod_phase(P, cT_d, modw_d, modb_d, modv_d):
    C = Ctx(P)
    ct = P.sb("ct", [128, KD, 2]); sg = P.sb("sg", [128, KD, 2]); bt = P.sb("bt", [128, 384])
    ot = P.sb("ot", [128, 96, 2])
    P.dma("sp", ct[:], cT_d); P.dma("act", bt[:], modb_d)
    P.act(sg[:], ct[:], AF.Sigmoid)
    P.tt("dve", ct[:], ct[:], sg[:], ALU.mult)
    for l in range(DEPTH):
        for cc in range(96):
            ps = dense_chunk(P, C, modw_d[l][cc], KD, ct, 2)
            P.ts("dve", ot[:, cc, :], ps[:, 0:2], bt[:, l * 96 + cc:l * 96 + cc + 1], ALU.add)
        P.dma("sp", modv_d[l], ot[:], is_output=True)


def rope_apply2(P, C, S, out_dram, x_ps, TT, cos_t, sin_t, rT, xr):
    P.copy("dve", xr[0:64, 0:TT], x_ps)
    ps2 = C.psum()
    P.mm(ps2[0:64, 0:TT], rT[0:64, 0:64], xr[0:64, 0:TT])
    st = S.get()
    P.tt("dve", st[0:64, 0:TT], xr[0:64, 0:TT], cos_t[0:64, 0:TT], ALU.mult)
    P.tt("dve", xr[0:64, 0:TT], ps2[0:64, 0:TT], sin_t[0:64, 0:TT], ALU.mult)
    P.tt("pool", st[0:64, 0:TT], st[0:64, 0:TT], xr[0:64, 0:TT], ALU.add)
    P.dma(C.q(), out_dram, st[0:64, 0:TT], is_output=True)


def t1_phase(P, h_d, mod_d, w_in_d, qg_d, kvg_d, wqn_d, wqr_d, wkn_d, wv_d, cos_d, sin_d, rT_d,
             zhy_d, qT_d, kT_d, krT_d, vtm_d, nt_ctx, nt_lat):
    C = Ctx(P)
    S = Stage(P)
    mt = load_mod(P, mod_d)
    rT = P.sb("rT", [64, 64]); qgt = P.sb("qgt", [128, 4]); kvgt = P.sb("kvgt", [128, 2])
    wv = P.sb("wv", [128, 2, 1024])
    P.dma("sp", rT[:], rT_d); P.dma("act", qgt[:], qg_d); P.dma("sp", kvgt[:], kvg_d)
    P.dma("act", wv[:], wv_d.rearrange("(k p) n -> p k n", p=128))
    cos_t = P.sb("cos_t", [64, 512]); sin_t = P.sb("sin_t", [64, 512])
    x = P.sb("x", [128, KD, 512]); xin = P.sb("xin", [128, KD, 512])
    zl = P.sb("zl", [128, 6, 512]); sq = P.sb("sq", [128, 6, 512]); zn = P.sb("zn", [128, 6, 512])
    rs = P.sb("rs", [128, 512]); xr = P.sb("xr", [64, 512]); tmp1 = P.sb("tmp1", [128, KD])
    hT3 = h_d.rearrange("(k p) t -> p k t", p=128)
    for (t0, TT, sel) in tiles_for(nt_ctx, nt_lat, 512):
        P.dma("sp", x[:, :, 0:TT], hT3[:, :, t0:t0 + TT])
        P.dma("act", cos_t[:, 0:TT], cos_d[:, t0:t0 + TT])
        P.dma("act", sin_t[:, 0:TT], sin_d[:, t0:t0 + TT])
        modulate(P, xin, x, mt, 0, 1, sel, TT, tmp1)
        for cc in range(31):
            m = 128 if cc < 30 else 64
            ps = dense_chunk(P, C, w_in_d[cc][:, :, 0:m], KD, xin, TT, m=m)
            if cc < 24:
                st = S.get()
                P.copy(S.eng(), st[:, 0:TT], ps[:, 0:TT])
                P.dma(C.q(), zhy_d[cc * 128:(cc + 1) * 128, t0:t0 + TT], st[:, 0:TT], is_output=True)
            elif cc < 30:
                P.copy(S.eng(), zl[:, cc - 24, 0:TT], ps[:, 0:TT])
            else:
                rope_apply2(P, C, S, krT_d[:, t0:t0 + TT], ps[0:64, 0:TT], TT, cos_t, sin_t, rT, xr)
        for (k0, nk, gt) in ((0, 4, qgt), (4, 2, kvgt)):
            pss = col_stats(P, C, zl[:, k0:k0 + nk, :], nk, TT, None, sq)
            rstd_from(P, rs, pss, TT, 1.0 / (nk * 128), RMS_EPS)
            for k in range(nk):
                P.stt("dve", zn[:, k0 + k, 0:TT], zl[:, k0 + k, 0:TT], gt[:, k:k + 1], rs[:, 0:TT], ALU.mult, ALU.mult)
        for h in range(8):
            ps = dense_chunk(P, C, wqn_d[h], 4, zn[:, 0:4, :], TT)
            st = S.get()
            P.copy(S.eng(), st[:, 0:TT], ps[:, 0:TT])
            P.dma(C.q(), qT_d[h, 0:128, t0:t0 + TT], st[:, 0:TT], is_output=True)
            ps = dense_chunk(P, C, wqr_d[h], 4, zn[:, 0:4, :], TT, m=64)
            rope_apply2(P, C, S, qT_d[h, 128:192, t0:t0 + TT], ps[0:64, 0:TT], TT, cos_t, sin_t, rT, xr)
            ps = dense_chunk(P, C, wkn_d[h], 2, zn[:, 4:6, :], TT)
            st = S.get()
            P.copy(S.eng(), st[:, 0:TT], ps[:, 0:TT])
            P.dma(C.q(), kT_d[h, :, t0:t0 + TT], st[:, 0:TT], is_output=True)
        for s0 in range(0, TT, 128):
            for n0 in range(0, 1024, 512):
                ps = C.psum()
                for k in range(2):
                    P.mm(ps[:, :], zn[:, 4 + k, s0:s0 + 128], wv[:, k, n0:n0 + 512], start=(k == 0), stop=(k == 1))
                st = S.get()
                P.copy(S.eng(), st[:, :], ps[:, :])
                P.dma(C.q(), vtm_d[t0 + s0:t0 + s0 + 128, n0:n0 + 512], st[:, :], is_output=True)


def attn_phase(P, qT_d, kT_d, krT_d, vtm_d, mix_d, ident_d, nheads=8):
    ktn = P.sb("ktn", [128, NTOK]); ktr = P.sb("ktr", [64, NTOK])
    va = P.sb("va", [128, NKC, 129])
    ident = P.sb("ident", [128, 128]); P.dma("sp", ident[:], ident_d)
    qn = [P.sb("qn", [128, 512]) for _ in range(2)]
    qr = [P.sb("qr", [64, 512]) for _ in range(2)]
    pts = [P.sb("pt", [128, 512]) for _ in range(3)]
    ps_s = [P.ps("ps_s") for _ in range(3)]
    acc = [P.ps("acc") for _ in range(4)]
    ps_t = P.ps("ps_t")
    rc = P.sb("rc", [128, 4])
    ost = [P.sb("ost", [128, 4, 128]) for _ in range(2)]
    obT = [P.sb("obT", [128, 512]) for _ in range(2)]
    P.memset("pool", va[:, :, 128:129], 1.0)
    P.dma("act", ktr[:], krT_d)
    blocks = [(0, 256, 2)] + [(CTX + i * 512, 512, NKC) for i in range(SEQ // 512)]
    bi = 0
    ci = 0
    for h in range(nheads):
        P.dma("sp", ktn[:], kT_d[h])
        P.dma("sp", va[:, :, 0:128], vtm_d[:, h * 128:(h + 1) * 128].rearrange("(c p) d -> p c d", p=128))
        for (q0, TQ, nkc) in blocks:
            bi += 1
            qnb, qrb, ob, obt = qn[bi % 2], qr[bi % 2], ost[bi % 2], obT[bi % 2]
            P.dma("sp", qnb[:, 0:TQ], qT_d[h, 0:128, q0:q0 + TQ])
            P.dma("act", qrb[:, 0:TQ], qT_d[h, 128:192, q0:q0 + TQ])
            nsub = TQ // 128

            def s_mm(kc):
                ps = ps_s[(ci + kc) % 3]
                P.mm(ps[:, 0:TQ], ktn[:, kc * 128:(kc + 1) * 128], qnb[:, 0:TQ], start=True, stop=False)
                P.mm(ps[:, 0:TQ], ktr[:, kc * 128:(kc + 1) * 128], qrb[:, 0:TQ], start=False, stop=True)

            s_mm(0)
            for kc in range(nkc):
                if kc + 1 < nkc:
                    s_mm(kc + 1)
                ps = ps_s[(ci + kc) % 3]
                pt = pts[(ci + kc) % 3]
                P.act(pt[:, 0:TQ], ps[:, 0:TQ], AF.Exp, scale=MLA_SCALE)
                for s in range(nsub):
                    P.mm(acc[s][:, 0:129], pt[:, s * 128:(s + 1) * 128], va[:, kc, :],
                         start=(kc == 0), stop=(kc == nkc - 1))
            ci += nkc
            for s in range(nsub):
                P.recip(rc[:, s:s + 1], acc[s][:, 128:129])
                P.ts("dve", ob[:, s, :], acc[s][:, 0:128], rc[:, s:s + 1], ALU.mult)
            for s in range(nsub):
                P.transpose(ps_t[:, s * 128:(s + 1) * 128], ob[:, s, :], ident[:])
            P.copy("act", obt[:, 0:TQ], ps_t[:, 0:TQ])
            P.dma("sp", mix_d[1024 + h * 128:1024 + (h + 1) * 128, q0:q0 + TQ], obt[:, 0:TQ], is_output=True)


def copy_phase(P, dst_d, src_d, rows, cols, col0=0):
    bufs = [P.sb("cp", [128, 4096]) for _ in range(2)]
    i = 0
    for r0 in range(0, rows, 128):
        for c0 in range(0, cols, 4096):
            n = min(4096, cols - c0)
            b = bufs[i % 2]; i += 1
            P.dma("sp", b[:, 0:n], src_d[r0:r0 + 128, col0 + c0:col0 + c0 + n])
            P.dma("act", dst_d[r0:r0 + 128, c0:c0 + n], b[:, 0:n], is_output=True)


def v16(a):
    return np.ascontiguousarray(a.reshape(-1, 128).T)


def build_mega(depth=DEPTH, dump=False):
    nc = new_nc()
    dt = lambda name, shape, kind="ExternalInput": nc.dram_tensor(name, list(shape), F32, kind=kind).ap()
    it = lambda name, shape: nc.dram_tensor(name, list(shape), F32).ap()
    nmod = min(DEPTH, depth + 1)
    hT0 = dt("hT0", [D, NTOK]); cT = dt("cT", [128, KD, 2]); modb = dt("modb", [128, 384])
    modw = [dt("modw%d" % l, [96, 128, KD, 128]) for l in range(nmod)]
    ident = dt("ident", [128, 128]); ropec = dt("ropec", [64, NTOK]); ropes = dt("ropes", [64, NTOK]); rT = dt("rT", [64, 64])
    cd = {k: dt(k, shp) for k, shp in HC_SHAPES.items()}
    L = []
    for l in range(depth):
        w = {}
        for n, shp in (("lng", [128, KD]), ("lnb", [128, KD]), ("lfg", [128, KD]), ("lfb", [128, KD]),
                       ("wq", [16, 128, KD, 128]), ("keysT", [128, 16, 128]), ("pu", [16384, D]), ("pv", [16384, D]),
                       ("w_o", [16, 128, KD, 128])):
            w[n] = dt("%s_%d" % (n, l), shp)
        if l % 2 == 0:
            for n, shp in (("w_in", [31, 128, KD, 128]), ("qg", [128, 4]), ("kvg", [128, 2]), ("wqn", [8, 128, 4, 128]),
                           ("wqr", [8, 128, 4, 64]), ("wkn", [8, 128, 2, 128]), ("wv", [256, 1024]),
                           ("hw1", [33, 64]), ("hw23", [2, 64, 64]), ("hbb", [64, 3]), ("hfq", [64, 1]), ("hw4", [64, 4096]),
                           ("convw", [3, 3072]), ("convb", [3072]), ("hbias", [2, 1024]),
                           ("convw_f", [128, 24, 3]), ("convb_f", [128, 24]), ("hbias_f", [128, 2, 8])):
                w[n] = dt("%s_%d" % (n, l), shp)
        else:
            for n, shp in (("s5p", [2, 64, 128, 67]), ("dsk", [128, KD]), ("w_g", [16, 128, KD, 128])):
                w[n] = dt("%s_%d" % (n, l), shp)
        L.append(w)
    out = dt("out", [D, SEQ], "ExternalOutput")
    h = it("h", [D, NTOK]); zhy = it("zhy", [3072, NTOK]); qT = it("qT", [8, 192, NTOK]); kT = it("kT", [8, 128, NTOK])
    krT = it("krT", [64, NTOK]); vtm = it("vtm", [NTOK, 1024]); mix = it("mix", [D, NTOK]); s5in = it("s5in", [D, NTOK])
    modv = it("modv", [DEPTH, 128, 96, 2]); hdnL = it("hdnL", [64, FFT_N]); hdnC = it("hdnC", [64, CTX])
    dumps = {}
    stk = ExitStack(); stk.__enter__()
    P = Prog(nc, stk)
    with P.phase():
        copy_phase(P, h, hT0, D, NTOK)
    with P.phase():
        C = Ctx(P)
        ct = P.sb("ct", [128, KD, 2]); sg = P.sb("sg", [128, KD, 2]); bt = P.sb("bt", [128, 384]); ot = P.sb("ot", [128, 96, 2])
        P.dma("sp", ct[:], cT); P.dma("act", bt[:], modb)
        P.act(sg[:], ct[:], AF.Sigmoid)
        P.tt("dve", ct[:], ct[:], sg[:], ALU.mult)
        for l in range(nmod):
            for cc in range(96):
                ps = dense_chunk(P, C, modw[l][cc], KD, ct, 2)
                P.ts("dve", ot[:, cc, :], ps[:, 0:2], bt[:, l * 96 + cc:l * 96 + cc + 1], ALU.add)
            P.dma("sp", modv[l], ot[:], is_output=True)
    for l in range(depth):
        w = L[l]
        nxt_odd = (l + 1 < DEPTH) and ((l + 1) % 2 == 1)
        if l % 2 == 0:
            with P.phase():
                t1_phase(P, h, modv[l], w["w_in"], w["qg"], w["kvg"], w["wqn"], w["wqr"], w["wkn"], w["wv"], ropec, ropes, rT,
                         zhy, qT, kT, krT, vtm, CTX, SEQ)
            with P.phase():
                hyena_mlp(P, cd["hc_featL"], FFT_N, hdnL, w["hw1"], w["hw23"], w["hbb"], w["hfq"])
            with P.phase():
                hyena_mlp(P, cd["hc_featC"], CTX, hdnC, w["hw1"], w["hw23"], w["hbb"], w["hfq"])
            with P.phase():
                hyena_lat(P, zhy, mix, hdnL, w["hw4"], w["convw"], w["convb"], w["hbias"], cd, CTX)
            with P.phase():
                hyena_ctx(P, zhy, mix, hdnC, w["hw4"], w["convw_f"], w["convb_f"], w["hbias_f"], cd)
            with P.phase():
                attn_phase(P, qT, kT, krT, vtm, mix, ident)
        else:
            with P.phase():
                s5_phase(P, s5in, mix, w["s5p"], ident, CTX, SEQ)
        if dump:
            dumps["mix%d" % l] = dt("dump_mix%d" % l, [D, NTOK], "ExternalOutput")
            with P.phase():
                copy_phase(P, dumps["mix%d" % l], mix, D, NTOK)
        with P.phase():
            post_phase(P, l % 2 == 1, h, mix, modv[l], modv[l + 1] if nxt_odd else None, w["w_o"], w.get("w_g"), w.get("dsk"),
                       w["lng"], w["lnb"], w["lfg"], w["lfb"], w["wq"], w["keysT"], ident, w["pu"], w["pv"],
                       s5in if nxt_odd else None, CTX, SEQ)
        if dump:
            dumps["h%d" % l] = dt("dump_h%d" % l, [D, NTOK], "ExternalOutput")
            with P.phase():
                copy_phase(P, dumps["h%d" % l], h, D, NTOK)
    with P.phase():
        copy_phase(P, out, h, D, SEQ, col0=CTX)
    stk.close()
    return nc


def make_inputs(inputs, b, depth=DEPTH):
    g = lambda k: np.asarray(inputs[k], dtype=np.float32)
    nmod = min(DEPTH, depth + 1)
    m = {}
    m["hT0"] = np.ascontiguousarray(np.concatenate([g("ctx")[b], g("x")[b]], 0).T)
    m["cT"] = np.ascontiguousarray(np.stack([g("c")[b], g("c_ctx")], -1).reshape(KD, 128, 2).transpose(1, 0, 2))
    m["modb"] = np.ascontiguousarray(g("mod_b").reshape(DEPTH * 96, 128).T)
    for l in range(nmod):
        m["modw%d" % l] = wl_layout(g("mod_w")[l])
    m["ident"] = np.eye(128, dtype=np.float32)
    cos, sin = rope_tables()
    m["ropec"] = np.ascontiguousarray(np.concatenate([np.ones((CTX, 64), np.float32), cos], 0).T)
    m["ropes"] = np.ascontiguousarray(np.concatenate([np.zeros((CTX, 64), np.float32), sin], 0).T)
    m["rT"] = rope_rT()
    m.update(hyena_consts())
    for l in range(depth):
        i = l // 2
        s = lambda n: "%s_%d" % (n, l)
        m[s("lng")], m[s("lnb")] = v16(g("ln_mix_g")[l]), v16(g("ln_mix_b")[l])
        m[s("lfg")], m[s("lfb")] = v16(g("ln_ffn_g")[l]), v16(g("ln_ffn_b")[l])
        m[s("wq")] = wl_layout(g("peer_w_q")[l])
        m[s("keysT")] = np.ascontiguousarray(g("peer_keys")[l].reshape(16, 128, 128).transpose(2, 0, 1))
        m[s("pu")], m[s("pv")] = g("peer_u")[l], g("peer_v")[l]
        if l % 2 == 0:
            m[s("w_o")] = wl_layout(g("ev_w_o")[i])
            w_in = np.concatenate([g("ev_w_in")[i], np.zeros((D, 64), np.float32)], axis=1)
            m[s("w_in")] = wl_layout(w_in)
            m[s("qg")], m[s("kvg")] = v16(g("mla_q_norm")[i]), v16(g("mla_kv_norm")[i])
            uq = g("mla_w_uq")[i].reshape(512, 8, 192)
            m[s("wqn")] = np.stack([wl_layout(np.ascontiguousarray(uq[:, h, :128]))[0] for h in range(8)])
            m[s("wqr")] = np.stack([np.ascontiguousarray(uq[:, h, 128:].reshape(4, 128, 64).transpose(1, 0, 2)) for h in range(8)])
            ukv = g("mla_w_ukv")[i].reshape(256, 8, 256)
            m[s("wkn")] = np.stack([wl_layout(np.ascontiguousarray(ukv[:, h, :128]))[0] for h in range(8)])
            m[s("wv")] = np.ascontiguousarray(ukv[:, :, 128:].reshape(256, 1024))
            m[s("hw1")] = g("hy_w1")[i]
            m[s("hw23")] = np.ascontiguousarray(np.stack([g("hy_w2")[i], g("hy_w3")[i]]))
            m[s("hbb")] = np.ascontiguousarray(np.stack([g("hy_b1")[i], g("hy_b2")[i], g("hy_b3")[i]], 1))
            m[s("hfq")] = np.ascontiguousarray(g("hy_freq")[i][:, None])
            m[s("hw4")] = g("hy_w4")[i]
            cw, cb, hb = g("ev_conv_w")[i], g("ev_conv_b")[i], g("hy_bias")[i]
            m[s("convw")], m[s("convb")], m[s("hbias")] = cw, cb, hb
            m[s("convw_f")] = np.ascontiguousarray(cw.T.reshape(24, 128, 3).transpose(1, 0, 2))
            m[s("convb_f")] = np.ascontiguousarray(cb.reshape(24, 128).T)
            m[s("hbias_f")] = np.ascontiguousarray(hb.reshape(2, 8, 128).transpose(2, 0, 1))
        else:
            m[s("w_o")] = wl_layout(g("od_w_o")[i])
            m[s("w_g")] = wl_layout(g("od_w_g")[i])
            m[s("dsk")] = v16(g("s5_d")[i])
            m[s("s5p")] = s5_pack_params(g("s5_lam_re")[i], g("s5_lam_im")[i], g("s5_log_dt")[i], g("s5_b_re")[i],
                                         g("s5_b_im")[i], g("s5_c_re")[i], g("s5_c_im")[i])
    return m


def kernel(**inputs):
    nc = build_mega()
    maps = [make_inputs(inputs, b) for b in range(BATCH)]
    res = run_spmd(nc, maps)
    return np.ascontiguousarray(np.stack([res[b]["out"].T for b in range(BATCH)], 0)).astype(np.float32)
```

```python
import math
from contextlib import ExitStack

import numpy as np
import concourse.bass as bass
import concourse.mybir as mybir
from concourse.bass_utils import run_bass_kernel_spmd

F32 = mybir.dt.float32
I32 = mybir.dt.int32
U32 = mybir.dt.uint32
BF16 = mybir.dt.bfloat16
ALU = mybir.AluOpType
AF = mybir.ActivationFunctionType
AX = mybir.AxisListType

NCORES = 8
VERBOSE = False
BF16_TABLES = True


class Prog:
    ENGS = ("sp", "act", "pe", "dve", "pool")
    NDMA = 12

    def __init__(self, nc, stack):
        self.nc = nc
        self.stack = stack
        self.ops = {e: [] for e in self.ENGS}
        self.cnt = {e: 0 for e in self.ENGS}
        self.sem = {e: stack.enter_context(nc.semaphore("sem_" + e)) for e in self.ENGS}
        self.known = {e: {} for e in self.ENGS}
        self.dma_sems = {}
        self.dma_n = {}
        for q in ("sp", "act", "pool"):
            self.dma_sems[q] = [stack.enter_context(nc.semaphore("dma_%s_%d" % (q, i))) for i in range(self.NDMA)]
            self.dma_n[q] = 0
        self.dma_uses = {}
        self.access = {}
        self.semobj = {}
        self.out_tokens = []
        self._uid = 0
        self.pstack = stack
        self.consts = {}

    def phase(self):
        prog = self

        class _Ph:
            def __enter__(self_):
                import time as _t
                self_.t0 = _t.time()
                prog.pstack = ExitStack()
                prog.pstack.__enter__()
                prog.consts = {}
                return prog

            def __exit__(self_, et, ev, tb):
                import time as _t
                t1 = _t.time()
                nops = sum(len(v) for v in prog.ops.values())
                if et is None:
                    prog.finish()
                if VERBOSE:
                    print("phase: %d ops, record %.1fs, emit %.1fs, max access list %d" % (
                        nops, t1 - self_.t0, _t.time() - t1, max([len(v) for v in prog.access.values()] + [0])), flush=True)
                prog.pstack.__exit__(et, ev, tb)
                prog.pstack = prog.stack
                prog.ops = {e: [] for e in prog.ENGS}
                prog.access = {}
                return False
        return _Ph()

    def sb(self, name, shape, dtype=F32):
        self._uid += 1
        return self.pstack.enter_context(self.nc.sbuf_tensor("%s_%d" % (name, self._uid), list(shape), dtype))

    def ps(self, name, shape=(128, 512), dtype=F32):
        self._uid += 1
        return self.pstack.enter_context(self.nc.psum_tensor("%s_%d" % (name, self._uid), list(shape), dtype))

    @staticmethod
    def _rng(ap):
        t = ap.tensor
        name = t.name
        pat = ap.ap
        off = ap.offset
        if not isinstance(off, int):
            return name, 0, 1 << 60
        space = str(t.space)
        if "DRAM" in space.upper() or "Dram" in space or "dram" in space:
            lo = hi = off
            for st, n in pat:
                if st >= 0:
                    hi += st * (n - 1)
                else:
                    lo += st * (n - 1)
            return name, lo, hi + 1
        pstep = pat[0][0]
        if pstep <= 0 or "PSUM" in space.upper():
            return name, 0, 1 << 60
        lo = hi = off % pstep
        for st, n in pat[1:]:
            if st >= 0:
                hi += st * (n - 1)
            else:
                lo += st * (n - 1)
        return name, lo, hi + 1

    def _wait_token(self, eng, tok, waits):
        sname, val = tok
        if eng == "pe" and sname == "Epe":
            return
        if self.known[eng].get(sname, 0) >= val:
            return
        self.known[eng][sname] = val
        waits.append((self.semobj[sname], val))

    def _deps(self, eng, reads, writes):
        waits = []
        for ap in reads:
            name, lo, hi = self._rng(ap)
            for ent in self.access.get(name, ()):
                if ent[2] and ent[0] < hi and lo < ent[1]:
                    self._wait_token(eng, ent[3], waits)
        for ap in writes:
            name, lo, hi = self._rng(ap)
            for ent in self.access.get(name, ()):
                if ent[0] < hi and lo < ent[1]:
                    self._wait_token(eng, ent[3], waits)
        return waits

    def _record(self, reads, writes, tok):
        for ap in reads:
            name, lo, hi = self._rng(ap)
            lst = self.access.setdefault(name, [])
            lst[:] = [e for e in lst if not (not e[2] and e[0] == lo and e[1] == hi and e[3][0] == tok[0])]
            lst.append([lo, hi, False, tok])
        for ap in writes:
            name, lo, hi = self._rng(ap)
            lst = self.access.setdefault(name, [])
            lst[:] = [e for e in lst if not (lo <= e[0] and e[1] <= hi)]
            lst.append([lo, hi, True, tok])

    def op(self, eng, fn, reads=(), writes=()):
        waits = self._deps(eng, reads, writes)
        self.cnt[eng] += 1
        sname = "E" + eng
        self.semobj[sname] = self.sem[eng]
        tok = (sname, self.cnt[eng])
        self._record(reads, writes, tok)
        self.ops[eng].append((waits, fn, (self.sem[eng], 1)))
        return tok

    def dma(self, q, out, in_, fn=None, is_output=False, slow=False):
        if slow and fn is None:
            fn = lambda e, out=out, in_=in_: e.dma_start(out=out, in_=in_, allow_slow_non_contiguous=True)
        slot = self.dma_n[q] % self.NDMA
        self.dma_n[q] += 1
        sem = self.dma_sems[q][slot]
        sname = "D%s%d" % (q, slot)
        self.semobj[sname] = sem
        uses = self.dma_uses.get(sname, 0)
        waits = self._deps(q, [in_], [out])
        if uses > 0:
            self._wait_token(q, (sname, 16 * uses), waits)
        self.dma_uses[sname] = uses + 1
        tok = (sname, 16 * (uses + 1))
        self._record([in_], [out], tok)
        if fn is None:
            fn = lambda e, out=out, in_=in_: e.dma_start(out=out, in_=in_)
        self.ops[q].append((waits, fn, (sem, 16)))
        if is_output:
            self.out_tokens.append(tok)
        return tok

    def allgather(self, out, in_, inc=1):
        if not hasattr(self, "cc_sem"):
            self.cc_sem = self.stack.enter_context(self.nc.semaphore("cc_sem"))
            self.cc_n = 0
            self.semobj["CC"] = self.cc_sem
        waits = self._deps("pool", [in_], [out])
        self.cc_n += inc
        tok = ("CC", self.cc_n)
        self._record([in_], [out], tok)
        fn = lambda e: e.collective_compute("AllGather", ALU.bypass, replica_groups=[list(range(NCORES))],
                                            ins=[in_], outs=[out])
        self.ops["pool"].append((waits, fn, (self.cc_sem, inc)))
        return tok

    def finish(self):
        final = []
        for e in self.ENGS:
            if self.cnt[e] > 0:
                final.append(("E" + e, self.cnt[e]))
        for sname, uses in self.dma_uses.items():
            final.append((sname, 16 * uses))
        if hasattr(self, "cc_sem") and self.cc_n > 0:
            final.append(("CC", self.cc_n))
        for e in self.ENGS:
            waits = []
            for tok in final:
                if tok[0] == "E" + e and e != "pe":
                    pass
                self._wait_token_force(e, tok, waits)
            self.ops[e].append((waits, None, None))
        nc = self.nc
        with nc.Block() as block:
            def run(eng_name):
                def body(e):
                    for waits, fn, inc in self.ops[eng_name]:
                        for s, v in waits:
                            e.wait_ge(s, v)
                        if fn is not None:
                            ins = fn(e)
                            ins.then_inc(inc[0], inc[1])
                return body
            block.sync(run("sp"))
            block.scalar(run("act"))
            block.tensor(run("pe"))
            block.vector(run("dve"))
            block.gpsimd(run("pool"))

    def _wait_token_force(self, eng, tok, waits):
        sname, val = tok
        if self.known[eng].get(sname, 0) >= val:
            return
        self.known[eng][sname] = val
        waits.append((self.semobj[sname], val))

    def mm(self, out, lhsT, rhs, start=True, stop=True):
        return self.op("pe", lambda e: e.matmul(out, lhsT, rhs, start=start, stop=stop),
                       reads=[lhsT, rhs] + ([] if start else [out]), writes=[out])

    def transpose(self, out, in_, ident):
        return self.op("pe", lambda e: e.transpose(out, in_, ident), reads=[in_, ident], writes=[out])

    def act(self, out, in_, func, bias=None, scale=None, accum_out=None):
        reads = [in_]
        kw = {}
        if bias is not None:
            kw["bias"] = bias
            if not isinstance(bias, (int, float)):
                reads.append(bias)
        if scale is not None:
            kw["scale"] = scale
            if not isinstance(scale, (int, float)):
                reads.append(scale)
        writes = [out]
        if accum_out is not None:
            kw["accum_out"] = accum_out
            writes.append(accum_out)
        return self.op("act", lambda e: e.activation(out, in_, func, **kw), reads=reads, writes=writes)

    def tt(self, eng, out, in0, in1, op):
        return self.op(eng, lambda e: e.tensor_tensor(out, in0, in1, op), reads=[in0, in1], writes=[out])

    def ts(self, eng, out, in0, s1, op0, s2=None, op1=None, accum_out=None):
        reads = [in0]
        if not isinstance(s1, (int, float)):
            reads.append(s1)
        if s2 is not None and not isinstance(s2, (int, float)):
            reads.append(s2)
        writes = [out]
        kw = {}
        if op1 is not None:
            kw["op1"] = op1
        if accum_out is not None:
            kw["accum_out"] = accum_out
            writes.append(accum_out)
        return self.op(eng, lambda e: e.tensor_scalar(out, in0, s1, s2, op0, **kw), reads=reads, writes=writes)

    def stt(self, eng, out, in0, scalar, in1, op0, op1, accum_out=None):
        reads = [in0, in1]
        if not isinstance(scalar, (int, float)):
            reads.append(scalar)
        writes = [out]
        kw = {}
        if accum_out is not None:
            kw["accum_out"] = accum_out
            writes.append(accum_out)
        return self.op(eng, lambda e: e.scalar_tensor_tensor(out, in0, scalar, in1, op0, op1, **kw),
                       reads=reads, writes=writes)

    def copy(self, eng, out, in_):
        if eng == "act":
            return self.op("act", lambda e: e.copy(out, in_), reads=[in_], writes=[out])
        return self.op(eng, lambda e: e.tensor_copy(out, in_), reads=[in_], writes=[out])

    def memset(self, eng, ap, val):
        return self.op(eng, lambda e: e.memset(ap, val), writes=[ap])

    def recip(self, out, in_):
        return self.op("dve", lambda e: e.reciprocal(out, in_), reads=[in_], writes=[out])


D = 2048
KD = D // 128
BATCH = 2
SEQ = 8192
CTX = 256
DEPTH = 4
TOK = 2112
TILES = [(0, 512, 0), (512, 512, 0), (1024, 512, 0), (1536, 512, 0), (2048, 64, 1)]
HY_W = 1024
IN_W = 3904
DN_ALPHA = (2.0 * DEPTH) ** 0.25
LN_EPS = 1e-5
RMS_EPS = 1e-6
MLA_SCALE = (128 + 64) ** -0.5


def new_nc():
    return bass.Bass("TRN2", target_bir_lowering=False)


def run_spmd(nc, in_maps):
    res = run_bass_kernel_spmd(nc, in_maps, core_ids=list(range(len(in_maps))))
    return res.results


def wl_layout(W):
    K, N = W.shape
    assert K % 128 == 0 and N % 128 == 0
    return np.ascontiguousarray(W.reshape(K // 128, 128, N // 128, 128).transpose(2, 1, 0, 3))


class Ctx:
    def __init__(self, P, ident_ap=None):
        self.P = P
        self.ones = P.sb("ones", [128, 128])
        P.memset("pool", self.ones[:], 1.0)
        self.wbufs = [P.sb("wbuf", [128, KD, 128]) for _ in range(3)]
        self.wi = 0
        self.psums = [P.ps("ps") for _ in range(4)]
        self.pi = 0
        self.qi = 0

    def wbuf(self):
        self.wi += 1
        return self.wbufs[self.wi % len(self.wbufs)]

    def psum(self):
        self.pi += 1
        return self.psums[self.pi % len(self.psums)]

    def q(self):
        self.qi += 1
        return ("sp", "act")[self.qi % 2]


def dense_chunk(P, C, w_ap, nk, x, TT, m=128):
    wb = C.wbuf()
    P.dma(C.q(), wb[:, 0:nk, 0:m], w_ap)
    ps = C.psum()
    for k in range(nk):
        P.mm(ps[0:m, 0:TT], wb[:, k, 0:m], x[:, k, 0:TT], start=(k == 0), stop=(k == nk - 1))
    return ps


def col_stats(P, C, x, nk, TT, scale, sq_tmp, center=False):
    ps = C.psum()
    for k in range(nk):
        P.act(sq_tmp[:, k, 0:TT], x[:, k, 0:TT], AF.Square)
    for k in range(nk):
        P.mm(ps[:, 0:TT], C.ones[:], sq_tmp[:, k, 0:TT], start=(k == 0), stop=(k == nk - 1))
    return ps


def rstd_from(P, out, ps, TT, scale, eps):
    P.act(out[:, 0:TT], ps[:, 0:TT], AF.Sqrt, bias=eps_ap(P, eps), scale=scale)
    P.recip(out[:, 0:TT], out[:, 0:TT])


def eps_ap(P, val):
    key = ("eps", val)
    if key not in P.consts:
        t = P.sb("eps", [128, 1])
        P.memset("pool", t[:], val)
        P.consts[key] = t
    return P.consts[key][:]


def build_mod():
    nc = new_nc()
    NCH = 48
    cT = nc.dram_tensor("cT", [128, KD, 3], F32, kind="ExternalInput").ap()
    w = nc.dram_tensor("w", [NCH, 128, KD, 128], F32, kind="ExternalInput").ap()
    b = nc.dram_tensor("b", [128, NCH], F32, kind="ExternalInput").ap()
    o = nc.dram_tensor("o", [128, NCH, 3], F32, kind="ExternalOutput").ap()
    with ExitStack() as st:
        P = Prog(nc, st)
        C = Ctx(P)
        ct = P.sb("ct", [128, KD, 3])
        sg = P.sb("sg", [128, KD, 3])
        bt = P.sb("bt", [128, NCH])
        ot = P.sb("ot", [128, NCH, 3])
        P.dma("sp", ct[:], cT)
        P.dma("act", bt[:], b)
        P.act(sg[:], ct[:], AF.Sigmoid)
        P.tt("dve", ct[:], ct[:], sg[:], ALU.mult)
        for cc in range(NCH):
            ps = dense_chunk(P, C, w[cc], KD, ct, 3)
            P.ts("dve", ot[:, cc, :], ps[:, 0:3], bt[:, cc:cc + 1], ALU.add)
        P.dma("sp", o, ot[:], is_output=True)
        P.finish()
    return nc


def run_mod(c, c_ctx, mod_w, mod_b):
    nc = build_mod()
    cT = np.stack([c[0], c[1], c_ctx], axis=-1).reshape(KD, 128, 3).transpose(1, 0, 2)
    cT = np.ascontiguousarray(cT)
    wall = mod_w.transpose(1, 0, 2).reshape(D, DEPTH * 6 * D)
    ball = mod_b.reshape(DEPTH * 6 * D)
    in_maps = []
    for core in range(NCORES):
        cols = slice(core * 6144, (core + 1) * 6144)
        in_maps.append({"cT": cT, "w": wl_layout(wall[:, cols]),
                        "b": np.ascontiguousarray(ball[cols].reshape(48, 128).T)})
    res = run_spmd(nc, in_maps)
    m = np.concatenate([r["o"].transpose(1, 0, 2).reshape(6144, 3) for r in res], axis=0)
    return np.ascontiguousarray(m.reshape(DEPTH, 6 * D, 3).transpose(0, 2, 1))


def mod_for_core(m_layer, core):
    b = core // 4
    mm_ = np.stack([m_layer[b], m_layer[2]], axis=-1)
    return np.ascontiguousarray(mm_.reshape(96, 128, 2).transpose(1, 0, 2))


class Stage:
    def __init__(self, P, n=3, shape=(128, 512)):
        self.bufs = [P.sb("stg", list(shape)) for _ in range(n)]
        self.i = 0
        self.e = 0

    def get(self):
        self.i += 1
        return self.bufs[self.i % len(self.bufs)]

    def eng(self):
        self.e += 1
        return ("dve", "act")[self.e % 2]


def load_mod(P, mod_ap):
    mt = P.sb("modt", [128, 96, 2])
    P.dma("sp", mt[:], mod_ap)
    return mt


def modulate(P, out, x, mt, shift_idx, scale_idx, sel, TT, tmp1):
    P.ts("dve", tmp1[:, 0:KD], mt[:, scale_idx * KD:(scale_idx + 1) * KD, sel], 1.0, ALU.add)
    for k in range(KD):
        eng = "dve" if k % 2 == 0 else "pool"
        P.ts(eng, out[:, k, 0:TT], x[:, k, 0:TT], tmp1[:, k:k + 1], ALU.mult,
             mt[:, shift_idx * KD + k, sel:sel + 1], ALU.add)


def rope_apply(P, C, S, out_dram, x_ps, m0, TT, cosT, sinT, rT, xr, t0):
    P.copy("dve", xr[0:64, 0:TT], x_ps)
    ps2 = C.psum()
    P.mm(ps2[0:64, 0:TT], rT[0:64, 0:64], xr[0:64, 0:TT])
    st = S.get()
    P.tt("dve", st[0:64, 0:TT], xr[0:64, 0:TT], cosT[0:64, t0:t0 + TT], ALU.mult)
    P.tt("dve", xr[0:64, 0:TT], ps2[0:64, 0:TT], sinT[0:64, t0:t0 + TT], ALU.mult)
    P.tt("pool", st[0:64, 0:TT], st[0:64, 0:TT], xr[0:64, 0:TT], ALU.add)
    P.dma(C.q(), out_dram, st[0:64, 0:TT], is_output=True)


def build_t1():
    nc = new_nc()
    dt = lambda name, shape, kind="ExternalInput": nc.dram_tensor(name, list(shape), F32, kind=kind).ap()
    hT = dt("hT", [D, TOK])
    mod = dt("mod", [128, 96, 2])
    w_in = dt("w_in", [31, 128, KD, 128])
    qg = dt("qg", [128, 4])
    kvg = dt("kvg", [128, 2])
    wqn = dt("wqn", [8, 128, 4, 128])
    wqr = dt("wqr", [8, 128, 4, 64])
    wkv = dt("wkv", [16, 128, 2, 128])
    cos_d = dt("cosT", [64, TOK])
    sin_d = dt("sinT", [64, TOK])
    rT_d = dt("rT", [64, 64])
    zhy = dt("zhy", [3072, TOK], "ExternalOutput")
    qT = dt("qT", [8, 192, TOK], "ExternalOutput")
    kvT = dt("kvT", [16, 128, TOK], "ExternalOutput")
    krT = dt("krT", [64, TOK], "ExternalOutput")
    with ExitStack() as stck:
        P = Prog(nc, stck)
        C = Ctx(P)
        S = Stage(P)
        mt = load_mod(P, mod)
        cosT = P.sb("cosT", [64, TOK]); sinT = P.sb("sinT", [64, TOK]); rT = P.sb("rT", [64, 64])
        qgt = P.sb("qgt", [128, 4]); kvgt = P.sb("kvgt", [128, 2])
        P.dma("sp", cosT[:], cos_d); P.dma("act", sinT[:], sin_d); P.dma("sp", rT[:], rT_d)
        P.dma("act", qgt[:], qg); P.dma("sp", kvgt[:], kvg)
        x = P.sb("x", [128, KD, 512])
        xin = P.sb("xin", [128, KD, 512])
        zl = P.sb("zl", [128, 6, 512])
        sq = P.sb("sq", [128, 6, 512])
        zn = P.sb("zn", [128, 6, 512])
        rs = P.sb("rs", [128, 512])
        xr = P.sb("xr", [64, 512])
        tmp1 = P.sb("tmp1", [128, KD])
        hT3 = hT.rearrange("(k p) t -> p k t", p=128)
        for (t0, TT, sel) in TILES:
            P.dma("sp", x[:, :, 0:TT], hT3[:, :, t0:t0 + TT])
            modulate(P, xin, x, mt, 0, 1, sel, TT, tmp1)
            for cc in range(31):
                m = 128 if cc < 30 else 64
                ps = dense_chunk(P, C, w_in[cc][:, :, 0:m], KD, xin, TT, m=m)
                if cc < 24:
                    st = S.get()
                    P.copy(S.eng(), st[:, 0:TT], ps[:, 0:TT])
                    P.dma(C.q(), zhy[cc * 128:(cc + 1) * 128, t0:t0 + TT], st[:, 0:TT], is_output=True)
                elif cc < 30:
                    P.copy(S.eng(), zl[:, cc - 24, 0:TT], ps[:, 0:TT])
                else:
                    rope_apply(P, C, S, krT[:, t0:t0 + TT], ps[0:64, 0:TT], 0, TT, cosT, sinT, rT, xr, t0)
            for (k0, nk, gt) in ((0, 4, qgt), (4, 2, kvgt)):
                pss = col_stats(P, C, zl[:, k0:k0 + nk, :], nk, TT, None, sq)
                rstd_from(P, rs, pss, TT, 1.0 / (nk * 128), RMS_EPS)
                for k in range(nk):
                    P.stt("dve", zn[:, k0 + k, 0:TT], zl[:, k0 + k, 0:TT], gt[:, k:k + 1], rs[:, 0:TT],
                          ALU.mult, ALU.mult)
            for h in range(8):
                ps = dense_chunk(P, C, wqn[h], 4, zn[:, 0:4, :], TT)
                st = S.get()
                P.copy(S.eng(), st[:, 0:TT], ps[:, 0:TT])
                P.dma(C.q(), qT[h, 0:128, t0:t0 + TT], st[:, 0:TT], is_output=True)
                ps = dense_chunk(P, C, wqr[h], 4, zn[:, 0:4, :], TT, m=64)
                rope_apply(P, C, S, qT[h, 128:192, t0:t0 + TT], ps[0:64, 0:TT], 0, TT, cosT, sinT, rT, xr, t0)
            for cc in range(16):
                ps = dense_chunk(P, C, wkv[cc], 2, zn[:, 4:6, :], TT)
                st = S.get()
                P.copy(S.eng(), st[:, 0:TT], ps[:, 0:TT])
                P.dma(C.q(), kvT[cc, :, t0:t0 + TT], st[:, 0:TT], is_output=True)
        P.finish()
    return nc


def rope_tables():
    rows = SEQ // 64
    row = np.repeat(np.arange(rows), 64).astype(np.float32)
    col = np.tile(np.arange(64), rows).astype(np.float32)
    half = 32
    inv = (10000.0 ** (-np.arange(0, half, 2, dtype=np.float32) / half)).astype(np.float32)
    ar, ac = row[:, None] * inv, col[:, None] * inv
    ang = np.concatenate([ar, ar, ac, ac], axis=-1)
    return np.cos(ang).astype(np.float32), np.sin(ang).astype(np.float32)


def rope_rT():
    R = np.zeros((64, 64), np.float32)
    for a in range(2):
        for f in range(16):
            R[a * 32 + f, a * 32 + 16 + f] = -1.0
            R[a * 32 + 16 + f, a * 32 + f] = 1.0
    return np.ascontiguousarray(R.T)


def core_tokens_T(h_lat, h_ctx, core):
    b, j = core // 4, core % 4
    return np.ascontiguousarray(np.concatenate([h_lat[b, j * 2048:(j + 1) * 2048], h_ctx[b, j * 64:(j + 1) * 64]], 0).T)


def t1_weight_maps(w_in, q_norm, w_uq, kv_norm, w_ukv):
    w_in_p = np.concatenate([w_in, np.zeros((D, 64), np.float32)], axis=1)
    uq = w_uq.reshape(512, 8, 192)
    wqn = np.stack([wl_layout(np.ascontiguousarray(uq[:, h, :128]))[0] for h in range(8)])
    wqr = np.stack([np.ascontiguousarray(uq[:, h, 128:].reshape(4, 128, 64).transpose(1, 0, 2)) for h in range(8)])
    return {"w_in": wl_layout(w_in_p), "qg": np.ascontiguousarray(q_norm.reshape(4, 128).T),
            "kvg": np.ascontiguousarray(kv_norm.reshape(2, 128).T), "wqn": wqn, "wqr": wqr,
            "wkv": wl_layout(w_ukv), "rT": rope_rT()}


def rope_core(cos, sin, core):
    j = core % 4
    c = np.concatenate([cos[j * 2048:(j + 1) * 2048], np.ones((64, 64), np.float32)], 0).T
    s = np.concatenate([sin[j * 2048:(j + 1) * 2048], np.zeros((64, 64), np.float32)], 0).T
    return np.ascontiguousarray(c), np.ascontiguousarray(s)


NTOK = CTX + SEQ
NKC = NTOK // 128


def build_attn():
    nc = new_nc()
    dt = lambda name, shape, kind="ExternalInput": nc.dram_tensor(name, list(shape), F32, kind=kind).ap()
    QT = dt("QT", [2, 192, NTOK])
    KT = dt("KT", [2, 192, NTOK])
    Vd = dt("V", [2, NTOK, 128])
    att = dt("att", [2, NTOK, 128], "ExternalOutput")
    with ExitStack() as stck:
        P = Prog(nc, stck)
        ktn = P.sb("ktn", [128, NTOK]); ktr = P.sb("ktr", [64, NTOK])
        va = P.sb("va", [128, NKC, 129])
        qn = [P.sb("qn", [128, 512]) for _ in range(2)]
        qr = [P.sb("qr", [64, 512]) for _ in range(2)]
        pts = [P.sb("pt", [128, 512]) for _ in range(3)]
        ps_s = [P.ps("ps_s") for _ in range(3)]
        acc = [P.ps("acc") for _ in range(4)]
        rc = P.sb("rc", [128, 4])
        ost = [P.sb("ost", [128, 4, 128]) for _ in range(2)]
        P.memset("pool", va[:, :, 128:129], 1.0)
        blocks = [(0, 256, 2)] + [(CTX + i * 512, 512, NKC) for i in range(SEQ // 512)]
        bi = 0
        ci = 0
        for h in range(2):
            P.dma("sp", ktn[:], KT[h, 0:128, :])
            P.dma("act", ktr[:], KT[h, 128:192, :])
            P.dma("sp", va[:, :, 0:128], Vd[h].rearrange("(c p) d -> p c d", p=128))
            for (q0, TQ, nkc) in blocks:
                bi += 1
                qnb, qrb, ob = qn[bi % 2], qr[bi % 2], ost[bi % 2]
                P.dma("sp", qnb[:, 0:TQ], QT[h, 0:128, q0:q0 + TQ])
                P.dma("act", qrb[:, 0:TQ], QT[h, 128:192, q0:q0 + TQ])
                nsub = TQ // 128

                def s_mm(kc):
                    ps = ps_s[(ci + kc) % 3]
                    P.mm(ps[:, 0:TQ], ktn[:, kc * 128:(kc + 1) * 128], qnb[:, 0:TQ], start=True, stop=False)
                    P.mm(ps[:, 0:TQ], ktr[:, kc * 128:(kc + 1) * 128], qrb[:, 0:TQ], start=False, stop=True)

                s_mm(0)
                for kc in range(nkc):
                    if kc + 1 < nkc:
                        s_mm(kc + 1)
                    ps = ps_s[(ci + kc) % 3]
                    pt = pts[(ci + kc) % 3]
                    P.act(pt[:, 0:TQ], ps[:, 0:TQ], AF.Exp, scale=MLA_SCALE)
                    for s in range(nsub):
                        P.mm(acc[s][:, 0:129], pt[:, s * 128:(s + 1) * 128], va[:, kc, :],
                             start=(kc == 0), stop=(kc == nkc - 1))
                ci += nkc
                for s in range(nsub):
                    P.recip(rc[:, s:s + 1], acc[s][:, 128:129])
                    P.ts("dve", ob[:, s, :], acc[s][:, 0:128], rc[:, s:s + 1], ALU.mult)
                P.dma("sp", att[h, q0:q0 + TQ, :].rearrange("(s p) d -> p s d", p=128), ob[:, 0:nsub, :],
                      is_output=True)
        P.finish()
    return nc


TT2 = 256
TILES2 = [(i * TT2, TT2, 0) for i in range(2048 // TT2)] + [(2048, 64, 1)]
GELU_C = 2.0 * math.sqrt(2.0 / math.pi)


def gelu_tanh(P, out, x, tmp, eng="dve"):
    P.tt(eng, tmp, x, x, ALU.mult)
    P.ts(eng, tmp, tmp, 0.044715, ALU.mult, 1.0, ALU.add)
    P.tt(eng, tmp, tmp, x, ALU.mult)
    P.act(tmp, tmp, AF.Sigmoid, scale=GELU_C)
    P.tt(eng, out, x, tmp, ALU.mult)


def layer_norm_T(P, C, u, TT, g_ap, b_ap, sq, rs):
    ps = C.psum()
    for k in range(KD):
        P.mm(ps[:, 0:TT], C.ones[:], u[:, k, 0:TT], start=(k == 0), stop=(k == KD - 1))
    P.act(rs[:, 0:TT], ps[:, 0:TT], AF.Copy, scale=-1.0 / D)
    for k in range(KD):
        eng = "dve" if k % 2 == 0 else "pool"
        P.tt(eng, u[:, k, 0:TT], u[:, k, 0:TT], rs[:, 0:TT], ALU.add)
    pss = col_stats(P, C, u, KD, TT, None, sq)
    rstd_from(P, rs, pss, TT, 1.0 / D, LN_EPS)
    for k in range(KD):
        P.tt("dve", u[:, k, 0:TT], u[:, k, 0:TT], rs[:, 0:TT], ALU.mult)
        P.act(u[:, k, 0:TT], u[:, k, 0:TT], AF.Identity, bias=b_ap[:, k:k + 1], scale=g_ap[:, k:k + 1])


class Peer:
    def __init__(self, P, C, keysT_d, ident_d, tab_dtype=F32):
        self.P, self.C = P, C
        self.tab_dtype = tab_dtype
        self.keysT = P.sb("keysT", [128, 16, 128])
        self.ident = P.sb("ident", [128, 128])
        P.dma("sp", self.keysT[:], keysT_d)
        P.dma("act", self.ident[:], ident_d)
        io_i = P.sb("io_i", [128, 256], I32)
        self.iota = P.sb("iota", [128, 256])
        P.op("pool", lambda e: e.iota(io_i[:], [[1, 256]], base=0, channel_multiplier=0), writes=[io_i[:]])
        P.copy("dve", self.iota[:], io_i[:])
        self.qT = P.sb("pq", [128, 16, 128])
        self.sc = P.sb("psc", [128, 16, 128])
        self.sc2 = P.sb("psc2", [128, 128])
        self.top = P.sb("ptop", [128, 16, 16])
        self.ti = P.sb("pti", [128, 16, 16], U32)
        self.tif = P.sb("ptif", [128, 16, 16])
        self.cand = P.sb("pcand", [128, 8, 256])
        self.cand2 = P.sb("pcand2", [128, 256])
        self.fidx = P.sb("pfidx", [128, 8, 256])
        self.c16 = P.sb("pc16", [128, 8, 16])
        self.cj = P.sb("pcj", [128, 8, 16], U32)
        self.cjf = P.sb("pcjf", [128, 8, 16])
        self.junk = P.sb("pjunk", [128, 256])
        self.isel = P.sb("pisel", [128, 128])
        self.isel32 = P.sb("pisel32", [128, 128], I32)
        self.gw = P.sb("pgw", [128, 8, 16])
        self.nmax = P.sb("pnmax", [128, 8])
        self.ssum = P.sb("pssum", [128, 8])
        self.apre = P.sb("papre", [128, 128])
        self.atmp = P.sb("patmp", [128, 128])
        self.wgt = P.sb("pwgt", [128, 128])
        self.xtok = P.sb("pxtok", [128, D])
        self.gbuf = [P.sb("pg", [128, D], tab_dtype) for _ in range(3 if tab_dtype == F32 else 6)]
        self.gi = 0
        self.accs = [P.sb("pacc", [128, D]) for _ in range(2)]
        self.fjunk = P.sb("pfj", [128, D]) if tab_dtype != F32 else None

    def gather(self, tab_d, col):
        P = self.P
        self.gi += 1
        buf = self.gbuf[self.gi % len(self.gbuf)]
        idx_ap = self.isel32[:, col:col + 1]
        P.dma("pool", buf[:], idx_ap,
              fn=lambda e, buf=buf, idx_ap=idx_ap: e.indirect_dma_start(
                  out=buf[:], out_offset=None, in_=tab_d,
                  in_offset=bass.IndirectOffsetOnAxis(ap=idx_ap, axis=0)))
        return buf

    def subtile(self, xin, s0, wq_d, u_d, v_d, fT_out):
        P, C = self.P, self.C
        xs = xin[:, :, s0:s0 + 128]
        for cc in range(16):
            ps = dense_chunk(P, C, wq_d[cc], KD, xs, 128)
            P.copy("act", self.qT[:, cc, :], ps[:, 0:128])
        for g4 in range(4):
            ps = C.psum()
            for j in range(4):
                k = g4 * 4 + j
                P.transpose(ps[:, j * 128:(j + 1) * 128], xs[:, k, :], self.ident[:])
            P.copy("act", self.xtok[:, g4 * 512:(g4 + 1) * 512], ps[:])
        for g4 in range(4):
            ps = C.psum()
            for j in range(4):
                hs = g4 * 4 + j
                P.mm(ps[:, j * 128:(j + 1) * 128], self.qT[:, hs, :], self.keysT[:, hs, :])
            P.copy("dve", self.sc[:, g4 * 4:(g4 + 1) * 4, :], ps[:].rearrange("p (a b) -> p a b", a=4))
        V = lambda fn, r, w: P.op("dve", fn, reads=r, writes=w)
        for hs in range(16):
            sc, sc2, top, ti = self.sc[:, hs, :], self.sc2[:], self.top[:, hs, :], self.ti[:, hs, :]
            V(lambda e, a=top[:, 0:8], b=sc: e.max(a, b), [sc], [top[:, 0:8]])
            V(lambda e, a=ti[:, 0:8], b=top[:, 0:8], c=sc: e.max_index(a, b, c), [top[:, 0:8], sc], [ti[:, 0:8]])
            V(lambda e, a=sc2, b=top[:, 0:8], c=sc: e.match_replace(a, b, c, -1e30), [top[:, 0:8], sc], [sc2])
            V(lambda e, a=top[:, 8:16], b=sc2: e.max(a, b), [sc2], [top[:, 8:16]])
            V(lambda e, a=ti[:, 8:16], b=top[:, 8:16], c=sc2: e.max_index(a, b, c), [top[:, 8:16], sc2], [ti[:, 8:16]])
        P.copy("dve", self.tif[:], self.ti[:])
        for h in range(8):
            c3 = self.cand[:, h, :].rearrange("p (a b) -> p a b", a=16)
            f3 = self.fidx[:, h, :].rearrange("p (a b) -> p a b", a=16)
            t0b = self.top[:, 2 * h, :].unsqueeze(2).to_broadcast([128, 16, 16])
            t1b = self.top[:, 2 * h + 1, :].unsqueeze(1).to_broadcast([128, 16, 16])
            P.tt("dve", c3, t0b, t1b, ALU.add)
            i0b = self.tif[:, 2 * h, :].unsqueeze(2).to_broadcast([128, 16, 16])
            i1b = self.tif[:, 2 * h + 1, :].unsqueeze(1).to_broadcast([128, 16, 16])
            P.stt("dve", f3, i0b, 128.0, i1b, ALU.mult, ALU.add)
            cd, cd2, c16, cj = self.cand[:, h, :], self.cand2[:], self.c16[:, h, :], self.cj[:, h, :]
            V(lambda e, a=c16[:, 0:8], b=cd: e.max(a, b), [cd], [c16[:, 0:8]])
            V(lambda e, a=cj[:, 0:8], b=c16[:, 0:8], c=cd: e.max_index(a, b, c), [c16[:, 0:8], cd], [cj[:, 0:8]])
            V(lambda e, a=cd2, b=c16[:, 0:8], c=cd: e.match_replace(a, b, c, -1e30), [c16[:, 0:8], cd], [cd2])
            V(lambda e, a=c16[:, 8:16], b=cd2: e.max(a, b), [cd2], [c16[:, 8:16]])
            V(lambda e, a=cj[:, 8:16], b=c16[:, 8:16], c=cd2: e.max_index(a, b, c), [c16[:, 8:16], cd2], [cj[:, 8:16]])
        P.copy("dve", self.cjf[:], self.cj[:])
        for h in range(8):
            for k in range(16):
                P.stt("dve", self.junk[:], self.iota[:], self.cjf[:, h, k:k + 1], self.fidx[:, h, :],
                      ALU.is_equal, ALU.mult, accum_out=self.isel[:, h * 16 + k:h * 16 + k + 1])
        P.ts("dve", self.isel[:], self.isel[:], 16383.0, ALU.min, 0.0, ALU.max)
        P.copy("dve", self.isel32[:], self.isel[:])
        P.ts("dve", self.nmax[:], self.c16[:, :, 0], -1.0, ALU.mult)
        for h in range(8):
            P.act(self.gw[:, h, :], self.c16[:, h, :], AF.Exp, bias=self.nmax[:, h:h + 1],
                  accum_out=self.ssum[:, h:h + 1])
        P.recip(self.ssum[:], self.ssum[:])
        P.tt("dve", self.gw[:], self.gw[:], self.ssum[:].unsqueeze(2).to_broadcast([128, 8, 16]), ALU.mult)
        if getattr(self, "stop_after_topk", False):
            return
        for hk in range(getattr(self, "n_gather", 128)):
            buf = self.gather(u_d, hk)
            dst = buf[:] if self.tab_dtype == F32 else self.fjunk[:]
            P.stt("dve", dst, buf[:], 1.0, self.xtok[:], ALU.mult, ALU.mult,
                  accum_out=self.apre[:, hk:hk + 1])
        gelu_tanh(P, self.wgt[:], self.apre[:], self.atmp[:])
        P.tt("dve", self.wgt[:], self.wgt[:], self.gw[:].rearrange("p a b -> p (a b)"), ALU.mult)
        for hk in range(getattr(self, "n_gather", 128)):
            buf = self.gather(v_d, hk)
            j = hk % 2
            eng = "dve"
            acc = self.accs[j]
            if hk < 2:
                P.ts(eng, acc[:], buf[:], self.wgt[:, hk:hk + 1], ALU.mult)
            else:
                P.stt(eng, acc[:], buf[:], self.wgt[:, hk:hk + 1], acc[:], ALU.mult, ALU.add)
        P.tt("pool", self.accs[0][:], self.accs[0][:], self.accs[1][:], ALU.add)
        for g4 in range(4):
            ps = C.psum()
            for j in range(4):
                k = g4 * 4 + j
                P.transpose(ps[:, j * 128:(j + 1) * 128], self.accs[0][:, k * 128:(k + 1) * 128], self.ident[:])
            for j in range(4):
                fT_out(g4 * 4 + j, ps[:, j * 128:(j + 1) * 128])


def tiles_for(nt_ctx, nt_lat, TT):
    out = []
    t = 0
    while t < nt_ctx:
        n = min(TT, nt_ctx - t)
        out.append((t, n, 1))
        t += n
    while t < nt_ctx + nt_lat:
        n = min(TT, nt_ctx + nt_lat - t)
        out.append((t, n, 0))
        t += n
    return out


def load_vec16(P, q, ap16):
    t = P.sb("v16", [128, KD])
    P.dma(q, t[:], ap16)
    return t


def post_phase(P, odd, h_d, mix_d, mod_d, modn_d, w_o_d, w_g_d, dskip_d, lng_d, lnb_d, lfg_d, lfb_d,
               wq_d, keysT_d, ident_d, u_d, v_d, s5in_d, nt_ctx, nt_lat, tab_dtype=F32):
    C = Ctx(P)
    S = Stage(P, n=2, shape=(128, TT2))
    mt = load_mod(P, mod_d)
    mtn = load_mod(P, modn_d) if modn_d is not None else None
    lng, lnb = load_vec16(P, "sp", lng_d), load_vec16(P, "act", lnb_d)
    lfg, lfb = load_vec16(P, "sp", lfg_d), load_vec16(P, "act", lfb_d)
    dsk = load_vec16(P, "sp", dskip_d) if odd else None
    pe = Peer(P, C, keysT_d, ident_d, tab_dtype)
    x = P.sb("x", [128, KD, TT2])
    mx = P.sb("mx", [128, KD, TT2])
    u = P.sb("u", [128, KD, TT2])
    sq = P.sb("sq", [128, KD, TT2])
    rs = P.sb("rs", [128, TT2])
    tmp1 = P.sb("tmp1", [128, KD])
    tch = P.sb("tch", [128, TT2])
    tch2 = P.sb("tch2", [128, TT2])
    h3 = h_d.rearrange("(k p) t -> p k t", p=128)
    m3 = mix_d.rearrange("(k p) t -> p k t", p=128)
    s3 = s5in_d.rearrange("(k p) t -> p k t", p=128) if s5in_d is not None else None
    for (t0, TT, sel) in tiles_for(nt_ctx, nt_lat, TT2):
        P.dma("sp", x[:, :, 0:TT], h3[:, :, t0:t0 + TT])
        P.dma("act", mx[:, :, 0:TT], m3[:, :, t0:t0 + TT])
        m2 = lambda k: mt[:, 2 * KD + k, sel:sel + 1]
        m5 = lambda k: mt[:, 5 * KD + k, sel:sel + 1]
        if odd:
            modulate(P, sq, x, mt, 0, 1, sel, TT, tmp1)
            for k in range(KD):
                P.stt("dve", mx[:, k, 0:TT], sq[:, k, 0:TT], dsk[:, k:k + 1], mx[:, k, 0:TT], ALU.mult, ALU.add)
            for k in range(KD):
                gelu_tanh(P, mx[:, k, 0:TT], mx[:, k, 0:TT], sq[:, k, 0:TT], eng=("dve", "pool")[k % 2])
        for cc in range(KD):
            ps = dense_chunk(P, C, w_o_d[cc], KD, mx, TT)
            if odd:
                ps2 = dense_chunk(P, C, w_g_d[cc], KD, mx, TT)
                P.act(tch2[:, 0:TT], ps2[:, 0:TT], AF.Sigmoid)
                P.stt("dve", tch[:, 0:TT], ps[:, 0:TT], m2(cc), tch2[:, 0:TT], ALU.mult, ALU.mult)
            else:
                P.act(tch[:, 0:TT], ps[:, 0:TT], AF.Copy, scale=m2(cc))
            P.stt("dve", u[:, cc, 0:TT], x[:, cc, 0:TT], DN_ALPHA, tch[:, 0:TT], ALU.mult, ALU.add)
        layer_norm_T(P, C, u, TT, lng, lnb, sq, rs)
        modulate(P, mx, u, mt, 3, 4, sel, TT, tmp1)
        for s0 in range(0, TT, 128):
            def f_out(k, ps, s0=s0):
                P.act(tch[:, 0:128], ps, AF.Copy, scale=m5(k))
                P.stt("dve", u[:, k, s0:s0 + 128], u[:, k, s0:s0 + 128], DN_ALPHA, tch[:, 0:128], ALU.mult, ALU.add)
            pe.subtile(mx, s0, wq_d, u_d, v_d, f_out)
        layer_norm_T(P, C, u, TT, lfg, lfb, sq, rs)
        P.dma("sp", h3[:, :, t0:t0 + TT], u[:, :, 0:TT], is_output=True)
        if s3 is not None:
            modulate(P, sq, u, mtn, 0, 1, sel, TT, tmp1)
            P.dma("act", s3[:, :, t0:t0 + TT], sq[:, :, 0:TT], is_output=True)


S5_T = 512
TWO_PI_LO = 6.28318


def s5_pack_params(lam_re, lam_im, log_dt, b_re, b_im, c_re, c_im):
    ldt = np.repeat(log_dt[:, :, None], 64, axis=2)
    cols = [lam_re[..., None], lam_im[..., None], ldt[..., None], b_re, b_im,
            c_re.transpose(0, 1, 3, 2), c_im.transpose(0, 1, 3, 2)]
    pk = np.concatenate(cols, axis=-1)
    return np.ascontiguousarray(pk.reshape(2, 64, 128, 67))


def s5_phase(P, u_d, y_d, prm_d, ident_d, nt_ctx, nt_lat, n_ct=16):
    NT = nt_ctx + nt_lat
    T = S5_T
    ident = P.sb("ident", [128, 128])
    P.dma("sp", ident[:], ident_d)
    io_i = P.sb("io_i", [128, T + 1], I32)
    iota = P.sb("iota", [128, T + 1])
    P.op("pool", lambda e: e.iota(io_i[:], [[1, T + 1]], base=0, channel_multiplier=0), writes=[io_i[:]])
    P.copy("dve", iota[:], io_i[:])
    ones = P.sb("ones", [128, T])
    P.memset("pool", ones[:], 1.0)
    u = P.sb("u", [128, NT])
    Y0 = P.sb("Y0", [128, NT])
    cosT = [[P.sb("cosT", [128, T + 1]) for d in range(2)] for q in range(4)]
    sinT = [[P.sb("sinT", [128, T + 1]) for d in range(2)] for q in range(4)]
    Bre = [[P.sb("Bre", [128, 128]) for d in range(2)] for q in range(4)]
    Bim = [[P.sb("Bim", [128, 128]) for d in range(2)] for q in range(4)]
    Cre = [[P.sb("Cre", [128, 128]) for d in range(2)] for q in range(4)]
    Cni = [[P.sb("Cni", [128, 128]) for d in range(2)] for q in range(4)]
    rv = [[P.sb("rv", [128, 1]) for d in range(2)] for q in range(4)]
    rb = [[P.sb("rb", [128, T]) for d in range(2)] for q in range(4)]
    init = [[P.sb("init", [128, 2]) for d in range(2)] for q in range(4)]
    prm = P.sb("prm", [128, 67])
    sc = P.sb("sc", [128, 16])
    bb = P.sb("bb", [128, 32])
    bp = P.sb("bp", [128, 2, 128])
    tw = P.sb("tw", [128, T + 1]); tw2 = P.sb("tw2", [128, T + 1]); twi = P.sb("twi", [128, T + 1], I32)
    ps_t = P.ps("ps_t")
    ps_b = [[P.ps("ps_br"), P.ps("ps_bi")] for _ in range(2)]
    ps_y = [P.ps("ps_y") for _ in range(2)]
    W = [dict((n, P.sb(n, [128, T])) for n in ("br", "bi", "t1", "t2", "t3", "t4", "zr", "zi", "sr", "si", "xr", "xi"))
         for _ in range(2)]
    wi = 0
    yi = 0

    def wrap_sin(out, fr):
        P.ts("dve", tw2[:], fr, 0.5, ALU.is_gt)
        P.tt("dve", fr, fr, tw2[:], ALU.subtract)
        P.ts("dve", tw2[:], fr, -0.5, ALU.is_lt)
        P.tt("dve", fr, fr, tw2[:], ALU.add)
        P.act(out, fr, AF.Sin, scale=TWO_PI_LO)

    for ct in range(n_ct):
        P.dma("sp", u[:], u_d[ct * 128:(ct + 1) * 128, :])
        for q in range(4):
            st = ct * 4 + q
            for d in range(2):
                P.dma("act", prm[:], prm_d[d, st])
                lr, li, ld = prm[:, 0:1], prm[:, 1:2], prm[:, 2:3]
                c = lambda i: sc[:, i:i + 1]
                P.act(c(0), ld, AF.Exp)
                P.act(rv[q][d][:], lr, AF.Exp, scale=c(0))
                P.ts("dve", c(1), li, c(0), ALU.mult, 1.0 / (2.0 * math.pi), ALU.mult)
                P.ts("dve", rb[q][d][:], ones[:], rv[q][d][:], ALU.mult)
                P.ts("dve", tw[:], iota[:], c(1), ALU.mult)
                P.copy("dve", twi[:], tw[:])
                P.copy("dve", tw2[:], twi[:])
                P.tt("dve", tw[:], tw[:], tw2[:], ALU.subtract)
                P.ts("dve", cosT[q][d][:], tw[:], 0.25, ALU.add)
                wrap_sin(sinT[q][d][:], tw[:])
                P.copy("dve", tw[:], cosT[q][d][:])
                wrap_sin(cosT[q][d][:], tw[:])
                cos1, sin1 = cosT[q][d][:, 1:2], sinT[q][d][:, 1:2]
                P.ts("dve", c(2), cos1, rv[q][d][:], ALU.mult, -1.0, ALU.add)
                P.ts("dve", c(3), sin1, rv[q][d][:], ALU.mult)
                P.ts("dve", c(4), lr, lr, ALU.mult)
                P.stt("dve", c(4), li, li, c(4), ALU.mult, ALU.add)
                P.recip(c(5), c(4))
                P.ts("dve", c(6), c(2), lr, ALU.mult)
                P.stt("dve", c(6), c(3), li, c(6), ALU.mult, ALU.add)
                P.ts("dve", c(7), c(6), c(5), ALU.mult)
                P.ts("dve", c(8), c(2), li, ALU.mult)
                P.stt("dve", c(8), c(3), lr, c(8), ALU.mult, ALU.subtract)
                P.ts("dve", c(9), c(8), c(5), ALU.mult)
                b_re, b_im, cr, ci = prm[:, 3:19], prm[:, 19:35], prm[:, 35:51], prm[:, 51:67]
                P.ts("dve", bb[:, 0:16], b_im, c(9), ALU.mult)
                P.stt("dve", bb[:, 0:16], b_re, c(7), bb[:, 0:16], ALU.mult, ALU.subtract)
                P.ts("dve", bb[:, 16:32], b_re, c(9), ALU.mult)
                P.stt("dve", bb[:, 16:32], b_im, c(7), bb[:, 16:32], ALU.mult, ALU.add)
                P.memset("pool", bp[:], 0.0)
                P.memset("pool", Cre[q][d][:], 0.0)
                P.memset("pool", Cni[q][d][:], 0.0)
                for gl in range(2):
                    pr = slice(gl * 64, gl * 64 + 64)
                    cs = slice(32 * q + 16 * gl, 32 * q + 16 * gl + 16)
                    P.copy("dve", bp[pr, 0, cs], bb[pr, 0:16])
                    P.copy("dve", bp[pr, 1, cs], bb[pr, 16:32])
                    P.copy("dve", Cre[q][d][pr, cs], cr[pr, :])
                    P.ts("dve", Cni[q][d][pr, cs], ci[pr, :], -1.0, ALU.mult)
                P.transpose(ps_t[:, 0:128], bp[:, 0, :], ident[:])
                P.transpose(ps_t[:, 128:256], bp[:, 1, :], ident[:])
                P.copy("act", Bre[q][d][:], ps_t[:, 0:128])
                P.copy("act", Bim[q][d][:], ps_t[:, 128:256])
                P.memset("pool", init[q][d][:], 0.0)
        for d in range(2):
            chunks = [(0, nt_ctx)] if nt_ctx else []
            lat = [(nt_ctx + i, min(T, nt_lat - i)) for i in range(0, nt_lat, T)]
            chunks = chunks + (lat if d == 0 else lat[::-1])
            for (a, Tc) in chunks:
                yi += 1
                psy = ps_y[yi % 2]
                for q in range(4):
                    wi += 1
                    Wq = W[wi % 2]
                    pbr, pbi = ps_b[wi % 2]
                    cs_, sn_ = cosT[q][d], sinT[q][d]
                    P.mm(pbr[:, 0:Tc], Bre[q][d][:], u[:, a:a + Tc])
                    P.mm(pbi[:, 0:Tc], Bim[q][d][:], u[:, a:a + Tc])
                    if d == 0:
                        P.copy("act", Wq["br"][:, 0:Tc], pbr[:, 0:Tc])
                        P.copy("act", Wq["bi"][:, 0:Tc], pbi[:, 0:Tc])
                    else:
                        P.copy("act", Wq["br"][:, 0:Tc], pbr[:, 0:Tc][:, ::-1])
                        P.copy("act", Wq["bi"][:, 0:Tc], pbi[:, 0:Tc][:, ::-1])
                    X = lambda n: Wq[n][:, 0:Tc]
                    P.tt("pool", X("t1"), cs_[:, 0:Tc], X("br"), ALU.mult)
                    P.tt("pool", X("t2"), sn_[:, 0:Tc], X("bi"), ALU.mult)
                    P.tt("pool", X("xr"), X("t1"), X("t2"), ALU.add)
                    P.tt("pool", X("t3"), cs_[:, 0:Tc], X("bi"), ALU.mult)
                    P.tt("pool", X("t4"), sn_[:, 0:Tc], X("br"), ALU.mult)
                    P.tt("dve", X("xi"), X("t3"), X("t4"), ALU.subtract)
                    i0 = init[q][d]
                    P.op("dve", lambda e, o=X("zr"), a0=rb[q][d][:, 0:Tc], a1=X("xr"), ini=i0[:, 0:1]:
                         e.tensor_tensor_scan(o, a0, a1, ini, ALU.mult, ALU.add),
                         reads=[rb[q][d][:, 0:Tc], X("xr"), i0[:, 0:1]], writes=[X("zr")])
                    P.op("dve", lambda e, o=X("zi"), a0=rb[q][d][:, 0:Tc], a1=X("xi"), ini=i0[:, 1:2]:
                         e.tensor_tensor_scan(o, a0, a1, ini, ALU.mult, ALU.add),
                         reads=[rb[q][d][:, 0:Tc], X("xi"), i0[:, 1:2]], writes=[X("zi")])
                    zr_l, zi_l = Wq["zr"][:, Tc - 1:Tc], Wq["zi"][:, Tc - 1:Tc]
                    ecr, eci = cs_[:, Tc:Tc + 1], sn_[:, Tc:Tc + 1]
                    P.ts("dve", sc[:, 10:11], zi_l, eci, ALU.mult)
                    P.stt("dve", i0[:, 0:1], zr_l, ecr, sc[:, 10:11], ALU.mult, ALU.subtract)
                    P.ts("dve", sc[:, 11:12], zi_l, ecr, ALU.mult)
                    P.stt("dve", i0[:, 1:2], zr_l, eci, sc[:, 11:12], ALU.mult, ALU.add)
                    P.tt("pool", X("t1"), cs_[:, 0:Tc], X("zr"), ALU.mult)
                    P.tt("pool", X("t2"), sn_[:, 0:Tc], X("zi"), ALU.mult)
                    P.tt("dve", X("t3"), sn_[:, 0:Tc], X("zr"), ALU.mult)
                    P.tt("dve", X("t4"), cs_[:, 0:Tc], X("zi"), ALU.mult)
                    so_r = X("sr") if d == 0 else X("sr")[:, ::-1]
                    so_i = X("si") if d == 0 else X("si")[:, ::-1]
                    P.tt("pool", so_r, X("t1"), X("t2"), ALU.subtract)
                    P.tt("dve", so_i, X("t3"), X("t4"), ALU.add)
                    P.mm(psy[:, 0:Tc], Cre[q][d][:], X("sr"), start=(q == 0), stop=False)
                    P.mm(psy[:, 0:Tc], Cni[q][d][:], X("si"), start=False, stop=(q == 3))
                if d == 0:
                    P.copy("act", Y0[:, a:a + Tc], psy[:, 0:Tc])
                else:
                    P.tt("dve", Y0[:, a:a + Tc], Y0[:, a:a + Tc], psy[:, 0:Tc], ALU.add)
        P.dma("sp", y_d[ct * 128:(ct + 1) * 128, :], Y0[:], is_output=True)


HY_G = 16
FFT_N = 2 * SEQ


def hyena_consts():
    a = np.arange(128)
    ang = 2.0 * np.pi * np.outer(a, a) / 128.0
    Cm, Sm = np.cos(ang), np.sin(ang)
    tw = 2.0 * np.pi * np.outer(a, a) / FFT_N
    L = SEQ
    n = np.arange(FFT_N)
    pos = np.where(n < L, n, np.where(n == L, 0, FFT_N - n))

    def feats(Lx, p):
        t = np.linspace(0.0, 1.0, Lx, dtype=np.float32)[:, None]
        w = ((2.0 * math.pi / Lx) * np.arange(Lx, dtype=np.float32))[:, None]
        f = np.linspace(1e-4, 15, 16, dtype=np.float32)[None, :]
        ft = np.concatenate([t, np.cos(f * w), -np.sin(f * w)], axis=-1).astype(np.float32)
        return ft[p], t[p, 0]
    f_lat, t_lat = feats(L, pos)
    perm = (np.arange(128)[None, :] * 128 + np.arange(128)[:, None]).reshape(-1)
    f_ctx, t_ctx = feats(CTX, np.arange(CTX))
    deltas = np.abs(np.linspace(math.log(1e-2) / 1.5, math.log(1e-2) / 0.3, 2 * HY_W, dtype=np.float32))
    c = lambda x: np.ascontiguousarray(x, dtype=np.float32)
    return {
        "hc_FA": c(np.concatenate([Cm, -Sm], 1)), "hc_CS": c(np.concatenate([Cm, Sm], 1)),
        "hc_NSC": c(np.concatenate([-Sm, Cm], 1)), "hc_Cm": c(Cm), "hc_Sm": c(Sm), "hc_NSm": c(-Sm),
        "hc_TWr": c(np.cos(tw)), "hc_TWi": c(-np.sin(tw)),
        "hc_featL": c(f_lat[perm].T), "hc_featC": c(f_ctx.T),
        "hc_tposL": c(t_lat.reshape(128, 128)), "hc_tposC": c(np.tile(t_ctx[None, :], (128, 1))),
        "hc_ndelL": c(np.tile(-deltas[None, :], (128, 1))), "hc_ndelC": c((-deltas).reshape(16, 128).T),
    }


HC_SHAPES = {"hc_FA": [128, 256], "hc_CS": [128, 256], "hc_NSC": [128, 256], "hc_Cm": [128, 128],
             "hc_Sm": [128, 128], "hc_NSm": [128, 128], "hc_TWr": [128, 128], "hc_TWi": [128, 128],
             "hc_featL": [33, FFT_N], "hc_featC": [33, CTX], "hc_tposL": [128, 128], "hc_tposC": [128, CTX],
             "hc_ndelL": [128, 2 * HY_W], "hc_ndelC": [128, 16]}


def hyena_mlp(P, feat_d, npos, hdn_d, w1_d, w23_d, b_d, fq_d):
    w1 = P.sb("mw1", [33, 64]); w23 = P.sb("mw23", [64, 2, 64]); b = P.sb("mb", [64, 3]); fq = P.sb("mfq", [64, 1])
    P.dma("sp", w1[:], w1_d); P.dma("act", w23[:], w23_d.rearrange("l k m -> k l m"))
    P.dma("sp", b[:], b_d); P.dma("act", fq[:], fq_d)
    fq2 = P.sb("mfq2", [64, 1])
    P.ts("dve", fq2[:], fq[:], 1.0 / (2.0 * math.pi), ALU.mult)
    ft = [P.sb("mft", [33, 512]) for _ in range(2)]
    hb = [P.sb("mhb", [64, 512]) for _ in range(3)]
    tw = P.sb("mtw", [64, 512]); tm = P.sb("mtm", [64, 512])
    pss = [P.ps("mps") for _ in range(2)]
    for i, p0 in enumerate(range(0, npos, 512)):
        n = min(512, npos - p0)
        f = ft[i % 2]
        P.dma(("sp", "act")[i % 2], f[:, 0:n], feat_d[:, p0:p0 + n])
        cur = f[0:33, 0:n]
        for layer in range(3):
            ps = pss[(i * 3 + layer) % 2]
            lhsT = w1[:] if layer == 0 else w23[:, layer - 1, :]
            P.mm(ps[0:64, 0:n], lhsT, cur)
            P.ts("dve", tw[:, 0:n], ps[0:64, 0:n], b[:, layer:layer + 1], ALU.add, fq2[:], ALU.mult)
            for _ in range(2):
                P.ts("dve", tm[:, 0:n], tw[:, 0:n], 0.5, ALU.is_gt)
                P.tt("dve", tw[:, 0:n], tw[:, 0:n], tm[:, 0:n], ALU.subtract)
                P.ts("dve", tm[:, 0:n], tw[:, 0:n], -0.5, ALU.is_lt)
                P.tt("dve", tw[:, 0:n], tw[:, 0:n], tm[:, 0:n], ALU.add)
            h = hb[layer]
            P.act(h[:, 0:n], tw[:, 0:n], AF.Sin, scale=TWO_PI_LO)
            cur = h[:, 0:n]
        P.dma("sp", hdn_d[:, p0:p0 + n], cur, is_output=True)


class HyFFT:
    def __init__(self, P, cd):
        self.P = P
        ld = lambda name: self._ld(cd[name], HC_SHAPES[name])
        self.FA, self.CS, self.NSC = ld("hc_FA"), ld("hc_CS"), ld("hc_NSC")
        self.Cm, self.Sm, self.NSm = ld("hc_Cm"), ld("hc_Sm"), ld("hc_NSm")
        self.TWr, self.TWi = ld("hc_TWr"), ld("hc_TWi")
        self.psA = [P.ps("fpA") for _ in range(2)]
        self.psB = [P.ps("fpB") for _ in range(4)]
        self.ia = 0
        self.ib = 0
        self.t = [P.sb("ft", [128, 512]) for _ in range(4)]
        G = HY_G
        self.F = [P.sb("fF", [128, G, 128]) for _ in range(4)]
        self.K = [P.sb("fK", [128, G, 128]) for _ in range(2)]

    def _ld(self, d, shape):
        t = self.P.sb("hc", shape)
        self.P.dma("sp", t[:], d)
        return t

    def cmul_from_psum(self, ps_re, ps_im, br, bi, out_re, out_im, conj=False):
        P = self.P
        t1, t2, t3, t4 = [x[:, 0:1] for x in self.t]
        shp = ps_re.shape
        n = 1
        for s in shp[1:]:
            n *= s
        v = lambda x: x[:, 0:n].rearrange("p (a b) -> p a b", a=shp[1]) if len(shp) == 3 else x[:, 0:n]
        t1, t2, t3, t4 = [v(x) for x in self.t]
        P.tt("dve", t1, ps_re, br, ALU.mult)
        P.tt("dve", t2, ps_im, bi, ALU.mult)
        P.tt("pool", out_re, t1, t2, ALU.add if conj else ALU.subtract)
        P.tt("dve", t3, ps_re, bi, ALU.mult)
        P.tt("dve", t4, ps_im, br, ALU.mult)
        P.tt("pool", out_im, t4, t3, ALU.subtract if conj else ALU.add)

    def fwd(self, x, npart, dst_re, dst_im, kmul=None):
        P = self.P
        G = HY_G
        Ar, Ai = self.F[0], self.F[1]
        for c0 in range(0, G, 2):
            self.ia += 1
            ps = self.psA[self.ia % 2]
            for j in range(2):
                P.mm(ps[:, j * 256:(j + 1) * 256], x[0:npart, c0 + j, :], self.FA[0:npart, :])
            p4 = ps[:].rearrange("p (c r k) -> p c r k", c=2, r=2)
            twr = self.TWr[:].unsqueeze(1).to_broadcast([128, 2, 128])
            twi = self.TWi[:].unsqueeze(1).to_broadcast([128, 2, 128])
            self.cmul_from_psum(p4[:, :, 0, :], p4[:, :, 1, :], twr, twi, Ar[:, c0:c0 + 2, :], Ai[:, c0:c0 + 2, :])
        for c0 in range(0, G, 4):
            self.ib += 1
            pr, pi = self.psB[(2 * self.ib) % 4], self.psB[(2 * self.ib + 1) % 4]
            ar = Ar[:, c0:c0 + 4, :].rearrange("p c k -> p (c k)")
            ai = Ai[:, c0:c0 + 4, :].rearrange("p c k -> p (c k)")
            P.mm(pr[:], self.Cm[:], ar, start=True, stop=False)
            P.mm(pr[:], self.Sm[:], ai, start=False, stop=True)
            P.mm(pi[:], self.Cm[:], ai, start=True, stop=False)
            P.mm(pi[:], self.NSm[:], ar, start=False, stop=True)
            pr3 = pr[:].rearrange("p (c k) -> p c k", c=4)
            pi3 = pi[:].rearrange("p (c k) -> p c k", c=4)
            if kmul is None:
                P.copy("act", dst_re[:, c0:c0 + 4, :], pr3)
                P.copy("act", dst_im[:, c0:c0 + 4, :], pi3)
            else:
                self.cmul_from_psum(pr3, pi3, kmul[0][:, c0:c0 + 4, :], kmul[1][:, c0:c0 + 4, :],
                                    dst_re[:, c0:c0 + 4, :], dst_im[:, c0:c0 + 4, :])

    def inv(self, Yr, Yi, consume):
        P = self.P
        G = HY_G
        Br, Bi = self.F[0], self.F[1]
        for c0 in range(0, G, 2):
            self.ia += 1
            ps = self.psA[self.ia % 2]
            for j in range(2):
                o = ps[:, j * 256:(j + 1) * 256]
                P.mm(o, Yr[:, c0 + j, :], self.CS[:], start=True, stop=False)
                P.mm(o, Yi[:, c0 + j, :], self.NSC[:], start=False, stop=True)
            p4 = ps[:].rearrange("p (c r k) -> p c r k", c=2, r=2)
            twr = self.TWr[:].unsqueeze(1).to_broadcast([128, 2, 128])
            twi = self.TWi[:].unsqueeze(1).to_broadcast([128, 2, 128])
            self.cmul_from_psum(p4[:, :, 0, :], p4[:, :, 1, :], twr, twi, Br[:, c0:c0 + 2, :], Bi[:, c0:c0 + 2, :],
                                conj=True)
        for c0 in range(0, G, 4):
            self.ib += 1
            pr = self.psB[self.ib % 4]
            br = Br[:, c0:c0 + 4, :].rearrange("p c k -> p (c k)")
            bi = Bi[:, c0:c0 + 4, :].rearrange("p c k -> p (c k)")
            P.mm(pr[0:64, :], self.Cm[:, 0:64], br, start=True, stop=False)
            P.mm(pr[0:64, :], self.NSm[:, 0:64], bi, start=False, stop=True)
            consume(c0, pr[0:64, :].rearrange("p (c k) -> p c k", c=4))


def hyena_lat(P, zhy_d, mix_d, hdn_d, w4_d, convw_d, convb_d, hbias_d, cd, t_off, n_groups=HY_W // HY_G):
    G = HY_G
    ff = HyFFT(P, cd)
    ones = P.sb("hones", [128, 128]); P.memset("pool", ones[:], 1.0)
    tpos = P.sb("htpos", [128, 128]); P.dma("sp", tpos[:], cd["hc_tposL"])
    ndel = P.sb("hndel", [128, 2 * HY_W]); P.dma("act", ndel[:], cd["hc_ndelL"])
    w4 = P.sb("hw4", [64, 4096]); P.dma("sp", w4[:], w4_d)
    w4v = w4[:].rearrange("p (d o c) -> p d o c", d=2, o=2)
    T = [P.sb("hT", [128, G, 128]) for _ in range(3)]
    sh = [P.sb("hsh", [64, G, 128]) for _ in range(2)]
    hd = [P.sb("hhd", [64, 2048]) for _ in range(2)]
    cw = P.sb("hcw", [64, 3, 3, G]); cb = P.sb("hcb", [64, 3, G]); hbv = P.sb("hhb", [64, 2, G])
    red = P.sb("hred", [128, G]); rinv = P.sb("hrinv", [128, G])
    psK = [P.ps("hpsK") for _ in range(2)]
    dec = P.sb("hdec", [128, 16, G])
    L = SEQ
    for g in range(n_groups):
        c0 = g * G
        for j in range(3):
            for s in range(3):
                P.dma(("sp", "act")[(j + s) % 2], cw[:, j, s, :],
                      convw_d[s:s + 1, j * HY_W + c0:j * HY_W + c0 + G].to_broadcast([64, G]))
            P.dma("sp", cb[:, j, :], convb_d[j * HY_W + c0:j * HY_W + c0 + G].unsqueeze(0).to_broadcast([64, G]))
        for o in range(2):
            P.dma("act", hbv[:, o, :], hbias_d[o:o + 1, c0:c0 + G].to_broadcast([64, G]))
        for j in range(3):
            row0 = j * HY_W + c0
            src = lambda s: zhy_d[row0:row0 + G, t_off + s:t_off + s + L].rearrange("c (a b) -> a c b", b=128)
            dst = T[j]
            P.dma("sp", dst[0:64, :, :], src(0))
            P.dma("act", sh[0][:, :, 1:128], zhy_d[row0:row0 + G, t_off:t_off + L].rearrange("c (a b) -> a c b", b=128)[:, :, 0:127])
            P.dma("act", sh[0][1:64, :, 0:1], zhy_d[row0:row0 + G, t_off:t_off + L].rearrange("c (a b) -> a c b", b=128)[0:63, :, 127:128], slow=True)
            P.memset("pool", sh[0][0:1, :, 0:1], 0.0)
            P.dma("sp", sh[1][:, :, 0:127], zhy_d[row0:row0 + G, t_off:t_off + L].rearrange("c (a b) -> a c b", b=128)[:, :, 1:128])
            P.memset("pool", sh[1][0:64, :, 127:128], 0.0)
            P.dma("sp", sh[1][0:63, :, 127:128], zhy_d[row0:row0 + G, t_off:t_off + L].rearrange("c (a b) -> a c b", b=128)[1:64, :, 0:1], slow=True)
            bc = lambda a: a.unsqueeze(2).to_broadcast([64, G, 128])
            P.tt("dve", dst[0:64], dst[0:64], bc(cw[:, j, 1, :]), ALU.mult)
            P.tt("pool", sh[0][:], sh[0][:], bc(cw[:, j, 0, :]), ALU.mult)
            P.tt("dve", dst[0:64], dst[0:64], sh[0][:], ALU.add)
            P.tt("pool", sh[1][:], sh[1][:], bc(cw[:, j, 2, :]), ALU.mult)
            P.tt("dve", dst[0:64], dst[0:64], sh[1][:], ALU.add)
            P.tt("dve", dst[0:64], dst[0:64], bc(cb[:, j, :]), ALU.add)
        for o in range(2):
            kk = ff.F[2]
            for n2b in range(0, 128, 16):
                hb_ = hd[(n2b // 16) % 2]
                P.dma(("sp", "act")[(n2b // 16) % 2], hb_[:], hdn_d[:, n2b * 128:(n2b + 16) * 128])
                ps, psb = psK[0], psK[1]
                for j in range(16):
                    P.mm(ps[:, j * G:(j + 1) * G], hb_[:, j * 128:(j + 1) * 128], w4v[:, 0, o, c0:c0 + G])
                    P.mm(psb[:, j * G:(j + 1) * G], hb_[:, j * 128:(j + 1) * 128], w4v[:, 1, o, c0:c0 + G])
                P.tt("dve", dec[:], tpos[:, n2b:n2b + 16].unsqueeze(2).to_broadcast([128, 16, G]),
                     ndel[:, o * HY_W + c0:o * HY_W + c0 + G].unsqueeze(1).to_broadcast([128, 16, G]), ALU.mult)
                P.act(dec[:], dec[:], AF.Exp)
                P.tt("dve", kk[0:64, :, n2b:n2b + 16].rearrange("p c n -> p n c"),
                     ps[0:64, 0:16 * G].rearrange("p (n c) -> p n c", n=16), dec[0:64], ALU.mult)
                P.tt("dve", kk[64:128, :, n2b:n2b + 16].rearrange("p c n -> p n c"),
                     psb[64:128, 0:16 * G].rearrange("p (n c) -> p n c", n=16), dec[64:128], ALU.mult)
            P.op("dve", lambda e, o_=red[:], i_=kk[:]: e.tensor_reduce(o_, i_, AX.X, ALU.add, apply_absolute_value=True),
                 reads=[kk[:]], writes=[red[:]])
            psn = psK[0]
            P.mm(psn[:, 0:G], ones[:], red[:])
            P.recip(rinv[:], psn[:, 0:G])
            P.tt("dve", kk[:], kk[:], rinv[:].unsqueeze(2).to_broadcast([128, G, 128]), ALU.mult)
            P.memset("pool", kk[64:65, :, 0:1], 0.0)
            ff.fwd(kk, 128, ff.K[0], ff.K[1])
            y = T[2]
            gate = T[o]
            ff.fwd(y, 64, ff.F[2], ff.F[3], kmul=(ff.K[0], ff.K[1]))

            def consume(cc, ps3, o=o, y=y, gate=gate):
                t = ff.t[0][0:64, :].rearrange("p (c k) -> p c k", c=4)
                P.tt("pool", t, y[0:64, cc:cc + 4, :], hbv[:, o, cc:cc + 4].unsqueeze(2).to_broadcast([64, 4, 128]), ALU.mult)
                P.stt("dve", t, ps3, 1.0 / FFT_N, t, ALU.mult, ALU.add)
                P.tt("dve", y[0:64, cc:cc + 4, :], t, gate[0:64, cc:cc + 4, :], ALU.mult)
            ff.inv(ff.F[2], ff.F[3], consume)
        P.dma("sp", mix_d[c0:c0 + G, t_off:t_off + L].rearrange("c (a b) -> a c b", b=128), T[2][0:64, :, :], is_output=True)


def hyena_ctx(P, zhy_d, mix_d, hdn_d, w4_d, convw_d, convb_d, hbias_d, cd, n_ct=8):
    Lc = CTX
    w4 = P.sb("cw4", [64, 4096]); P.dma("sp", w4[:], w4_d)
    w4v = w4[:].rearrange("p (d o c) -> p d o c", d=2, o=2)
    hdn = P.sb("chdn", [64, Lc]); P.dma("act", hdn[:], hdn_d)
    tpos = P.sb("ctpos", [128, Lc]); P.dma("sp", tpos[:], cd["hc_tposC"])
    ndel = P.sb("cndel", [128, 16]); P.dma("act", ndel[:], cd["hc_ndelC"])
    cw = P.sb("ccw", [128, 24, 3]); cb = P.sb("ccb", [128, 24]); hbv = P.sb("chb", [128, 2, 8])
    P.dma("sp", cw[:], convw_d); P.dma("act", cb[:], convb_d); P.dma("sp", hbv[:], hbias_d)
    zp = P.sb("czp", [128, Lc + 2])
    X = [P.sb("cX", [128, Lc]) for _ in range(3)]
    hf = [P.sb("chf", [128, Lc]) for _ in range(2)]
    dec = P.sb("cdec", [128, Lc])
    acc = [P.sb("cacc", [128, Lc]) for _ in range(2)]
    rd = P.sb("crd", [128, 4])
    ps = [P.ps("cps") for _ in range(2)]
    P.memset("pool", zp[:], 0.0)
    for ct in range(n_ct):
        for j in range(3):
            k = j * 8 + ct
            P.dma("sp", zp[:, 1:Lc + 1], zhy_d[j * HY_W + ct * 128:j * HY_W + (ct + 1) * 128, 0:Lc])
            P.ts("dve", X[j][:], zp[:, 1:Lc + 1], cw[:, k, 1:2], ALU.mult, cb[:, k:k + 1], ALU.add)
            P.stt("dve", X[j][:], zp[:, 0:Lc], cw[:, k, 0:1], X[j][:], ALU.mult, ALU.add)
            P.stt("dve", X[j][:], zp[:, 2:Lc + 2], cw[:, k, 2:3], X[j][:], ALU.mult, ALU.add)
        y = X[2]
        for o in range(2):
            P.act(dec[:], tpos[:], AF.Exp, scale=ndel[:, o * 8 + ct:o * 8 + ct + 1])
            for d in range(2):
                P.mm(ps[d][:, 0:Lc], w4v[:, d, o, ct * 128:(ct + 1) * 128], hdn[:])
                P.tt("dve", hf[d][:], ps[d][:, 0:Lc], dec[:], ALU.mult)
                P.op("dve", lambda e, o_=rd[:, d:d + 1], i_=hf[d][:]: e.tensor_reduce(o_, i_, AX.X, ALU.add, apply_absolute_value=True),
                     reads=[hf[d][:]], writes=[rd[:, d:d + 1]])
            P.tt("dve", rd[:, 2:3], rd[:, 0:1], rd[:, 1:2], ALU.add)
            P.recip(rd[:, 3:4], rd[:, 2:3])
            for d in range(2):
                P.ts("dve", hf[d][:], hf[d][:], rd[:, 3:4], ALU.mult)
            P.ts("dve", acc[0][:], y[:], hf[0][:, 0:1], ALU.mult)
            P.memset("pool", acc[1][:], 0.0)
            for tau in range(1, Lc):
                n = Lc - tau
                P.stt("dve", acc[0][:, tau:Lc], y[:, 0:n], hf[0][:, tau:tau + 1], acc[0][:, tau:Lc], ALU.mult, ALU.add)
                P.stt("dve", acc[1][:, 0:n], y[:, tau:Lc], hf[1][:, tau:tau + 1], acc[1][:, 0:n], ALU.mult, ALU.add)
            P.tt("pool", acc[0][:], acc[0][:], acc[1][:], ALU.add)
            P.stt("dve", acc[0][:], y[:], hbv[:, o, ct:ct + 1], acc[0][:], ALU.mult, ALU.add)
            P.tt("dve", y[:], acc[0][:], X[o][:], ALU.mult)
        P.dma("sp", mix_d[ct * 128:(ct + 1) * 128, 0:Lc], y[:], is_output=True)


def mod_phase(P, cT_d, modw_d, modb_d, modv_d):
    C = Ctx(P)
    ct = P.sb("ct", [128, KD, 2]); sg = P.sb("sg", [128, KD, 2]); bt = P.sb("bt", [128, 384])
    ot = P.sb("ot", [128, 96, 2])
    P.dma("sp", ct[:], cT_d); P.dma("act", bt[:], modb_d)
    P.act(sg[:], ct[:], AF.Sigmoid)
    P.tt("dve", ct[:], ct[:], sg[:], ALU.mult)
    for l in range(DEPTH):
        for cc in range(96):
            ps = dense_chunk(P, C, modw_d[l][cc], KD, ct, 2)
            P.ts("dve", ot[:, cc, :], ps[:, 0:2], bt[:, l * 96 + cc:l * 96 + cc + 1], ALU.add)
        P.dma("sp", modv_d[l], ot[:], is_output=True)


def rope_apply2(P, C, S, out_dram, x_ps, TT, cos_t, sin_t, rT, xr):
    P.copy("dve", xr[0:64, 0:TT], x_ps)
    ps2 = C.psum()
    P.mm(ps2[0:64, 0:TT], rT[0:64, 0:64], xr[0:64, 0:TT])
    st = S.get()
    P.tt("dve", st[0:64, 0:TT], xr[0:64, 0:TT], cos_t[0:64, 0:TT], ALU.mult)
    P.tt("dve", xr[0:64, 0:TT], ps2[0:64, 0:TT], sin_t[0:64, 0:TT], ALU.mult)
    P.tt("pool", st[0:64, 0:TT], st[0:64, 0:TT], xr[0:64, 0:TT], ALU.add)
    P.dma(C.q(), out_dram, st[0:64, 0:TT], is_output=True)


def t1_phase(P, h_d, mod_d, w_in_d, qg_d, kvg_d, wqn_d, wqr_d, wkn_d, wv_d, cos_d, sin_d, rT_d,
             zhy_d, qT_d, kT_d, krT_d, vtm_d, nt_ctx, nt_lat):
    C = Ctx(P)
    S = Stage(P)
    mt = load_mod(P, mod_d)
    rT = P.sb("rT", [64, 64]); qgt = P.sb("qgt", [128, 4]); kvgt = P.sb("kvgt", [128, 2])
    wv = P.sb("wv", [128, 2, 1024])
    P.dma("sp", rT[:], rT_d); P.dma("act", qgt[:], qg_d); P.dma("sp", kvgt[:], kvg_d)
    P.dma("act", wv[:], wv_d.rearrange("(k p) n -> p k n", p=128))
    cos_t = P.sb("cos_t", [64, 512]); sin_t = P.sb("sin_t", [64, 512])
    x = P.sb("x", [128, KD, 512]); xin = P.sb("xin", [128, KD, 512])
    zl = P.sb("zl", [128, 6, 512]); sq = P.sb("sq", [128, 6, 512]); zn = P.sb("zn", [128, 6, 512])
    rs = P.sb("rs", [128, 512]); xr = P.sb("xr", [64, 512]); tmp1 = P.sb("tmp1", [128, KD])
    hT3 = h_d.rearrange("(k p) t -> p k t", p=128)
    for (t0, TT, sel) in tiles_for(nt_ctx, nt_lat, 512):
        P.dma("sp", x[:, :, 0:TT], hT3[:, :, t0:t0 + TT])
        P.dma("act", cos_t[:, 0:TT], cos_d[:, t0:t0 + TT])
        P.dma("act", sin_t[:, 0:TT], sin_d[:, t0:t0 + TT])
        modulate(P, xin, x, mt, 0, 1, sel, TT, tmp1)
        for cc in range(31):
            m = 128 if cc < 30 else 64
            ps = dense_chunk(P, C, w_in_d[cc][:, :, 0:m], KD, xin, TT, m=m)
            if cc < 24:
                st = S.get()
                P.copy(S.eng(), st[:, 0:TT], ps[:, 0:TT])
                P.dma(C.q(), zhy_d[cc * 128:(cc + 1) * 128, t0:t0 + TT], st[:, 0:TT], is_output=True)
            elif cc < 30:
                P.copy(S.eng(), zl[:, cc - 24, 0:TT], ps[:, 0:TT])
            else:
                rope_apply2(P, C, S, krT_d[:, t0:t0 + TT], ps[0:64, 0:TT], TT, cos_t, sin_t, rT, xr)
        for (k0, nk, gt) in ((0, 4, qgt), (4, 2, kvgt)):
            pss = col_stats(P, C, zl[:, k0:k0 + nk, :], nk, TT, None, sq)
            rstd_from(P, rs, pss, TT, 1.0 / (nk * 128), RMS_EPS)
            for k in range(nk):
                P.stt("dve", zn[:, k0 + k, 0:TT], zl[:, k0 + k, 0:TT], gt[:, k:k + 1], rs[:, 0:TT], ALU.mult, ALU.mult)
        for h in range(8):
            ps = dense_chunk(P, C, wqn_d[h], 4, zn[:, 0:4, :], TT)
            st = S.get()
            P.copy(S.eng(), st[:, 0:TT], ps[:, 0:TT])
            P.dma(C.q(), qT_d[h, 0:128, t0:t0 + TT], st[:, 0:TT], is_output=True)
            ps = dense_chunk(P, C, wqr_d[h], 4, zn[:, 0:4, :], TT, m=64)
            rope_apply2(P, C, S, qT_d[h, 128:192, t0:t0 + TT], ps[0:64, 0:TT], TT, cos_t, sin_t, rT, xr)
            ps = dense_chunk(P, C, wkn_d[h], 2, zn[:, 4:6, :], TT)
            st = S.get()
            P.copy(S.eng(), st[:, 0:TT], ps[:, 0:TT])
            P.dma(C.q(), kT_d[h, :, t0:t0 + TT], st[:, 0:TT], is_output=True)
        for s0 in range(0, TT, 128):
            for n0 in range(0, 1024, 512):
                ps = C.psum()
                for k in range(2):
                    P.mm(ps[:, :], zn[:, 4 + k, s0:s0 + 128], wv[:, k, n0:n0 + 512], start=(k == 0), stop=(k == 1))
                st = S.get()
                P.copy(S.eng(), st[:, :], ps[:, :])
                P.dma(C.q(), vtm_d[t0 + s0:t0 + s0 + 128, n0:n0 + 512], st[:, :], is_output=True)


def attn_phase(P, qT_d, kT_d, krT_d, vtm_d, mix_d, ident_d, nheads=8):
    ktn = P.sb("ktn", [128, NTOK]); ktr = P.sb("ktr", [64, NTOK])
    va = P.sb("va", [128, NKC, 129])
    ident = P.sb("ident", [128, 128]); P.dma("sp", ident[:], ident_d)
    qn = [P.sb("qn", [128, 512]) for _ in range(2)]
    qr = [P.sb("qr", [64, 512]) for _ in range(2)]
    pts = [P.sb("pt", [128, 512]) for _ in range(3)]
    ps_s = [P.ps("ps_s") for _ in range(3)]
    acc = [P.ps("acc") for _ in range(4)]
    ps_t = P.ps("ps_t")
    rc = P.sb("rc", [128, 4])
    ost = [P.sb("ost", [128, 4, 128]) for _ in range(2)]
    obT = [P.sb("obT", [128, 512]) for _ in range(2)]
    P.memset("pool", va[:, :, 128:129], 1.0)
    P.dma("act", ktr[:], krT_d)
    blocks = [(0, 256, 2)] + [(CTX + i * 512, 512, NKC) for i in range(SEQ // 512)]
    bi = 0
    ci = 0
    for h in range(nheads):
        P.dma("sp", ktn[:], kT_d[h])
        P.dma("sp", va[:, :, 0:128], vtm_d[:, h * 128:(h + 1) * 128].rearrange("(c p) d -> p c d", p=128))
        for (q0, TQ, nkc) in blocks:
            bi += 1
            qnb, qrb, ob, obt = qn[bi % 2], qr[bi % 2], ost[bi % 2], obT[bi % 2]
            P.dma("sp", qnb[:, 0:TQ], qT_d[h, 0:128, q0:q0 + TQ])
            P.dma("act", qrb[:, 0:TQ], qT_d[h, 128:192, q0:q0 + TQ])
            nsub = TQ // 128

            def s_mm(kc):
                ps = ps_s[(ci + kc) % 3]
                P.mm(ps[:, 0:TQ], ktn[:, kc * 128:(kc + 1) * 128], qnb[:, 0:TQ], start=True, stop=False)
                P.mm(ps[:, 0:TQ], ktr[:, kc * 128:(kc + 1) * 128], qrb[:, 0:TQ], start=False, stop=True)

            s_mm(0)
            for kc in range(nkc):
                if kc + 1 < nkc:
                    s_mm(kc + 1)
                ps = ps_s[(ci + kc) % 3]
                pt = pts[(ci + kc) % 3]
                P.act(pt[:, 0:TQ], ps[:, 0:TQ], AF.Exp, scale=MLA_SCALE)
                for s in range(nsub):
                    P.mm(acc[s][:, 0:129], pt[:, s * 128:(s + 1) * 128], va[:, kc, :],
                         start=(kc == 0), stop=(kc == nkc - 1))
            ci += nkc
            for s in range(nsub):
                P.recip(rc[:, s:s + 1], acc[s][:, 128:129])
                P.ts("dve", ob[:, s, :], acc[s][:, 0:128], rc[:, s:s + 1], ALU.mult)
            for s in range(nsub):
                P.transpose(ps_t[:, s * 128:(s + 1) * 128], ob[:, s, :], ident[:])
            P.copy("act", obt[:, 0:TQ], ps_t[:, 0:TQ])
            P.dma("sp", mix_d[1024 + h * 128:1024 + (h + 1) * 128, q0:q0 + TQ], obt[:, 0:TQ], is_output=True)


def tab_to_bf16(P, dst_d, src_d):
    R = 4
    src3 = src_d.rearrange("(c p r) d -> c p r d", p=128, r=R)
    dst3 = dst_d.rearrange("(c p r) d -> c p r d", p=128, r=R)
    fb = [P.sb("tbf", [128, R, D]) for _ in range(2)]
    hb = [P.sb("tbh", [128, R, D], BF16) for _ in range(2)]
    for c in range(16384 // (128 * R)):
        f, hh = fb[c % 2], hb[c % 2]
        P.dma(("sp", "act")[c % 2], f[:], src3[c])
        P.copy(("dve", "pool", "act")[c % 3], hh[:], f[:])
        P.dma(("act", "sp")[c % 2], dst3[c], hh[:], is_output=True)


def copy_phase(P, dst_d, src_d, rows, cols, col0=0):
    bufs = [P.sb("cp", [128, 4096]) for _ in range(2)]
    i = 0
    for r0 in range(0, rows, 128):
        for c0 in range(0, cols, 4096):
            n = min(4096, cols - c0)
            b = bufs[i % 2]; i += 1
            P.dma("sp", b[:, 0:n], src_d[r0:r0 + 128, col0 + c0:col0 + c0 + n])
            P.dma("act", dst_d[r0:r0 + 128, c0:c0 + n], b[:, 0:n], is_output=True)


def v16(a):
    return np.ascontiguousarray(a.reshape(-1, 128).T)


def build_mega(depth=DEPTH, dump=False):
    nc = new_nc()
    dt = lambda name, shape, kind="ExternalInput": nc.dram_tensor(name, list(shape), F32, kind=kind).ap()
    it = lambda name, shape: nc.dram_tensor(name, list(shape), F32).ap()
    nmod = min(DEPTH, depth + 1)
    hT0 = dt("hT0", [D, NTOK]); cT = dt("cT", [128, KD, 2]); modb = dt("modb", [128, 384])
    modw = [dt("modw%d" % l, [96, 128, KD, 128]) for l in range(nmod)]
    ident = dt("ident", [128, 128]); ropec = dt("ropec", [64, NTOK]); ropes = dt("ropes", [64, NTOK]); rT = dt("rT", [64, 64])
    cd = {k: dt(k, shp) for k, shp in HC_SHAPES.items()}
    L = []
    for l in range(depth):
        w = {}
        for n, shp in (("lng", [128, KD]), ("lnb", [128, KD]), ("lfg", [128, KD]), ("lfb", [128, KD]),
                       ("wq", [16, 128, KD, 128]), ("keysT", [128, 16, 128]), ("pu", [16384, D]), ("pv", [16384, D]),
                       ("w_o", [16, 128, KD, 128])):
            w[n] = dt("%s_%d" % (n, l), shp)
        if l % 2 == 0:
            for n, shp in (("w_in", [31, 128, KD, 128]), ("qg", [128, 4]), ("kvg", [128, 2]), ("wqn", [8, 128, 4, 128]),
                           ("wqr", [8, 128, 4, 64]), ("wkn", [8, 128, 2, 128]), ("wv", [256, 1024]),
                           ("hw1", [33, 64]), ("hw23", [2, 64, 64]), ("hbb", [64, 3]), ("hfq", [64, 1]), ("hw4", [64, 4096]),
                           ("convw", [3, 3072]), ("convb", [3072]), ("hbias", [2, 1024]),
                           ("convw_f", [128, 24, 3]), ("convb_f", [128, 24]), ("hbias_f", [128, 2, 8])):
                w[n] = dt("%s_%d" % (n, l), shp)
        else:
            for n, shp in (("s5p", [2, 64, 128, 67]), ("dsk", [128, KD]), ("w_g", [16, 128, KD, 128])):
                w[n] = dt("%s_%d" % (n, l), shp)
        L.append(w)
    out = dt("out", [D, SEQ], "ExternalOutput")
    h = it("h", [D, NTOK]); zhy = it("zhy", [3072, NTOK]); qT = it("qT", [8, 192, NTOK]); kT = it("kT", [8, 128, NTOK])
    krT = it("krT", [64, NTOK]); vtm = it("vtm", [NTOK, 1024]); mix = it("mix", [D, NTOK]); s5in = it("s5in", [D, NTOK])
    modv = it("modv", [DEPTH, 128, 96, 2]); hdnL = it("hdnL", [64, FFT_N]); hdnC = it("hdnC", [64, CTX])
    dumps = {}
    pu16 = [nc.dram_tensor("pu16_%d" % l, [16384, D], BF16).ap() for l in range(depth)] if BF16_TABLES else None
    pv16 = [nc.dram_tensor("pv16_%d" % l, [16384, D], BF16).ap() for l in range(depth)] if BF16_TABLES else None
    stk = ExitStack(); stk.__enter__()
    P = Prog(nc, stk)
    with P.phase():
        copy_phase(P, h, hT0, D, NTOK)
    if BF16_TABLES:
        for l in range(depth):
            with P.phase():
                tab_to_bf16(P, pu16[l], L[l]["pu"])
            with P.phase():
                tab_to_bf16(P, pv16[l], L[l]["pv"])
    with P.phase():
        C = Ctx(P)
        ct = P.sb("ct", [128, KD, 2]); sg = P.sb("sg", [128, KD, 2]); bt = P.sb("bt", [128, 384]); ot = P.sb("ot", [128, 96, 2])
        P.dma("sp", ct[:], cT); P.dma("act", bt[:], modb)
        P.act(sg[:], ct[:], AF.Sigmoid)
        P.tt("dve", ct[:], ct[:], sg[:], ALU.mult)
        for l in range(nmod):
            for cc in range(96):
                ps = dense_chunk(P, C, modw[l][cc], KD, ct, 2)
                P.ts("dve", ot[:, cc, :], ps[:, 0:2], bt[:, l * 96 + cc:l * 96 + cc + 1], ALU.add)
            P.dma("sp", modv[l], ot[:], is_output=True)
    for l in range(depth):
        w = L[l]
        nxt_odd = (l + 1 < DEPTH) and ((l + 1) % 2 == 1)
        if l % 2 == 0:
            with P.phase():
                t1_phase(P, h, modv[l], w["w_in"], w["qg"], w["kvg"], w["wqn"], w["wqr"], w["wkn"], w["wv"], ropec, ropes, rT,
                         zhy, qT, kT, krT, vtm, CTX, SEQ)
            with P.phase():
                hyena_mlp(P, cd["hc_featL"], FFT_N, hdnL, w["hw1"], w["hw23"], w["hbb"], w["hfq"])
            with P.phase():
                hyena_mlp(P, cd["hc_featC"], CTX, hdnC, w["hw1"], w["hw23"], w["hbb"], w["hfq"])
            with P.phase():
                hyena_lat(P, zhy, mix, hdnL, w["hw4"], w["convw"], w["convb"], w["hbias"], cd, CTX)
            with P.phase():
                hyena_ctx(P, zhy, mix, hdnC, w["hw4"], w["convw_f"], w["convb_f"], w["hbias_f"], cd)
            with P.phase():
                attn_phase(P, qT, kT, krT, vtm, mix, ident)
        else:
            with P.phase():
                s5_phase(P, s5in, mix, w["s5p"], ident, CTX, SEQ)
        if dump:
            dumps["mix%d" % l] = dt("dump_mix%d" % l, [D, NTOK], "ExternalOutput")
            with P.phase():
                copy_phase(P, dumps["mix%d" % l], mix, D, NTOK)
        with P.phase():
            post_phase(P, l % 2 == 1, h, mix, modv[l], modv[l + 1] if nxt_odd else None, w["w_o"], w.get("w_g"), w.get("dsk"),
                       w["lng"], w["lnb"], w["lfg"], w["lfb"], w["wq"], w["keysT"], ident,
                       pu16[l] if BF16_TABLES else w["pu"], pv16[l] if BF16_TABLES else w["pv"],
                       s5in if nxt_odd else None, CTX, SEQ, tab_dtype=BF16 if BF16_TABLES else F32)
        if dump:
            dumps["h%d" % l] = dt("dump_h%d" % l, [D, NTOK], "ExternalOutput")
            with P.phase():
                copy_phase(P, dumps["h%d" % l], h, D, NTOK)
    with P.phase():
        copy_phase(P, out, h, D, SEQ, col0=CTX)
    stk.close()
    return nc


def make_inputs(inputs, b, depth=DEPTH):
    g = lambda k: np.asarray(inputs[k], dtype=np.float32)
    nmod = min(DEPTH, depth + 1)
    m = {}
    m["hT0"] = np.ascontiguousarray(np.concatenate([g("ctx")[b], g("x")[b]], 0).T)
    m["cT"] = np.ascontiguousarray(np.stack([g("c")[b], g("c_ctx")], -1).reshape(KD, 128, 2).transpose(1, 0, 2))
    m["modb"] = np.ascontiguousarray(g("mod_b").reshape(DEPTH * 96, 128).T)
    for l in range(nmod):
        m["modw%d" % l] = wl_layout(g("mod_w")[l])
    m["ident"] = np.eye(128, dtype=np.float32)
    cos, sin = rope_tables()
    m["ropec"] = np.ascontiguousarray(np.concatenate([np.ones((CTX, 64), np.float32), cos], 0).T)
    m["ropes"] = np.ascontiguousarray(np.concatenate([np.zeros((CTX, 64), np.float32), sin], 0).T)
    m["rT"] = rope_rT()
    m.update(hyena_consts())
    for l in range(depth):
        i = l // 2
        s = lambda n: "%s_%d" % (n, l)
        m[s("lng")], m[s("lnb")] = v16(g("ln_mix_g")[l]), v16(g("ln_mix_b")[l])
        m[s("lfg")], m[s("lfb")] = v16(g("ln_ffn_g")[l]), v16(g("ln_ffn_b")[l])
        m[s("wq")] = wl_layout(g("peer_w_q")[l])
        m[s("keysT")] = np.ascontiguousarray(g("peer_keys")[l].reshape(16, 128, 128).transpose(2, 0, 1))
        m[s("pu")], m[s("pv")] = g("peer_u")[l], g("peer_v")[l]
        if l % 2 == 0:
            m[s("w_o")] = wl_layout(g("ev_w_o")[i])
            w_in = np.concatenate([g("ev_w_in")[i], np.zeros((D, 64), np.float32)], axis=1)
            m[s("w_in")] = wl_layout(w_in)
            m[s("qg")], m[s("kvg")] = v16(g("mla_q_norm")[i]), v16(g("mla_kv_norm")[i])
            uq = g("mla_w_uq")[i].reshape(512, 8, 192)
            m[s("wqn")] = np.stack([wl_layout(np.ascontiguousarray(uq[:, h, :128]))[0] for h in range(8)])
            m[s("wqr")] = np.stack([np.ascontiguousarray(uq[:, h, 128:].reshape(4, 128, 64).transpose(1, 0, 2)) for h in range(8)])
            ukv = g("mla_w_ukv")[i].reshape(256, 8, 256)
            m[s("wkn")] = np.stack([wl_layout(np.ascontiguousarray(ukv[:, h, :128]))[0] for h in range(8)])
            m[s("wv")] = np.ascontiguousarray(ukv[:, :, 128:].reshape(256, 1024))
            m[s("hw1")] = g("hy_w1")[i]
            m[s("hw23")] = np.ascontiguousarray(np.stack([g("hy_w2")[i], g("hy_w3")[i]]))
            m[s("hbb")] = np.ascontiguousarray(np.stack([g("hy_b1")[i], g("hy_b2")[i], g("hy_b3")[i]], 1))
            m[s("hfq")] = np.ascontiguousarray(g("hy_freq")[i][:, None])
            m[s("hw4")] = g("hy_w4")[i]
            cw, cb, hb = g("ev_conv_w")[i], g("ev_conv_b")[i], g("hy_bias")[i]
            m[s("convw")], m[s("convb")], m[s("hbias")] = cw, cb, hb
            m[s("convw_f")] = np.ascontiguousarray(cw.T.reshape(24, 128, 3).transpose(1, 0, 2))
            m[s("convb_f")] = np.ascontiguousarray(cb.reshape(24, 128).T)
            m[s("hbias_f")] = np.ascontiguousarray(hb.reshape(2, 8, 128).transpose(2, 0, 1))
        else:
            m[s("w_o")] = wl_layout(g("od_w_o")[i])
            m[s("w_g")] = wl_layout(g("od_w_g")[i])
            m[s("dsk")] = v16(g("s5_d")[i])
            m[s("s5p")] = s5_pack_params(g("s5_lam_re")[i], g("s5_lam_im")[i], g("s5_log_dt")[i], g("s5_b_re")[i],
                                         g("s5_b_im")[i], g("s5_c_re")[i], g("s5_c_im")[i])
    return m


def kernel(**inputs):
    nc = build_mega()
    maps = [make_inputs(inputs, b) for b in range(BATCH)]
    res = run_spmd(nc, maps)
    return np.ascontiguousarray(np.stack([res[b]["out"].T for b in range(BATCH)], 0)).astype(np.float32)
```

```python
import math
from contextlib import ExitStack

import numpy as np
import concourse.bass as bass
import concourse.mybir as mybir
from concourse.bass_utils import run_bass_kernel_spmd

F32 = mybir.dt.float32
I32 = mybir.dt.int32
U32 = mybir.dt.uint32
BF16 = mybir.dt.bfloat16
ALU = mybir.AluOpType
AF = mybir.ActivationFunctionType
AX = mybir.AxisListType

NCORES = 8
VERBOSE = False
BF16_TABLES = True
BF16_ATTN = True


class Prog:
    ENGS = ("sp", "act", "pe", "dve", "pool")
    NDMA = 12

    def __init__(self, nc, stack):
        self.nc = nc
        self.stack = stack
        self.ops = {e: [] for e in self.ENGS}
        self.cnt = {e: 0 for e in self.ENGS}
        self.sem = {e: stack.enter_context(nc.semaphore("sem_" + e)) for e in self.ENGS}
        self.known = {e: {} for e in self.ENGS}
        self.dma_sems = {}
        self.dma_n = {}
        for q in ("sp", "act", "pool"):
            self.dma_sems[q] = [stack.enter_context(nc.semaphore("dma_%s_%d" % (q, i))) for i in range(self.NDMA)]
            self.dma_n[q] = 0
        self.dma_uses = {}
        self.access = {}
        self.semobj = {}
        self.out_tokens = []
        self._uid = 0
        self.pstack = stack
        self.consts = {}

    def phase(self):
        prog = self

        class _Ph:
            def __enter__(self_):
                import time as _t
                self_.t0 = _t.time()
                prog.pstack = ExitStack()
                prog.pstack.__enter__()
                prog.consts = {}
                return prog

            def __exit__(self_, et, ev, tb):
                import time as _t
                t1 = _t.time()
                nops = sum(len(v) for v in prog.ops.values())
                if et is None:
                    prog.finish()
                if VERBOSE:
                    print("phase: %d ops, record %.1fs, emit %.1fs, max access list %d" % (
                        nops, t1 - self_.t0, _t.time() - t1, max([len(v) for v in prog.access.values()] + [0])), flush=True)
                prog.pstack.__exit__(et, ev, tb)
                prog.pstack = prog.stack
                prog.ops = {e: [] for e in prog.ENGS}
                prog.access = {}
                return False
        return _Ph()

    def sb(self, name, shape, dtype=F32):
        self._uid += 1
        return self.pstack.enter_context(self.nc.sbuf_tensor("%s_%d" % (name, self._uid), list(shape), dtype))

    def ps(self, name, shape=(128, 512), dtype=F32):
        self._uid += 1
        return self.pstack.enter_context(self.nc.psum_tensor("%s_%d" % (name, self._uid), list(shape), dtype))

    @staticmethod
    def _rng(ap):
        t = ap.tensor
        name = t.name
        pat = ap.ap
        off = ap.offset
        if not isinstance(off, int):
            return name, 0, 1 << 60
        space = str(t.space)
        if "DRAM" in space.upper() or "Dram" in space or "dram" in space:
            lo = hi = off
            for st, n in pat:
                if st >= 0:
                    hi += st * (n - 1)
                else:
                    lo += st * (n - 1)
            return name, lo, hi + 1
        pstep = pat[0][0]
        if pstep <= 0 or "PSUM" in space.upper():
            return name, 0, 1 << 60
        lo = hi = off % pstep
        for st, n in pat[1:]:
            if st >= 0:
                hi += st * (n - 1)
            else:
                lo += st * (n - 1)
        return name, lo, hi + 1

    def _wait_token(self, eng, tok, waits):
        sname, val = tok
        if eng == "pe" and sname == "Epe":
            return
        if self.known[eng].get(sname, 0) >= val:
            return
        self.known[eng][sname] = val
        waits.append((self.semobj[sname], val))

    def _deps(self, eng, reads, writes):
        waits = []
        for ap in reads:
            name, lo, hi = self._rng(ap)
            for ent in self.access.get(name, ()):
                if ent[2] and ent[0] < hi and lo < ent[1]:
                    self._wait_token(eng, ent[3], waits)
        for ap in writes:
            name, lo, hi = self._rng(ap)
            for ent in self.access.get(name, ()):
                if ent[0] < hi and lo < ent[1]:
                    self._wait_token(eng, ent[3], waits)
        return waits

    def _record(self, reads, writes, tok):
        for ap in reads:
            name, lo, hi = self._rng(ap)
            lst = self.access.setdefault(name, [])
            lst[:] = [e for e in lst if not (not e[2] and e[0] == lo and e[1] == hi and e[3][0] == tok[0])]
            lst.append([lo, hi, False, tok])
        for ap in writes:
            name, lo, hi = self._rng(ap)
            lst = self.access.setdefault(name, [])
            lst[:] = [e for e in lst if not (lo <= e[0] and e[1] <= hi)]
            lst.append([lo, hi, True, tok])

    def op(self, eng, fn, reads=(), writes=()):
        waits = self._deps(eng, reads, writes)
        self.cnt[eng] += 1
        sname = "E" + eng
        self.semobj[sname] = self.sem[eng]
        tok = (sname, self.cnt[eng])
        self._record(reads, writes, tok)
        self.ops[eng].append((waits, fn, (self.sem[eng], 1)))
        return tok

    def dma(self, q, out, in_, fn=None, is_output=False, slow=False):
        if slow and fn is None:
            fn = lambda e, out=out, in_=in_: e.dma_start(out=out, in_=in_, allow_slow_non_contiguous=True)
        slot = self.dma_n[q] % self.NDMA
        self.dma_n[q] += 1
        sem = self.dma_sems[q][slot]
        sname = "D%s%d" % (q, slot)
        self.semobj[sname] = sem
        uses = self.dma_uses.get(sname, 0)
        waits = self._deps(q, [in_], [out])
        if uses > 0:
            self._wait_token(q, (sname, 16 * uses), waits)
        self.dma_uses[sname] = uses + 1
        tok = (sname, 16 * (uses + 1))
        self._record([in_], [out], tok)
        if fn is None:
            fn = lambda e, out=out, in_=in_: e.dma_start(out=out, in_=in_)
        self.ops[q].append((waits, fn, (sem, 16)))
        if is_output:
            self.out_tokens.append(tok)
        return tok

    def allgather(self, out, in_, inc=1):
        if not hasattr(self, "cc_sem"):
            self.cc_sem = self.stack.enter_context(self.nc.semaphore("cc_sem"))
            self.cc_n = 0
            self.semobj["CC"] = self.cc_sem
        waits = self._deps("pool", [in_], [out])
        self.cc_n += inc
        tok = ("CC", self.cc_n)
        self._record([in_], [out], tok)
        fn = lambda e: e.collective_compute("AllGather", ALU.bypass, replica_groups=[list(range(NCORES))],
                                            ins=[in_], outs=[out])
        self.ops["pool"].append((waits, fn, (self.cc_sem, inc)))
        return tok

    def finish(self):
        final = []
        for e in self.ENGS:
            if self.cnt[e] > 0:
                final.append(("E" + e, self.cnt[e]))
        for sname, uses in self.dma_uses.items():
            final.append((sname, 16 * uses))
        if hasattr(self, "cc_sem") and self.cc_n > 0:
            final.append(("CC", self.cc_n))
        for e in self.ENGS:
            waits = []
            for tok in final:
                if tok[0] == "E" + e and e != "pe":
                    pass
                self._wait_token_force(e, tok, waits)
            self.ops[e].append((waits, None, None))
        nc = self.nc
        with nc.Block() as block:
            def run(eng_name):
                def body(e):
                    for waits, fn, inc in self.ops[eng_name]:
                        for s, v in waits:
                            e.wait_ge(s, v)
                        if fn is not None:
                            ins = fn(e)
                            ins.then_inc(inc[0], inc[1])
                return body
            block.sync(run("sp"))
            block.scalar(run("act"))
            block.tensor(run("pe"))
            block.vector(run("dve"))
            block.gpsimd(run("pool"))

    def _wait_token_force(self, eng, tok, waits):
        sname, val = tok
        if self.known[eng].get(sname, 0) >= val:
            return
        self.known[eng][sname] = val
        waits.append((self.semobj[sname], val))

    def mm(self, out, lhsT, rhs, start=True, stop=True):
        return self.op("pe", lambda e: e.matmul(out, lhsT, rhs, start=start, stop=stop),
                       reads=[lhsT, rhs] + ([] if start else [out]), writes=[out])

    def transpose(self, out, in_, ident):
        return self.op("pe", lambda e: e.transpose(out, in_, ident), reads=[in_, ident], writes=[out])

    def act(self, out, in_, func, bias=None, scale=None, accum_out=None):
        reads = [in_]
        kw = {}
        if bias is not None:
            kw["bias"] = bias
            if not isinstance(bias, (int, float)):
                reads.append(bias)
        if scale is not None:
            kw["scale"] = scale
            if not isinstance(scale, (int, float)):
                reads.append(scale)
        writes = [out]
        if accum_out is not None:
            kw["accum_out"] = accum_out
            writes.append(accum_out)
        return self.op("act", lambda e: e.activation(out, in_, func, **kw), reads=reads, writes=writes)

    def tt(self, eng, out, in0, in1, op):
        return self.op(eng, lambda e: e.tensor_tensor(out, in0, in1, op), reads=[in0, in1], writes=[out])

    def ts(self, eng, out, in0, s1, op0, s2=None, op1=None, accum_out=None):
        reads = [in0]
        if not isinstance(s1, (int, float)):
            reads.append(s1)
        if s2 is not None and not isinstance(s2, (int, float)):
            reads.append(s2)
        writes = [out]
        kw = {}
        if op1 is not None:
            kw["op1"] = op1
        if accum_out is not None:
            kw["accum_out"] = accum_out
            writes.append(accum_out)
        return self.op(eng, lambda e: e.tensor_scalar(out, in0, s1, s2, op0, **kw), reads=reads, writes=writes)

    def stt(self, eng, out, in0, scalar, in1, op0, op1, accum_out=None):
        reads = [in0, in1]
        if not isinstance(scalar, (int, float)):
            reads.append(scalar)
        writes = [out]
        kw = {}
        if accum_out is not None:
            kw["accum_out"] = accum_out
            writes.append(accum_out)
        return self.op(eng, lambda e: e.scalar_tensor_tensor(out, in0, scalar, in1, op0, op1, **kw),
                       reads=reads, writes=writes)

    def copy(self, eng, out, in_):
        if eng == "act":
            return self.op("act", lambda e: e.copy(out, in_), reads=[in_], writes=[out])
        return self.op(eng, lambda e: e.tensor_copy(out, in_), reads=[in_], writes=[out])

    def memset(self, eng, ap, val):
        return self.op(eng, lambda e: e.memset(ap, val), writes=[ap])

    def recip(self, out, in_):
        return self.op("dve", lambda e: e.reciprocal(out, in_), reads=[in_], writes=[out])


D = 2048
KD = D // 128
BATCH = 2
SEQ = 8192
CTX = 256
DEPTH = 4
TOK = 2112
TILES = [(0, 512, 0), (512, 512, 0), (1024, 512, 0), (1536, 512, 0), (2048, 64, 1)]
HY_W = 1024
IN_W = 3904
DN_ALPHA = (2.0 * DEPTH) ** 0.25
LN_EPS = 1e-5
RMS_EPS = 1e-6
MLA_SCALE = (128 + 64) ** -0.5


def new_nc():
    return bass.Bass("TRN2", target_bir_lowering=False)


def run_spmd(nc, in_maps):
    res = run_bass_kernel_spmd(nc, in_maps, core_ids=list(range(len(in_maps))))
    return res.results


def wl_layout(W):
    K, N = W.shape
    assert K % 128 == 0 and N % 128 == 0
    return np.ascontiguousarray(W.reshape(K // 128, 128, N // 128, 128).transpose(2, 1, 0, 3))


class Ctx:
    def __init__(self, P, ident_ap=None):
        self.P = P
        self.ones = P.sb("ones", [128, 128])
        P.memset("pool", self.ones[:], 1.0)
        self.wbufs = [P.sb("wbuf", [128, KD, 128]) for _ in range(3)]
        self.wi = 0
        self.psums = [P.ps("ps") for _ in range(4)]
        self.pi = 0
        self.qi = 0

    def wbuf(self):
        self.wi += 1
        return self.wbufs[self.wi % len(self.wbufs)]

    def psum(self):
        self.pi += 1
        return self.psums[self.pi % len(self.psums)]

    def q(self):
        self.qi += 1
        return ("sp", "act")[self.qi % 2]


def dense_chunk(P, C, w_ap, nk, x, TT, m=128):
    wb = C.wbuf()
    P.dma(C.q(), wb[:, 0:nk, 0:m], w_ap)
    ps = C.psum()
    for k in range(nk):
        P.mm(ps[0:m, 0:TT], wb[:, k, 0:m], x[:, k, 0:TT], start=(k == 0), stop=(k == nk - 1))
    return ps


def col_stats(P, C, x, nk, TT, scale, sq_tmp, center=False):
    ps = C.psum()
    for k in range(nk):
        P.act(sq_tmp[:, k, 0:TT], x[:, k, 0:TT], AF.Square)
    for k in range(nk):
        P.mm(ps[:, 0:TT], C.ones[:], sq_tmp[:, k, 0:TT], start=(k == 0), stop=(k == nk - 1))
    return ps


def rstd_from(P, out, ps, TT, scale, eps):
    P.act(out[:, 0:TT], ps[:, 0:TT], AF.Sqrt, bias=eps_ap(P, eps), scale=scale)
    P.recip(out[:, 0:TT], out[:, 0:TT])


def eps_ap(P, val):
    key = ("eps", val)
    if key not in P.consts:
        t = P.sb("eps", [128, 1])
        P.memset("pool", t[:], val)
        P.consts[key] = t
    return P.consts[key][:]


def build_mod():
    nc = new_nc()
    NCH = 48
    cT = nc.dram_tensor("cT", [128, KD, 3], F32, kind="ExternalInput").ap()
    w = nc.dram_tensor("w", [NCH, 128, KD, 128], F32, kind="ExternalInput").ap()
    b = nc.dram_tensor("b", [128, NCH], F32, kind="ExternalInput").ap()
    o = nc.dram_tensor("o", [128, NCH, 3], F32, kind="ExternalOutput").ap()
    with ExitStack() as st:
        P = Prog(nc, st)
        C = Ctx(P)
        ct = P.sb("ct", [128, KD, 3])
        sg = P.sb("sg", [128, KD, 3])
        bt = P.sb("bt", [128, NCH])
        ot = P.sb("ot", [128, NCH, 3])
        P.dma("sp", ct[:], cT)
        P.dma("act", bt[:], b)
        P.act(sg[:], ct[:], AF.Sigmoid)
        P.tt("dve", ct[:], ct[:], sg[:], ALU.mult)
        for cc in range(NCH):
            ps = dense_chunk(P, C, w[cc], KD, ct, 3)
            P.ts("dve", ot[:, cc, :], ps[:, 0:3], bt[:, cc:cc + 1], ALU.add)
        P.dma("sp", o, ot[:], is_output=True)
        P.finish()
    return nc


def run_mod(c, c_ctx, mod_w, mod_b):
    nc = build_mod()
    cT = np.stack([c[0], c[1], c_ctx], axis=-1).reshape(KD, 128, 3).transpose(1, 0, 2)
    cT = np.ascontiguousarray(cT)
    wall = mod_w.transpose(1, 0, 2).reshape(D, DEPTH * 6 * D)
    ball = mod_b.reshape(DEPTH * 6 * D)
    in_maps = []
    for core in range(NCORES):
        cols = slice(core * 6144, (core + 1) * 6144)
        in_maps.append({"cT": cT, "w": wl_layout(wall[:, cols]),
                        "b": np.ascontiguousarray(ball[cols].reshape(48, 128).T)})
    res = run_spmd(nc, in_maps)
    m = np.concatenate([r["o"].transpose(1, 0, 2).reshape(6144, 3) for r in res], axis=0)
    return np.ascontiguousarray(m.reshape(DEPTH, 6 * D, 3).transpose(0, 2, 1))


def mod_for_core(m_layer, core):
    b = core // 4
    mm_ = np.stack([m_layer[b], m_layer[2]], axis=-1)
    return np.ascontiguousarray(mm_.reshape(96, 128, 2).transpose(1, 0, 2))


class Stage:
    def __init__(self, P, n=3, shape=(128, 512), dtype=F32):
        self.bufs = [P.sb("stg", list(shape), dtype) for _ in range(n)]
        self.i = 0
        self.e = 0

    def get(self):
        self.i += 1
        return self.bufs[self.i % len(self.bufs)]

    def eng(self):
        self.e += 1
        return ("dve", "act")[self.e % 2]


def load_mod(P, mod_ap):
    mt = P.sb("modt", [128, 96, 2])
    P.dma("sp", mt[:], mod_ap)
    return mt


def modulate(P, out, x, mt, shift_idx, scale_idx, sel, TT, tmp1):
    P.ts("dve", tmp1[:, 0:KD], mt[:, scale_idx * KD:(scale_idx + 1) * KD, sel], 1.0, ALU.add)
    for k in range(KD):
        eng = "dve" if k % 2 == 0 else "pool"
        P.ts(eng, out[:, k, 0:TT], x[:, k, 0:TT], tmp1[:, k:k + 1], ALU.mult,
             mt[:, shift_idx * KD + k, sel:sel + 1], ALU.add)


def rope_apply(P, C, S, out_dram, x_ps, m0, TT, cosT, sinT, rT, xr, t0):
    P.copy("dve", xr[0:64, 0:TT], x_ps)
    ps2 = C.psum()
    P.mm(ps2[0:64, 0:TT], rT[0:64, 0:64], xr[0:64, 0:TT])
    st = S.get()
    P.tt("dve", st[0:64, 0:TT], xr[0:64, 0:TT], cosT[0:64, t0:t0 + TT], ALU.mult)
    P.tt("dve", xr[0:64, 0:TT], ps2[0:64, 0:TT], sinT[0:64, t0:t0 + TT], ALU.mult)
    P.tt("pool", st[0:64, 0:TT], st[0:64, 0:TT], xr[0:64, 0:TT], ALU.add)
    P.dma(C.q(), out_dram, st[0:64, 0:TT], is_output=True)


def build_t1():
    nc = new_nc()
    dt = lambda name, shape, kind="ExternalInput": nc.dram_tensor(name, list(shape), F32, kind=kind).ap()
    hT = dt("hT", [D, TOK])
    mod = dt("mod", [128, 96, 2])
    w_in = dt("w_in", [31, 128, KD, 128])
    qg = dt("qg", [128, 4])
    kvg = dt("kvg", [128, 2])
    wqn = dt("wqn", [8, 128, 4, 128])
    wqr = dt("wqr", [8, 128, 4, 64])
    wkv = dt("wkv", [16, 128, 2, 128])
    cos_d = dt("cosT", [64, TOK])
    sin_d = dt("sinT", [64, TOK])
    rT_d = dt("rT", [64, 64])
    zhy = dt("zhy", [3072, TOK], "ExternalOutput")
    qT = dt("qT", [8, 192, TOK], "ExternalOutput")
    kvT = dt("kvT", [16, 128, TOK], "ExternalOutput")
    krT = dt("krT", [64, TOK], "ExternalOutput")
    with ExitStack() as stck:
        P = Prog(nc, stck)
        C = Ctx(P)
        S = Stage(P)
        mt = load_mod(P, mod)
        cosT = P.sb("cosT", [64, TOK]); sinT = P.sb("sinT", [64, TOK]); rT = P.sb("rT", [64, 64])
        qgt = P.sb("qgt", [128, 4]); kvgt = P.sb("kvgt", [128, 2])
        P.dma("sp", cosT[:], cos_d); P.dma("act", sinT[:], sin_d); P.dma("sp", rT[:], rT_d)
        P.dma("act", qgt[:], qg); P.dma("sp", kvgt[:], kvg)
        x = P.sb("x", [128, KD, 512])
        xin = P.sb("xin", [128, KD, 512])
        zl = P.sb("zl", [128, 6, 512])
        sq = P.sb("sq", [128, 6, 512])
        zn = P.sb("zn", [128, 6, 512])
        rs = P.sb("rs", [128, 512])
        xr = P.sb("xr", [64, 512])
        tmp1 = P.sb("tmp1", [128, KD])
        hT3 = hT.rearrange("(k p) t -> p k t", p=128)
        for (t0, TT, sel) in TILES:
            P.dma("sp", x[:, :, 0:TT], hT3[:, :, t0:t0 + TT])
            modulate(P, xin, x, mt, 0, 1, sel, TT, tmp1)
            for cc in range(31):
                m = 128 if cc < 30 else 64
                ps = dense_chunk(P, C, w_in[cc][:, :, 0:m], KD, xin, TT, m=m)
                if cc < 24:
                    st = S.get()
                    P.copy(S.eng(), st[:, 0:TT], ps[:, 0:TT])
                    P.dma(C.q(), zhy[cc * 128:(cc + 1) * 128, t0:t0 + TT], st[:, 0:TT], is_output=True)
                elif cc < 30:
                    P.copy(S.eng(), zl[:, cc - 24, 0:TT], ps[:, 0:TT])
                else:
                    rope_apply(P, C, S, krT[:, t0:t0 + TT], ps[0:64, 0:TT], 0, TT, cosT, sinT, rT, xr, t0)
            for (k0, nk, gt) in ((0, 4, qgt), (4, 2, kvgt)):
                pss = col_stats(P, C, zl[:, k0:k0 + nk, :], nk, TT, None, sq)
                rstd_from(P, rs, pss, TT, 1.0 / (nk * 128), RMS_EPS)
                for k in range(nk):
                    P.stt("dve", zn[:, k0 + k, 0:TT], zl[:, k0 + k, 0:TT], gt[:, k:k + 1], rs[:, 0:TT],
                          ALU.mult, ALU.mult)
            for h in range(8):
                ps = dense_chunk(P, C, wqn[h], 4, zn[:, 0:4, :], TT)
                st = S.get()
                P.copy(S.eng(), st[:, 0:TT], ps[:, 0:TT])
                P.dma(C.q(), qT[h, 0:128, t0:t0 + TT], st[:, 0:TT], is_output=True)
                ps = dense_chunk(P, C, wqr[h], 4, zn[:, 0:4, :], TT, m=64)
                rope_apply(P, C, S, qT[h, 128:192, t0:t0 + TT], ps[0:64, 0:TT], 0, TT, cosT, sinT, rT, xr, t0)
            for cc in range(16):
                ps = dense_chunk(P, C, wkv[cc], 2, zn[:, 4:6, :], TT)
                st = S.get()
                P.copy(S.eng(), st[:, 0:TT], ps[:, 0:TT])
                P.dma(C.q(), kvT[cc, :, t0:t0 + TT], st[:, 0:TT], is_output=True)
        P.finish()
    return nc


def rope_tables():
    rows = SEQ // 64
    row = np.repeat(np.arange(rows), 64).astype(np.float32)
    col = np.tile(np.arange(64), rows).astype(np.float32)
    half = 32
    inv = (10000.0 ** (-np.arange(0, half, 2, dtype=np.float32) / half)).astype(np.float32)
    ar, ac = row[:, None] * inv, col[:, None] * inv
    ang = np.concatenate([ar, ar, ac, ac], axis=-1)
    return np.cos(ang).astype(np.float32), np.sin(ang).astype(np.float32)


def rope_rT():
    R = np.zeros((64, 64), np.float32)
    for a in range(2):
        for f in range(16):
            R[a * 32 + f, a * 32 + 16 + f] = -1.0
            R[a * 32 + 16 + f, a * 32 + f] = 1.0
    return np.ascontiguousarray(R.T)


def core_tokens_T(h_lat, h_ctx, core):
    b, j = core // 4, core % 4
    return np.ascontiguousarray(np.concatenate([h_lat[b, j * 2048:(j + 1) * 2048], h_ctx[b, j * 64:(j + 1) * 64]], 0).T)


def t1_weight_maps(w_in, q_norm, w_uq, kv_norm, w_ukv):
    w_in_p = np.concatenate([w_in, np.zeros((D, 64), np.float32)], axis=1)
    uq = w_uq.reshape(512, 8, 192)
    wqn = np.stack([wl_layout(np.ascontiguousarray(uq[:, h, :128]))[0] for h in range(8)])
    wqr = np.stack([np.ascontiguousarray(uq[:, h, 128:].reshape(4, 128, 64).transpose(1, 0, 2)) for h in range(8)])
    return {"w_in": wl_layout(w_in_p), "qg": np.ascontiguousarray(q_norm.reshape(4, 128).T),
            "kvg": np.ascontiguousarray(kv_norm.reshape(2, 128).T), "wqn": wqn, "wqr": wqr,
            "wkv": wl_layout(w_ukv), "rT": rope_rT()}


def rope_core(cos, sin, core):
    j = core % 4
    c = np.concatenate([cos[j * 2048:(j + 1) * 2048], np.ones((64, 64), np.float32)], 0).T
    s = np.concatenate([sin[j * 2048:(j + 1) * 2048], np.zeros((64, 64), np.float32)], 0).T
    return np.ascontiguousarray(c), np.ascontiguousarray(s)


NTOK = CTX + SEQ
NKC = NTOK // 128


def build_attn():
    nc = new_nc()
    dt = lambda name, shape, kind="ExternalInput": nc.dram_tensor(name, list(shape), F32, kind=kind).ap()
    QT = dt("QT", [2, 192, NTOK])
    KT = dt("KT", [2, 192, NTOK])
    Vd = dt("V", [2, NTOK, 128])
    att = dt("att", [2, NTOK, 128], "ExternalOutput")
    with ExitStack() as stck:
        P = Prog(nc, stck)
        ktn = P.sb("ktn", [128, NTOK]); ktr = P.sb("ktr", [64, NTOK])
        va = P.sb("va", [128, NKC, 129])
        qn = [P.sb("qn", [128, 512]) for _ in range(2)]
        qr = [P.sb("qr", [64, 512]) for _ in range(2)]
        pts = [P.sb("pt", [128, 512]) for _ in range(3)]
        ps_s = [P.ps("ps_s") for _ in range(3)]
        acc = [P.ps("acc") for _ in range(4)]
        rc = P.sb("rc", [128, 4])
        ost = [P.sb("ost", [128, 4, 128]) for _ in range(2)]
        P.memset("pool", va[:, :, 128:129], 1.0)
        blocks = [(0, 256, 2)] + [(CTX + i * 512, 512, NKC) for i in range(SEQ // 512)]
        bi = 0
        ci = 0
        for h in range(2):
            P.dma("sp", ktn[:], KT[h, 0:128, :])
            P.dma("act", ktr[:], KT[h, 128:192, :])
            P.dma("sp", va[:, :, 0:128], Vd[h].rearrange("(c p) d -> p c d", p=128))
            for (q0, TQ, nkc) in blocks:
                bi += 1
                qnb, qrb, ob = qn[bi % 2], qr[bi % 2], ost[bi % 2]
                P.dma("sp", qnb[:, 0:TQ], QT[h, 0:128, q0:q0 + TQ])
                P.dma("act", qrb[:, 0:TQ], QT[h, 128:192, q0:q0 + TQ])
                nsub = TQ // 128

                def s_mm(kc):
                    ps = ps_s[(ci + kc) % 3]
                    P.mm(ps[:, 0:TQ], ktn[:, kc * 128:(kc + 1) * 128], qnb[:, 0:TQ], start=True, stop=False)
                    P.mm(ps[:, 0:TQ], ktr[:, kc * 128:(kc + 1) * 128], qrb[:, 0:TQ], start=False, stop=True)

                s_mm(0)
                for kc in range(nkc):
                    if kc + 1 < nkc:
                        s_mm(kc + 1)
                    ps = ps_s[(ci + kc) % 3]
                    pt = pts[(ci + kc) % 3]
                    P.act(pt[:, 0:TQ], ps[:, 0:TQ], AF.Exp, scale=MLA_SCALE)
                    for s in range(nsub):
                        P.mm(acc[s][:, 0:129], pt[:, s * 128:(s + 1) * 128], va[:, kc, :],
                             start=(kc == 0), stop=(kc == nkc - 1))
                ci += nkc
                for s in range(nsub):
                    P.recip(rc[:, s:s + 1], acc[s][:, 128:129])
                    P.ts("dve", ob[:, s, :], acc[s][:, 0:128], rc[:, s:s + 1], ALU.mult)
                P.dma("sp", att[h, q0:q0 + TQ, :].rearrange("(s p) d -> p s d", p=128), ob[:, 0:nsub, :],
                      is_output=True)
        P.finish()
    return nc


TT2 = 256
TILES2 = [(i * TT2, TT2, 0) for i in range(2048 // TT2)] + [(2048, 64, 1)]
GELU_C = 2.0 * math.sqrt(2.0 / math.pi)


def gelu_tanh(P, out, x, tmp, eng="dve"):
    P.tt(eng, tmp, x, x, ALU.mult)
    P.ts(eng, tmp, tmp, 0.044715, ALU.mult, 1.0, ALU.add)
    P.tt(eng, tmp, tmp, x, ALU.mult)
    P.act(tmp, tmp, AF.Sigmoid, scale=GELU_C)
    P.tt(eng, out, x, tmp, ALU.mult)


def layer_norm_T(P, C, u, TT, g_ap, b_ap, sq, rs):
    ps = C.psum()
    for k in range(KD):
        P.mm(ps[:, 0:TT], C.ones[:], u[:, k, 0:TT], start=(k == 0), stop=(k == KD - 1))
    P.act(rs[:, 0:TT], ps[:, 0:TT], AF.Copy, scale=-1.0 / D)
    for k in range(KD):
        eng = "dve" if k % 2 == 0 else "pool"
        P.tt(eng, u[:, k, 0:TT], u[:, k, 0:TT], rs[:, 0:TT], ALU.add)
    pss = col_stats(P, C, u, KD, TT, None, sq)
    rstd_from(P, rs, pss, TT, 1.0 / D, LN_EPS)
    for k in range(KD):
        P.tt("dve", u[:, k, 0:TT], u[:, k, 0:TT], rs[:, 0:TT], ALU.mult)
        P.act(u[:, k, 0:TT], u[:, k, 0:TT], AF.Identity, bias=b_ap[:, k:k + 1], scale=g_ap[:, k:k + 1])


class Peer:
    def __init__(self, P, C, keysT_d, ident_d, tab_dtype=F32):
        self.P, self.C = P, C
        self.tab_dtype = tab_dtype
        self.keysT = P.sb("keysT", [128, 16, 128])
        self.ident = P.sb("ident", [128, 128])
        P.dma("sp", self.keysT[:], keysT_d)
        P.dma("act", self.ident[:], ident_d)
        io_i = P.sb("io_i", [128, 256], I32)
        self.iota = P.sb("iota", [128, 256])
        P.op("pool", lambda e: e.iota(io_i[:], [[1, 256]], base=0, channel_multiplier=0), writes=[io_i[:]])
        P.copy("dve", self.iota[:], io_i[:])
        self.qT = P.sb("pq", [128, 16, 128])
        self.sc = P.sb("psc", [128, 16, 128])
        self.sc2 = P.sb("psc2", [128, 128])
        self.top = P.sb("ptop", [128, 16, 16])
        self.ti = P.sb("pti", [128, 16, 16], U32)
        self.tif = P.sb("ptif", [128, 16, 16])
        self.cand = P.sb("pcand", [128, 8, 256])
        self.cand2 = P.sb("pcand2", [128, 256])
        self.fidx = P.sb("pfidx", [128, 8, 256])
        self.c16 = P.sb("pc16", [128, 8, 16])
        self.cj = P.sb("pcj", [128, 8, 16], U32)
        self.cjf = P.sb("pcjf", [128, 8, 16])
        self.junk = P.sb("pjunk", [128, 256])
        self.isel = P.sb("pisel", [128, 128])
        self.isel32 = P.sb("pisel32", [128, 128], I32)
        self.gw = P.sb("pgw", [128, 8, 16])
        self.nmax = P.sb("pnmax", [128, 8])
        self.ssum = P.sb("pssum", [128, 8])
        self.apre = P.sb("papre", [128, 128])
        self.atmp = P.sb("patmp", [128, 128])
        self.wgt = P.sb("pwgt", [128, 128])
        self.xtok = P.sb("pxtok", [128, D])
        self.gbuf = [P.sb("pg", [128, D], tab_dtype) for _ in range(3 if tab_dtype == F32 else 6)]
        self.gi = 0
        self.accs = [P.sb("pacc", [128, D]) for _ in range(2)]
        self.fjunk = P.sb("pfj", [128, D]) if tab_dtype != F32 else None

    def gather(self, tab_d, col):
        P = self.P
        self.gi += 1
        buf = self.gbuf[self.gi % len(self.gbuf)]
        idx_ap = self.isel32[:, col:col + 1]
        P.dma("pool", buf[:], idx_ap,
              fn=lambda e, buf=buf, idx_ap=idx_ap: e.indirect_dma_start(
                  out=buf[:], out_offset=None, in_=tab_d,
                  in_offset=bass.IndirectOffsetOnAxis(ap=idx_ap, axis=0)))
        return buf

    def subtile(self, xin, s0, wq_d, u_d, v_d, fT_out):
        P, C = self.P, self.C
        xs = xin[:, :, s0:s0 + 128]
        for cc in range(16):
            ps = dense_chunk(P, C, wq_d[cc], KD, xs, 128)
            P.copy("act", self.qT[:, cc, :], ps[:, 0:128])
        for g4 in range(4):
            ps = C.psum()
            for j in range(4):
                k = g4 * 4 + j
                P.transpose(ps[:, j * 128:(j + 1) * 128], xs[:, k, :], self.ident[:])
            P.copy("act", self.xtok[:, g4 * 512:(g4 + 1) * 512], ps[:])
        for g4 in range(4):
            ps = C.psum()
            for j in range(4):
                hs = g4 * 4 + j
                P.mm(ps[:, j * 128:(j + 1) * 128], self.qT[:, hs, :], self.keysT[:, hs, :])
            P.copy("dve", self.sc[:, g4 * 4:(g4 + 1) * 4, :], ps[:].rearrange("p (a b) -> p a b", a=4))
        V = lambda fn, r, w: P.op("dve", fn, reads=r, writes=w)
        for hs in range(16):
            sc, sc2, top, ti = self.sc[:, hs, :], self.sc2[:], self.top[:, hs, :], self.ti[:, hs, :]
            V(lambda e, a=top[:, 0:8], b=sc: e.max(a, b), [sc], [top[:, 0:8]])
            V(lambda e, a=ti[:, 0:8], b=top[:, 0:8], c=sc: e.max_index(a, b, c), [top[:, 0:8], sc], [ti[:, 0:8]])
            V(lambda e, a=sc2, b=top[:, 0:8], c=sc: e.match_replace(a, b, c, -1e30), [top[:, 0:8], sc], [sc2])
            V(lambda e, a=top[:, 8:16], b=sc2: e.max(a, b), [sc2], [top[:, 8:16]])
            V(lambda e, a=ti[:, 8:16], b=top[:, 8:16], c=sc2: e.max_index(a, b, c), [top[:, 8:16], sc2], [ti[:, 8:16]])
        P.copy("dve", self.tif[:], self.ti[:])
        for h in range(8):
            c3 = self.cand[:, h, :].rearrange("p (a b) -> p a b", a=16)
            f3 = self.fidx[:, h, :].rearrange("p (a b) -> p a b", a=16)
            t0b = self.top[:, 2 * h, :].unsqueeze(2).to_broadcast([128, 16, 16])
            t1b = self.top[:, 2 * h + 1, :].unsqueeze(1).to_broadcast([128, 16, 16])
            P.tt("dve", c3, t0b, t1b, ALU.add)
            i0b = self.tif[:, 2 * h, :].unsqueeze(2).to_broadcast([128, 16, 16])
            i1b = self.tif[:, 2 * h + 1, :].unsqueeze(1).to_broadcast([128, 16, 16])
            P.stt("dve", f3, i0b, 128.0, i1b, ALU.mult, ALU.add)
            cd, cd2, c16, cj = self.cand[:, h, :], self.cand2[:], self.c16[:, h, :], self.cj[:, h, :]
            V(lambda e, a=c16[:, 0:8], b=cd: e.max(a, b), [cd], [c16[:, 0:8]])
            V(lambda e, a=cj[:, 0:8], b=c16[:, 0:8], c=cd: e.max_index(a, b, c), [c16[:, 0:8], cd], [cj[:, 0:8]])
            V(lambda e, a=cd2, b=c16[:, 0:8], c=cd: e.match_replace(a, b, c, -1e30), [c16[:, 0:8], cd], [cd2])
            V(lambda e, a=c16[:, 8:16], b=cd2: e.max(a, b), [cd2], [c16[:, 8:16]])
            V(lambda e, a=cj[:, 8:16], b=c16[:, 8:16], c=cd2: e.max_index(a, b, c), [c16[:, 8:16], cd2], [cj[:, 8:16]])
        P.copy("dve", self.cjf[:], self.cj[:])
        for h in range(8):
            for k in range(16):
                P.stt("dve", self.junk[:], self.iota[:], self.cjf[:, h, k:k + 1], self.fidx[:, h, :],
                      ALU.is_equal, ALU.mult, accum_out=self.isel[:, h * 16 + k:h * 16 + k + 1])
        P.ts("dve", self.isel[:], self.isel[:], 16383.0, ALU.min, 0.0, ALU.max)
        P.copy("dve", self.isel32[:], self.isel[:])
        P.ts("dve", self.nmax[:], self.c16[:, :, 0], -1.0, ALU.mult)
        for h in range(8):
            P.act(self.gw[:, h, :], self.c16[:, h, :], AF.Exp, bias=self.nmax[:, h:h + 1],
                  accum_out=self.ssum[:, h:h + 1])
        P.recip(self.ssum[:], self.ssum[:])
        P.tt("dve", self.gw[:], self.gw[:], self.ssum[:].unsqueeze(2).to_broadcast([128, 8, 16]), ALU.mult)
        if getattr(self, "stop_after_topk", False):
            return
        for hk in range(getattr(self, "n_gather", 128)):
            buf = self.gather(u_d, hk)
            dst = buf[:] if self.tab_dtype == F32 else self.fjunk[:]
            P.stt("dve", dst, buf[:], 1.0, self.xtok[:], ALU.mult, ALU.mult,
                  accum_out=self.apre[:, hk:hk + 1])
        gelu_tanh(P, self.wgt[:], self.apre[:], self.atmp[:])
        P.tt("dve", self.wgt[:], self.wgt[:], self.gw[:].rearrange("p a b -> p (a b)"), ALU.mult)
        for hk in range(getattr(self, "n_gather", 128)):
            buf = self.gather(v_d, hk)
            j = hk % 2
            eng = "dve"
            acc = self.accs[j]
            if hk < 2:
                P.ts(eng, acc[:], buf[:], self.wgt[:, hk:hk + 1], ALU.mult)
            else:
                P.stt(eng, acc[:], buf[:], self.wgt[:, hk:hk + 1], acc[:], ALU.mult, ALU.add)
        P.tt("pool", self.accs[0][:], self.accs[0][:], self.accs[1][:], ALU.add)
        for g4 in range(4):
            ps = C.psum()
            for j in range(4):
                k = g4 * 4 + j
                P.transpose(ps[:, j * 128:(j + 1) * 128], self.accs[0][:, k * 128:(k + 1) * 128], self.ident[:])
            for j in range(4):
                fT_out(g4 * 4 + j, ps[:, j * 128:(j + 1) * 128])


def tiles_for(nt_ctx, nt_lat, TT):
    out = []
    t = 0
    while t < nt_ctx:
        n = min(TT, nt_ctx - t)
        out.append((t, n, 1))
        t += n
    while t < nt_ctx + nt_lat:
        n = min(TT, nt_ctx + nt_lat - t)
        out.append((t, n, 0))
        t += n
    return out


def load_vec16(P, q, ap16):
    t = P.sb("v16", [128, KD])
    P.dma(q, t[:], ap16)
    return t


def post_phase(P, odd, h_d, mix_d, mod_d, modn_d, w_o_d, w_g_d, dskip_d, lng_d, lnb_d, lfg_d, lfb_d,
               wq_d, keysT_d, ident_d, u_d, v_d, s5in_d, nt_ctx, nt_lat, tab_dtype=F32):
    C = Ctx(P)
    S = Stage(P, n=2, shape=(128, TT2))
    mt = load_mod(P, mod_d)
    mtn = load_mod(P, modn_d) if modn_d is not None else None
    lng, lnb = load_vec16(P, "sp", lng_d), load_vec16(P, "act", lnb_d)
    lfg, lfb = load_vec16(P, "sp", lfg_d), load_vec16(P, "act", lfb_d)
    dsk = load_vec16(P, "sp", dskip_d) if odd else None
    pe = Peer(P, C, keysT_d, ident_d, tab_dtype)
    x = P.sb("x", [128, KD, TT2])
    mx = P.sb("mx", [128, KD, TT2])
    u = P.sb("u", [128, KD, TT2])
    sq = P.sb("sq", [128, KD, TT2])
    rs = P.sb("rs", [128, TT2])
    tmp1 = P.sb("tmp1", [128, KD])
    tch = P.sb("tch", [128, TT2])
    tch2 = P.sb("tch2", [128, TT2])
    h3 = h_d.rearrange("(k p) t -> p k t", p=128)
    m3 = mix_d.rearrange("(k p) t -> p k t", p=128)
    s3 = s5in_d.rearrange("(k p) t -> p k t", p=128) if s5in_d is not None else None
    for (t0, TT, sel) in tiles_for(nt_ctx, nt_lat, TT2):
        P.dma("sp", x[:, :, 0:TT], h3[:, :, t0:t0 + TT])
        P.dma("act", mx[:, :, 0:TT], m3[:, :, t0:t0 + TT])
        m2 = lambda k: mt[:, 2 * KD + k, sel:sel + 1]
        m5 = lambda k: mt[:, 5 * KD + k, sel:sel + 1]
        if odd:
            modulate(P, sq, x, mt, 0, 1, sel, TT, tmp1)
            for k in range(KD):
                P.stt("dve", mx[:, k, 0:TT], sq[:, k, 0:TT], dsk[:, k:k + 1], mx[:, k, 0:TT], ALU.mult, ALU.add)
            for k in range(KD):
                gelu_tanh(P, mx[:, k, 0:TT], mx[:, k, 0:TT], sq[:, k, 0:TT], eng=("dve", "pool")[k % 2])
        for cc in range(KD):
            ps = dense_chunk(P, C, w_o_d[cc], KD, mx, TT)
            if odd:
                ps2 = dense_chunk(P, C, w_g_d[cc], KD, mx, TT)
                P.act(tch2[:, 0:TT], ps2[:, 0:TT], AF.Sigmoid)
                P.stt("dve", tch[:, 0:TT], ps[:, 0:TT], m2(cc), tch2[:, 0:TT], ALU.mult, ALU.mult)
            else:
                P.act(tch[:, 0:TT], ps[:, 0:TT], AF.Copy, scale=m2(cc))
            P.stt("dve", u[:, cc, 0:TT], x[:, cc, 0:TT], DN_ALPHA, tch[:, 0:TT], ALU.mult, ALU.add)
        layer_norm_T(P, C, u, TT, lng, lnb, sq, rs)
        modulate(P, mx, u, mt, 3, 4, sel, TT, tmp1)
        for s0 in range(0, TT, 128):
            def f_out(k, ps, s0=s0):
                P.act(tch[:, 0:128], ps, AF.Copy, scale=m5(k))
                P.stt("dve", u[:, k, s0:s0 + 128], u[:, k, s0:s0 + 128], DN_ALPHA, tch[:, 0:128], ALU.mult, ALU.add)
            pe.subtile(mx, s0, wq_d, u_d, v_d, f_out)
        layer_norm_T(P, C, u, TT, lfg, lfb, sq, rs)
        P.dma("sp", h3[:, :, t0:t0 + TT], u[:, :, 0:TT], is_output=True)
        if s3 is not None:
            modulate(P, sq, u, mtn, 0, 1, sel, TT, tmp1)
            P.dma("act", s3[:, :, t0:t0 + TT], sq[:, :, 0:TT], is_output=True)


S5_T = 512
TWO_PI_LO = 6.28318


def s5_pack_params(lam_re, lam_im, log_dt, b_re, b_im, c_re, c_im):
    ldt = np.repeat(log_dt[:, :, None], 64, axis=2)
    cols = [lam_re[..., None], lam_im[..., None], ldt[..., None], b_re, b_im,
            c_re.transpose(0, 1, 3, 2), c_im.transpose(0, 1, 3, 2)]
    pk = np.concatenate(cols, axis=-1)
    return np.ascontiguousarray(pk.reshape(2, 64, 128, 67))


def s5_phase(P, u_d, y_d, prm_d, ident_d, nt_ctx, nt_lat, n_ct=16):
    NT = nt_ctx + nt_lat
    T = S5_T
    ident = P.sb("ident", [128, 128])
    P.dma("sp", ident[:], ident_d)
    io_i = P.sb("io_i", [128, T + 1], I32)
    iota = P.sb("iota", [128, T + 1])
    P.op("pool", lambda e: e.iota(io_i[:], [[1, T + 1]], base=0, channel_multiplier=0), writes=[io_i[:]])
    P.copy("dve", iota[:], io_i[:])
    ones = P.sb("ones", [128, T])
    P.memset("pool", ones[:], 1.0)
    u = P.sb("u", [128, NT])
    Y0 = P.sb("Y0", [128, NT])
    cosT = [[P.sb("cosT", [128, T + 1]) for d in range(2)] for q in range(4)]
    sinT = [[P.sb("sinT", [128, T + 1]) for d in range(2)] for q in range(4)]
    Bre = [[P.sb("Bre", [128, 128]) for d in range(2)] for q in range(4)]
    Bim = [[P.sb("Bim", [128, 128]) for d in range(2)] for q in range(4)]
    Cre = [[P.sb("Cre", [128, 128]) for d in range(2)] for q in range(4)]
    Cni = [[P.sb("Cni", [128, 128]) for d in range(2)] for q in range(4)]
    rv = [[P.sb("rv", [128, 1]) for d in range(2)] for q in range(4)]
    rb = [[P.sb("rb", [128, T]) for d in range(2)] for q in range(4)]
    init = [[P.sb("init", [128, 2]) for d in range(2)] for q in range(4)]
    prm = P.sb("prm", [128, 67])
    sc = P.sb("sc", [128, 16])
    bb = P.sb("bb", [128, 32])
    bp = P.sb("bp", [128, 2, 128])
    tw = P.sb("tw", [128, T + 1]); tw2 = P.sb("tw2", [128, T + 1]); twi = P.sb("twi", [128, T + 1], I32)
    ps_t = P.ps("ps_t")
    ps_b = [[P.ps("ps_br"), P.ps("ps_bi")] for _ in range(2)]
    ps_y = [P.ps("ps_y") for _ in range(2)]
    W = [dict((n, P.sb(n, [128, T])) for n in ("br", "bi", "t1", "t2", "t3", "t4", "zr", "zi", "sr", "si", "xr", "xi"))
         for _ in range(2)]
    wi = 0
    yi = 0

    def wrap_sin(out, fr):
        P.ts("dve", tw2[:], fr, 0.5, ALU.is_gt)
        P.tt("dve", fr, fr, tw2[:], ALU.subtract)
        P.ts("dve", tw2[:], fr, -0.5, ALU.is_lt)
        P.tt("dve", fr, fr, tw2[:], ALU.add)
        P.act(out, fr, AF.Sin, scale=TWO_PI_LO)

    for ct in range(n_ct):
        P.dma("sp", u[:], u_d[ct * 128:(ct + 1) * 128, :])
        for q in range(4):
            st = ct * 4 + q
            for d in range(2):
                P.dma("act", prm[:], prm_d[d, st])
                lr, li, ld = prm[:, 0:1], prm[:, 1:2], prm[:, 2:3]
                c = lambda i: sc[:, i:i + 1]
                P.act(c(0), ld, AF.Exp)
                P.act(rv[q][d][:], lr, AF.Exp, scale=c(0))
                P.ts("dve", c(1), li, c(0), ALU.mult, 1.0 / (2.0 * math.pi), ALU.mult)
                P.ts("dve", rb[q][d][:], ones[:], rv[q][d][:], ALU.mult)
                P.ts("dve", tw[:], iota[:], c(1), ALU.mult)
                P.copy("dve", twi[:], tw[:])
                P.copy("dve", tw2[:], twi[:])
                P.tt("dve", tw[:], tw[:], tw2[:], ALU.subtract)
                P.ts("dve", cosT[q][d][:], tw[:], 0.25, ALU.add)
                wrap_sin(sinT[q][d][:], tw[:])
                P.copy("dve", tw[:], cosT[q][d][:])
                wrap_sin(cosT[q][d][:], tw[:])
                cos1, sin1 = cosT[q][d][:, 1:2], sinT[q][d][:, 1:2]
                P.ts("dve", c(2), cos1, rv[q][d][:], ALU.mult, -1.0, ALU.add)
                P.ts("dve", c(3), sin1, rv[q][d][:], ALU.mult)
                P.ts("dve", c(4), lr, lr, ALU.mult)
                P.stt("dve", c(4), li, li, c(4), ALU.mult, ALU.add)
                P.recip(c(5), c(4))
                P.ts("dve", c(6), c(2), lr, ALU.mult)
                P.stt("dve", c(6), c(3), li, c(6), ALU.mult, ALU.add)
                P.ts("dve", c(7), c(6), c(5), ALU.mult)
                P.ts("dve", c(8), c(2), li, ALU.mult)
                P.stt("dve", c(8), c(3), lr, c(8), ALU.mult, ALU.subtract)
                P.ts("dve", c(9), c(8), c(5), ALU.mult)
                b_re, b_im, cr, ci = prm[:, 3:19], prm[:, 19:35], prm[:, 35:51], prm[:, 51:67]
                P.ts("dve", bb[:, 0:16], b_im, c(9), ALU.mult)
                P.stt("dve", bb[:, 0:16], b_re, c(7), bb[:, 0:16], ALU.mult, ALU.subtract)
                P.ts("dve", bb[:, 16:32], b_re, c(9), ALU.mult)
                P.stt("dve", bb[:, 16:32], b_im, c(7), bb[:, 16:32], ALU.mult, ALU.add)
                P.memset("pool", bp[:], 0.0)
                P.memset("pool", Cre[q][d][:], 0.0)
                P.memset("pool", Cni[q][d][:], 0.0)
                for gl in range(2):
                    pr = slice(gl * 64, gl * 64 + 64)
                    cs = slice(32 * q + 16 * gl, 32 * q + 16 * gl + 16)
                    P.copy("dve", bp[pr, 0, cs], bb[pr, 0:16])
                    P.copy("dve", bp[pr, 1, cs], bb[pr, 16:32])
                    P.copy("dve", Cre[q][d][pr, cs], cr[pr, :])
                    P.ts("dve", Cni[q][d][pr, cs], ci[pr, :], -1.0, ALU.mult)
                P.transpose(ps_t[:, 0:128], bp[:, 0, :], ident[:])
                P.transpose(ps_t[:, 128:256], bp[:, 1, :], ident[:])
                P.copy("act", Bre[q][d][:], ps_t[:, 0:128])
                P.copy("act", Bim[q][d][:], ps_t[:, 128:256])
                P.memset("pool", init[q][d][:], 0.0)
        for d in range(2):
            chunks = [(0, nt_ctx)] if nt_ctx else []
            lat = [(nt_ctx + i, min(T, nt_lat - i)) for i in range(0, nt_lat, T)]
            chunks = chunks + (lat if d == 0 else lat[::-1])
            for (a, Tc) in chunks:
                yi += 1
                psy = ps_y[yi % 2]
                for q in range(4):
                    wi += 1
                    Wq = W[wi % 2]
                    pbr, pbi = ps_b[wi % 2]
                    cs_, sn_ = cosT[q][d], sinT[q][d]
                    P.mm(pbr[:, 0:Tc], Bre[q][d][:], u[:, a:a + Tc])
                    P.mm(pbi[:, 0:Tc], Bim[q][d][:], u[:, a:a + Tc])
                    if d == 0:
                        P.copy("act", Wq["br"][:, 0:Tc], pbr[:, 0:Tc])
                        P.copy("act", Wq["bi"][:, 0:Tc], pbi[:, 0:Tc])
                    else:
                        P.copy("act", Wq["br"][:, 0:Tc], pbr[:, 0:Tc][:, ::-1])
                        P.copy("act", Wq["bi"][:, 0:Tc], pbi[:, 0:Tc][:, ::-1])
                    X = lambda n: Wq[n][:, 0:Tc]
                    P.tt("pool", X("t1"), cs_[:, 0:Tc], X("br"), ALU.mult)
                    P.tt("pool", X("t2"), sn_[:, 0:Tc], X("bi"), ALU.mult)
                    P.tt("pool", X("xr"), X("t1"), X("t2"), ALU.add)
                    P.tt("pool", X("t3"), cs_[:, 0:Tc], X("bi"), ALU.mult)
                    P.tt("pool", X("t4"), sn_[:, 0:Tc], X("br"), ALU.mult)
                    P.tt("dve", X("xi"), X("t3"), X("t4"), ALU.subtract)
                    i0 = init[q][d]
                    P.op("dve", lambda e, o=X("zr"), a0=rb[q][d][:, 0:Tc], a1=X("xr"), ini=i0[:, 0:1]:
                         e.tensor_tensor_scan(o, a0, a1, ini, ALU.mult, ALU.add),
                         reads=[rb[q][d][:, 0:Tc], X("xr"), i0[:, 0:1]], writes=[X("zr")])
                    P.op("dve", lambda e, o=X("zi"), a0=rb[q][d][:, 0:Tc], a1=X("xi"), ini=i0[:, 1:2]:
                         e.tensor_tensor_scan(o, a0, a1, ini, ALU.mult, ALU.add),
                         reads=[rb[q][d][:, 0:Tc], X("xi"), i0[:, 1:2]], writes=[X("zi")])
                    zr_l, zi_l = Wq["zr"][:, Tc - 1:Tc], Wq["zi"][:, Tc - 1:Tc]
                    ecr, eci = cs_[:, Tc:Tc + 1], sn_[:, Tc:Tc + 1]
                    P.ts("dve", sc[:, 10:11], zi_l, eci, ALU.mult)
                    P.stt("dve", i0[:, 0:1], zr_l, ecr, sc[:, 10:11], ALU.mult, ALU.subtract)
                    P.ts("dve", sc[:, 11:12], zi_l, ecr, ALU.mult)
                    P.stt("dve", i0[:, 1:2], zr_l, eci, sc[:, 11:12], ALU.mult, ALU.add)
                    P.tt("pool", X("t1"), cs_[:, 0:Tc], X("zr"), ALU.mult)
                    P.tt("pool", X("t2"), sn_[:, 0:Tc], X("zi"), ALU.mult)
                    P.tt("dve", X("t3"), sn_[:, 0:Tc], X("zr"), ALU.mult)
                    P.tt("dve", X("t4"), cs_[:, 0:Tc], X("zi"), ALU.mult)
                    so_r = X("sr") if d == 0 else X("sr")[:, ::-1]
                    so_i = X("si") if d == 0 else X("si")[:, ::-1]
                    P.tt("pool", so_r, X("t1"), X("t2"), ALU.subtract)
                    P.tt("dve", so_i, X("t3"), X("t4"), ALU.add)
                    P.mm(psy[:, 0:Tc], Cre[q][d][:], X("sr"), start=(q == 0), stop=False)
                    P.mm(psy[:, 0:Tc], Cni[q][d][:], X("si"), start=False, stop=(q == 3))
                if d == 0:
                    P.copy("act", Y0[:, a:a + Tc], psy[:, 0:Tc])
                else:
                    P.tt("dve", Y0[:, a:a + Tc], Y0[:, a:a + Tc], psy[:, 0:Tc], ALU.add)
        P.dma("sp", y_d[ct * 128:(ct + 1) * 128, :], Y0[:], is_output=True)


HY_G = 16
FFT_N = 2 * SEQ


def hyena_consts():
    a = np.arange(128)
    ang = 2.0 * np.pi * np.outer(a, a) / 128.0
    Cm, Sm = np.cos(ang), np.sin(ang)
    tw = 2.0 * np.pi * np.outer(a, a) / FFT_N
    L = SEQ
    n = np.arange(FFT_N)
    pos = np.where(n < L, n, np.where(n == L, 0, FFT_N - n))

    def feats(Lx, p):
        t = np.linspace(0.0, 1.0, Lx, dtype=np.float32)[:, None]
        w = ((2.0 * math.pi / Lx) * np.arange(Lx, dtype=np.float32))[:, None]
        f = np.linspace(1e-4, 15, 16, dtype=np.float32)[None, :]
        ft = np.concatenate([t, np.cos(f * w), -np.sin(f * w)], axis=-1).astype(np.float32)
        return ft[p], t[p, 0]
    f_lat, t_lat = feats(L, pos)
    perm = (np.arange(128)[None, :] * 128 + np.arange(128)[:, None]).reshape(-1)
    f_ctx, t_ctx = feats(CTX, np.arange(CTX))
    deltas = np.abs(np.linspace(math.log(1e-2) / 1.5, math.log(1e-2) / 0.3, 2 * HY_W, dtype=np.float32))
    c = lambda x: np.ascontiguousarray(x, dtype=np.float32)
    return {
        "hc_FA": c(np.concatenate([Cm, -Sm], 1)), "hc_CS": c(np.concatenate([Cm, Sm], 1)),
        "hc_NSC": c(np.concatenate([-Sm, Cm], 1)), "hc_Cm": c(Cm), "hc_Sm": c(Sm), "hc_NSm": c(-Sm),
        "hc_TWr": c(np.cos(tw)), "hc_TWi": c(-np.sin(tw)),
        "hc_featL": c(f_lat[perm].T), "hc_featC": c(f_ctx.T),
        "hc_tposL": c(t_lat.reshape(128, 128)), "hc_tposC": c(np.tile(t_ctx[None, :], (128, 1))),
        "hc_ndelL": c(np.tile(-deltas[None, :], (128, 1))), "hc_ndelC": c((-deltas).reshape(16, 128).T),
    }


HC_SHAPES = {"hc_FA": [128, 256], "hc_CS": [128, 256], "hc_NSC": [128, 256], "hc_Cm": [128, 128],
             "hc_Sm": [128, 128], "hc_NSm": [128, 128], "hc_TWr": [128, 128], "hc_TWi": [128, 128],
             "hc_featL": [33, FFT_N], "hc_featC": [33, CTX], "hc_tposL": [128, 128], "hc_tposC": [128, CTX],
             "hc_ndelL": [128, 2 * HY_W], "hc_ndelC": [128, 16]}


def hyena_mlp(P, feat_d, npos, hdn_d, w1_d, w23_d, b_d, fq_d):
    w1 = P.sb("mw1", [33, 64]); w23 = P.sb("mw23", [64, 2, 64]); b = P.sb("mb", [64, 3]); fq = P.sb("mfq", [64, 1])
    P.dma("sp", w1[:], w1_d); P.dma("act", w23[:], w23_d.rearrange("l k m -> k l m"))
    P.dma("sp", b[:], b_d); P.dma("act", fq[:], fq_d)
    fq2 = P.sb("mfq2", [64, 1])
    P.ts("dve", fq2[:], fq[:], 1.0 / (2.0 * math.pi), ALU.mult)
    ft = [P.sb("mft", [33, 512]) for _ in range(2)]
    hb = [P.sb("mhb", [64, 512]) for _ in range(3)]
    tw = P.sb("mtw", [64, 512]); tm = P.sb("mtm", [64, 512])
    pss = [P.ps("mps") for _ in range(2)]
    for i, p0 in enumerate(range(0, npos, 512)):
        n = min(512, npos - p0)
        f = ft[i % 2]
        P.dma(("sp", "act")[i % 2], f[:, 0:n], feat_d[:, p0:p0 + n])
        cur = f[0:33, 0:n]
        for layer in range(3):
            ps = pss[(i * 3 + layer) % 2]
            lhsT = w1[:] if layer == 0 else w23[:, layer - 1, :]
            P.mm(ps[0:64, 0:n], lhsT, cur)
            P.ts("dve", tw[:, 0:n], ps[0:64, 0:n], b[:, layer:layer + 1], ALU.add, fq2[:], ALU.mult)
            for _ in range(2):
                P.ts("dve", tm[:, 0:n], tw[:, 0:n], 0.5, ALU.is_gt)
                P.tt("dve", tw[:, 0:n], tw[:, 0:n], tm[:, 0:n], ALU.subtract)
                P.ts("dve", tm[:, 0:n], tw[:, 0:n], -0.5, ALU.is_lt)
                P.tt("dve", tw[:, 0:n], tw[:, 0:n], tm[:, 0:n], ALU.add)
            h = hb[layer]
            P.act(h[:, 0:n], tw[:, 0:n], AF.Sin, scale=TWO_PI_LO)
            cur = h[:, 0:n]
        P.dma("sp", hdn_d[:, p0:p0 + n], cur, is_output=True)


class HyFFT:
    def __init__(self, P, cd):
        self.P = P
        ld = lambda name: self._ld(cd[name], HC_SHAPES[name])
        self.FA, self.CS, self.NSC = ld("hc_FA"), ld("hc_CS"), ld("hc_NSC")
        self.Cm, self.Sm, self.NSm = ld("hc_Cm"), ld("hc_Sm"), ld("hc_NSm")
        self.TWr, self.TWi = ld("hc_TWr"), ld("hc_TWi")
        self.psA = [P.ps("fpA") for _ in range(2)]
        self.psB = [P.ps("fpB") for _ in range(4)]
        self.ia = 0
        self.ib = 0
        self.t = [P.sb("ft", [128, 512]) for _ in range(4)]
        G = HY_G
        self.F = [P.sb("fF", [128, G, 128]) for _ in range(4)]
        self.K = [P.sb("fK", [128, G, 128]) for _ in range(2)]

    def _ld(self, d, shape):
        t = self.P.sb("hc", shape)
        self.P.dma("sp", t[:], d)
        return t

    def cmul_from_psum(self, ps_re, ps_im, br, bi, out_re, out_im, conj=False):
        P = self.P
        t1, t2, t3, t4 = [x[:, 0:1] for x in self.t]
        shp = ps_re.shape
        n = 1
        for s in shp[1:]:
            n *= s
        v = lambda x: x[:, 0:n].rearrange("p (a b) -> p a b", a=shp[1]) if len(shp) == 3 else x[:, 0:n]
        t1, t2, t3, t4 = [v(x) for x in self.t]
        P.tt("dve", t1, ps_re, br, ALU.mult)
        P.tt("dve", t2, ps_im, bi, ALU.mult)
        P.tt("pool", out_re, t1, t2, ALU.add if conj else ALU.subtract)
        P.tt("dve", t3, ps_re, bi, ALU.mult)
        P.tt("dve", t4, ps_im, br, ALU.mult)
        P.tt("pool", out_im, t4, t3, ALU.subtract if conj else ALU.add)

    def fwd(self, x, npart, dst_re, dst_im, kmul=None):
        P = self.P
        G = HY_G
        Ar, Ai = self.F[0], self.F[1]
        for c0 in range(0, G, 2):
            self.ia += 1
            ps = self.psA[self.ia % 2]
            for j in range(2):
                P.mm(ps[:, j * 256:(j + 1) * 256], x[0:npart, c0 + j, :], self.FA[0:npart, :])
            p4 = ps[:].rearrange("p (c r k) -> p c r k", c=2, r=2)
            twr = self.TWr[:].unsqueeze(1).to_broadcast([128, 2, 128])
            twi = self.TWi[:].unsqueeze(1).to_broadcast([128, 2, 128])
            self.cmul_from_psum(p4[:, :, 0, :], p4[:, :, 1, :], twr, twi, Ar[:, c0:c0 + 2, :], Ai[:, c0:c0 + 2, :])
        for c0 in range(0, G, 4):
            self.ib += 1
            pr, pi = self.psB[(2 * self.ib) % 4], self.psB[(2 * self.ib + 1) % 4]
            ar = Ar[:, c0:c0 + 4, :].rearrange("p c k -> p (c k)")
            ai = Ai[:, c0:c0 + 4, :].rearrange("p c k -> p (c k)")
            P.mm(pr[:], self.Cm[:], ar, start=True, stop=False)
            P.mm(pr[:], self.Sm[:], ai, start=False, stop=True)
            P.mm(pi[:], self.Cm[:], ai, start=True, stop=False)
            P.mm(pi[:], self.NSm[:], ar, start=False, stop=True)
            pr3 = pr[:].rearrange("p (c k) -> p c k", c=4)
            pi3 = pi[:].rearrange("p (c k) -> p c k", c=4)
            if kmul is None:
                P.copy("act", dst_re[:, c0:c0 + 4, :], pr3)
                P.copy("act", dst_im[:, c0:c0 + 4, :], pi3)
            else:
                self.cmul_from_psum(pr3, pi3, kmul[0][:, c0:c0 + 4, :], kmul[1][:, c0:c0 + 4, :],
                                    dst_re[:, c0:c0 + 4, :], dst_im[:, c0:c0 + 4, :])

    def inv(self, Yr, Yi, consume):
        P = self.P
        G = HY_G
        Br, Bi = self.F[0], self.F[1]
        for c0 in range(0, G, 2):
            self.ia += 1
            ps = self.psA[self.ia % 2]
            for j in range(2):
                o = ps[:, j * 256:(j + 1) * 256]
                P.mm(o, Yr[:, c0 + j, :], self.CS[:], start=True, stop=False)
                P.mm(o, Yi[:, c0 + j, :], self.NSC[:], start=False, stop=True)
            p4 = ps[:].rearrange("p (c r k) -> p c r k", c=2, r=2)
            twr = self.TWr[:].unsqueeze(1).to_broadcast([128, 2, 128])
            twi = self.TWi[:].unsqueeze(1).to_broadcast([128, 2, 128])
            self.cmul_from_psum(p4[:, :, 0, :], p4[:, :, 1, :], twr, twi, Br[:, c0:c0 + 2, :], Bi[:, c0:c0 + 2, :],
                                conj=True)
        for c0 in range(0, G, 4):
            self.ib += 1
            pr = self.psB[self.ib % 4]
            br = Br[:, c0:c0 + 4, :].rearrange("p c k -> p (c k)")
            bi = Bi[:, c0:c0 + 4, :].rearrange("p c k -> p (c k)")
            P.mm(pr[0:64, :], self.Cm[:, 0:64], br, start=True, stop=False)
            P.mm(pr[0:64, :], self.NSm[:, 0:64], bi, start=False, stop=True)
            consume(c0, pr[0:64, :].rearrange("p (c k) -> p c k", c=4))


def hyena_lat(P, zhy_d, mix_d, hdn_d, w4_d, convw_d, convb_d, hbias_d, cd, t_off, n_groups=HY_W // HY_G):
    G = HY_G
    ff = HyFFT(P, cd)
    ones = P.sb("hones", [128, 128]); P.memset("pool", ones[:], 1.0)
    tpos = P.sb("htpos", [128, 128]); P.dma("sp", tpos[:], cd["hc_tposL"])
    ndel = P.sb("hndel", [128, 2 * HY_W]); P.dma("act", ndel[:], cd["hc_ndelL"])
    w4 = P.sb("hw4", [64, 4096]); P.dma("sp", w4[:], w4_d)
    w4v = w4[:].rearrange("p (d o c) -> p d o c", d=2, o=2)
    T = [P.sb("hT", [128, G, 128]) for _ in range(3)]
    sh = [P.sb("hsh", [64, G, 128]) for _ in range(2)]
    hd = [P.sb("hhd", [64, 2048]) for _ in range(2)]
    cw = P.sb("hcw", [64, 3, 3, G]); cb = P.sb("hcb", [64, 3, G]); hbv = P.sb("hhb", [64, 2, G])
    red = P.sb("hred", [128, G]); rinv = P.sb("hrinv", [128, G])
    psK = [P.ps("hpsK") for _ in range(2)]
    dec = P.sb("hdec", [128, 16, G])
    L = SEQ
    for g in range(n_groups):
        c0 = g * G
        for j in range(3):
            for s in range(3):
                P.dma(("sp", "act")[(j + s) % 2], cw[:, j, s, :],
                      convw_d[s:s + 1, j * HY_W + c0:j * HY_W + c0 + G].to_broadcast([64, G]))
            P.dma("sp", cb[:, j, :], convb_d[j * HY_W + c0:j * HY_W + c0 + G].unsqueeze(0).to_broadcast([64, G]))
        for o in range(2):
            P.dma("act", hbv[:, o, :], hbias_d[o:o + 1, c0:c0 + G].to_broadcast([64, G]))
        for j in range(3):
            row0 = j * HY_W + c0
            src = lambda s: zhy_d[row0:row0 + G, t_off + s:t_off + s + L].rearrange("c (a b) -> a c b", b=128)
            dst = T[j]
            P.dma("sp", dst[0:64, :, :], src(0))
            P.dma("act", sh[0][:, :, 1:128], zhy_d[row0:row0 + G, t_off:t_off + L].rearrange("c (a b) -> a c b", b=128)[:, :, 0:127])
            P.dma("act", sh[0][1:64, :, 0:1], zhy_d[row0:row0 + G, t_off:t_off + L].rearrange("c (a b) -> a c b", b=128)[0:63, :, 127:128], slow=True)
            P.memset("pool", sh[0][0:1, :, 0:1], 0.0)
            P.dma("sp", sh[1][:, :, 0:127], zhy_d[row0:row0 + G, t_off:t_off + L].rearrange("c (a b) -> a c b", b=128)[:, :, 1:128])
            P.memset("pool", sh[1][0:64, :, 127:128], 0.0)
            P.dma("sp", sh[1][0:63, :, 127:128], zhy_d[row0:row0 + G, t_off:t_off + L].rearrange("c (a b) -> a c b", b=128)[1:64, :, 0:1], slow=True)
            bc = lambda a: a.unsqueeze(2).to_broadcast([64, G, 128])
            P.tt("dve", dst[0:64], dst[0:64], bc(cw[:, j, 1, :]), ALU.mult)
            P.tt("pool", sh[0][:], sh[0][:], bc(cw[:, j, 0, :]), ALU.mult)
            P.tt("dve", dst[0:64], dst[0:64], sh[0][:], ALU.add)
            P.tt("pool", sh[1][:], sh[1][:], bc(cw[:, j, 2, :]), ALU.mult)
            P.tt("dve", dst[0:64], dst[0:64], sh[1][:], ALU.add)
            P.tt("dve", dst[0:64], dst[0:64], bc(cb[:, j, :]), ALU.add)
        for o in range(2):
            kk = ff.F[2]
            for n2b in range(0, 128, 16):
                hb_ = hd[(n2b // 16) % 2]
                P.dma(("sp", "act")[(n2b // 16) % 2], hb_[:], hdn_d[:, n2b * 128:(n2b + 16) * 128])
                ps, psb = psK[0], psK[1]
                for j in range(16):
                    P.mm(ps[:, j * G:(j + 1) * G], hb_[:, j * 128:(j + 1) * 128], w4v[:, 0, o, c0:c0 + G])
                    P.mm(psb[:, j * G:(j + 1) * G], hb_[:, j * 128:(j + 1) * 128], w4v[:, 1, o, c0:c0 + G])
                P.tt("dve", dec[:], tpos[:, n2b:n2b + 16].unsqueeze(2).to_broadcast([128, 16, G]),
                     ndel[:, o * HY_W + c0:o * HY_W + c0 + G].unsqueeze(1).to_broadcast([128, 16, G]), ALU.mult)
                P.act(dec[:], dec[:], AF.Exp)
                P.tt("dve", kk[0:64, :, n2b:n2b + 16].rearrange("p c n -> p n c"),
                     ps[0:64, 0:16 * G].rearrange("p (n c) -> p n c", n=16), dec[0:64], ALU.mult)
                P.tt("dve", kk[64:128, :, n2b:n2b + 16].rearrange("p c n -> p n c"),
                     psb[64:128, 0:16 * G].rearrange("p (n c) -> p n c", n=16), dec[64:128], ALU.mult)
            P.op("dve", lambda e, o_=red[:], i_=kk[:]: e.tensor_reduce(o_, i_, AX.X, ALU.add, apply_absolute_value=True),
                 reads=[kk[:]], writes=[red[:]])
            psn = psK[0]
            P.mm(psn[:, 0:G], ones[:], red[:])
            P.recip(rinv[:], psn[:, 0:G])
            P.tt("dve", kk[:], kk[:], rinv[:].unsqueeze(2).to_broadcast([128, G, 128]), ALU.mult)
            P.memset("pool", kk[64:65, :, 0:1], 0.0)
            ff.fwd(kk, 128, ff.K[0], ff.K[1])
            y = T[2]
            gate = T[o]
            ff.fwd(y, 64, ff.F[2], ff.F[3], kmul=(ff.K[0], ff.K[1]))

            def consume(cc, ps3, o=o, y=y, gate=gate):
                t = ff.t[0][0:64, :].rearrange("p (c k) -> p c k", c=4)
                P.tt("pool", t, y[0:64, cc:cc + 4, :], hbv[:, o, cc:cc + 4].unsqueeze(2).to_broadcast([64, 4, 128]), ALU.mult)
                P.stt("dve", t, ps3, 1.0 / FFT_N, t, ALU.mult, ALU.add)
                P.tt("dve", y[0:64, cc:cc + 4, :], t, gate[0:64, cc:cc + 4, :], ALU.mult)
            ff.inv(ff.F[2], ff.F[3], consume)
        P.dma("sp", mix_d[c0:c0 + G, t_off:t_off + L].rearrange("c (a b) -> a c b", b=128), T[2][0:64, :, :], is_output=True)


def hyena_ctx(P, zhy_d, mix_d, hdn_d, w4_d, convw_d, convb_d, hbias_d, cd, n_ct=8):
    Lc = CTX
    w4 = P.sb("cw4", [64, 4096]); P.dma("sp", w4[:], w4_d)
    w4v = w4[:].rearrange("p (d o c) -> p d o c", d=2, o=2)
    hdn = P.sb("chdn", [64, Lc]); P.dma("act", hdn[:], hdn_d)
    tpos = P.sb("ctpos", [128, Lc]); P.dma("sp", tpos[:], cd["hc_tposC"])
    ndel = P.sb("cndel", [128, 16]); P.dma("act", ndel[:], cd["hc_ndelC"])
    cw = P.sb("ccw", [128, 24, 3]); cb = P.sb("ccb", [128, 24]); hbv = P.sb("chb", [128, 2, 8])
    P.dma("sp", cw[:], convw_d); P.dma("act", cb[:], convb_d); P.dma("sp", hbv[:], hbias_d)
    zp = P.sb("czp", [128, Lc + 2])
    X = [P.sb("cX", [128, Lc]) for _ in range(3)]
    hf = [P.sb("chf", [128, Lc]) for _ in range(2)]
    dec = P.sb("cdec", [128, Lc])
    acc = [P.sb("cacc", [128, Lc]) for _ in range(2)]
    rd = P.sb("crd", [128, 4])
    ps = [P.ps("cps") for _ in range(2)]
    P.memset("pool", zp[:], 0.0)
    for ct in range(n_ct):
        for j in range(3):
            k = j * 8 + ct
            P.dma("sp", zp[:, 1:Lc + 1], zhy_d[j * HY_W + ct * 128:j * HY_W + (ct + 1) * 128, 0:Lc])
            P.ts("dve", X[j][:], zp[:, 1:Lc + 1], cw[:, k, 1:2], ALU.mult, cb[:, k:k + 1], ALU.add)
            P.stt("dve", X[j][:], zp[:, 0:Lc], cw[:, k, 0:1], X[j][:], ALU.mult, ALU.add)
            P.stt("dve", X[j][:], zp[:, 2:Lc + 2], cw[:, k, 2:3], X[j][:], ALU.mult, ALU.add)
        y = X[2]
        for o in range(2):
            P.act(dec[:], tpos[:], AF.Exp, scale=ndel[:, o * 8 + ct:o * 8 + ct + 1])
            for d in range(2):
                P.mm(ps[d][:, 0:Lc], w4v[:, d, o, ct * 128:(ct + 1) * 128], hdn[:])
                P.tt("dve", hf[d][:], ps[d][:, 0:Lc], dec[:], ALU.mult)
                P.op("dve", lambda e, o_=rd[:, d:d + 1], i_=hf[d][:]: e.tensor_reduce(o_, i_, AX.X, ALU.add, apply_absolute_value=True),
                     reads=[hf[d][:]], writes=[rd[:, d:d + 1]])
            P.tt("dve", rd[:, 2:3], rd[:, 0:1], rd[:, 1:2], ALU.add)
            P.recip(rd[:, 3:4], rd[:, 2:3])
            for d in range(2):
                P.ts("dve", hf[d][:], hf[d][:], rd[:, 3:4], ALU.mult)
            P.ts("dve", acc[0][:], y[:], hf[0][:, 0:1], ALU.mult)
            P.memset("pool", acc[1][:], 0.0)
            for tau in range(1, Lc):
                n = Lc - tau
                P.stt("dve", acc[0][:, tau:Lc], y[:, 0:n], hf[0][:, tau:tau + 1], acc[0][:, tau:Lc], ALU.mult, ALU.add)
                P.stt("dve", acc[1][:, 0:n], y[:, tau:Lc], hf[1][:, tau:tau + 1], acc[1][:, 0:n], ALU.mult, ALU.add)
            P.tt("pool", acc[0][:], acc[0][:], acc[1][:], ALU.add)
            P.stt("dve", acc[0][:], y[:], hbv[:, o, ct:ct + 1], acc[0][:], ALU.mult, ALU.add)
            P.tt("dve", y[:], acc[0][:], X[o][:], ALU.mult)
        P.dma("sp", mix_d[ct * 128:(ct + 1) * 128, 0:Lc], y[:], is_output=True)


def mod_phase(P, cT_d, modw_d, modb_d, modv_d):
    C = Ctx(P)
    ct = P.sb("ct", [128, KD, 2]); sg = P.sb("sg", [128, KD, 2]); bt = P.sb("bt", [128, 384])
    ot = P.sb("ot", [128, 96, 2])
    P.dma("sp", ct[:], cT_d); P.dma("act", bt[:], modb_d)
    P.act(sg[:], ct[:], AF.Sigmoid)
    P.tt("dve", ct[:], ct[:], sg[:], ALU.mult)
    for l in range(DEPTH):
        for cc in range(96):
            ps = dense_chunk(P, C, modw_d[l][cc], KD, ct, 2)
            P.ts("dve", ot[:, cc, :], ps[:, 0:2], bt[:, l * 96 + cc:l * 96 + cc + 1], ALU.add)
        P.dma("sp", modv_d[l], ot[:], is_output=True)


def rope_apply2(P, C, S, out_dram, x_ps, TT, cos_t, sin_t, rT, xr, S16=None):
    P.copy("dve", xr[0:64, 0:TT], x_ps)
    ps2 = C.psum()
    P.mm(ps2[0:64, 0:TT], rT[0:64, 0:64], xr[0:64, 0:TT])
    st = S.get()
    P.tt("dve", st[0:64, 0:TT], xr[0:64, 0:TT], cos_t[0:64, 0:TT], ALU.mult)
    P.tt("dve", xr[0:64, 0:TT], ps2[0:64, 0:TT], sin_t[0:64, 0:TT], ALU.mult)
    P.tt("pool", st[0:64, 0:TT], st[0:64, 0:TT], xr[0:64, 0:TT], ALU.add)
    if S16 is None:
        P.dma(C.q(), out_dram, st[0:64, 0:TT], is_output=True)
    else:
        t16 = S16.get()
        P.copy("act", t16[0:64, 0:TT], st[0:64, 0:TT])
        P.dma(C.q(), out_dram, t16[0:64, 0:TT], is_output=True)


def t1_phase(P, h_d, mod_d, w_in_d, qg_d, kvg_d, wqn_d, wqr_d, wkn_d, wv_d, cos_d, sin_d, rT_d,
             zhy_d, qT_d, kT_d, krT_d, vtm_d, nt_ctx, nt_lat):
    C = Ctx(P)
    S = Stage(P)
    lowp = qT_d.dtype != F32
    S16 = Stage(P, dtype=qT_d.dtype) if lowp else None
    SQ = S16 if lowp else S
    mt = load_mod(P, mod_d)
    rT = P.sb("rT", [64, 64]); qgt = P.sb("qgt", [128, 4]); kvgt = P.sb("kvgt", [128, 2])
    wv = P.sb("wv", [128, 2, 1024])
    P.dma("sp", rT[:], rT_d); P.dma("act", qgt[:], qg_d); P.dma("sp", kvgt[:], kvg_d)
    P.dma("act", wv[:], wv_d.rearrange("(k p) n -> p k n", p=128))
    cos_t = P.sb("cos_t", [64, 512]); sin_t = P.sb("sin_t", [64, 512])
    x = P.sb("x", [128, KD, 512]); xin = P.sb("xin", [128, KD, 512])
    zl = P.sb("zl", [128, 6, 512]); sq = P.sb("sq", [128, 6, 512]); zn = P.sb("zn", [128, 6, 512])
    rs = P.sb("rs", [128, 512]); xr = P.sb("xr", [64, 512]); tmp1 = P.sb("tmp1", [128, KD])
    hT3 = h_d.rearrange("(k p) t -> p k t", p=128)
    for (t0, TT, sel) in tiles_for(nt_ctx, nt_lat, 512):
        P.dma("sp", x[:, :, 0:TT], hT3[:, :, t0:t0 + TT])
        P.dma("act", cos_t[:, 0:TT], cos_d[:, t0:t0 + TT])
        P.dma("act", sin_t[:, 0:TT], sin_d[:, t0:t0 + TT])
        modulate(P, xin, x, mt, 0, 1, sel, TT, tmp1)
        for cc in range(31):
            m = 128 if cc < 30 else 64
            ps = dense_chunk(P, C, w_in_d[cc][:, :, 0:m], KD, xin, TT, m=m)
            if cc < 24:
                st = S.get()
                P.copy(S.eng(), st[:, 0:TT], ps[:, 0:TT])
                P.dma(C.q(), zhy_d[cc * 128:(cc + 1) * 128, t0:t0 + TT], st[:, 0:TT], is_output=True)
            elif cc < 30:
                P.copy(S.eng(), zl[:, cc - 24, 0:TT], ps[:, 0:TT])
            else:
                rope_apply2(P, C, S, krT_d[:, t0:t0 + TT], ps[0:64, 0:TT], TT, cos_t, sin_t, rT, xr, S16)
        for (k0, nk, gt) in ((0, 4, qgt), (4, 2, kvgt)):
            pss = col_stats(P, C, zl[:, k0:k0 + nk, :], nk, TT, None, sq)
            rstd_from(P, rs, pss, TT, 1.0 / (nk * 128), RMS_EPS)
            for k in range(nk):
                P.stt("dve", zn[:, k0 + k, 0:TT], zl[:, k0 + k, 0:TT], gt[:, k:k + 1], rs[:, 0:TT], ALU.mult, ALU.mult)
        for h in range(8):
            ps = dense_chunk(P, C, wqn_d[h], 4, zn[:, 0:4, :], TT)
            st = SQ.get()
            P.copy(S.eng(), st[:, 0:TT], ps[:, 0:TT])
            P.dma(C.q(), qT_d[h, 0:128, t0:t0 + TT], st[:, 0:TT], is_output=True)
            ps = dense_chunk(P, C, wqr_d[h], 4, zn[:, 0:4, :], TT, m=64)
            rope_apply2(P, C, S, qT_d[h, 128:192, t0:t0 + TT], ps[0:64, 0:TT], TT, cos_t, sin_t, rT, xr, S16)
            ps = dense_chunk(P, C, wkn_d[h], 2, zn[:, 4:6, :], TT)
            st = SQ.get()
            P.copy(S.eng(), st[:, 0:TT], ps[:, 0:TT])
            P.dma(C.q(), kT_d[h, :, t0:t0 + TT], st[:, 0:TT], is_output=True)
        for s0 in range(0, TT, 128):
            for n0 in range(0, 1024, 512):
                ps = C.psum()
                for k in range(2):
                    P.mm(ps[:, :], zn[:, 4 + k, s0:s0 + 128], wv[:, k, n0:n0 + 512], start=(k == 0), stop=(k == 1))
                st = SQ.get()
                P.copy(S.eng(), st[:, :], ps[:, :])
                P.dma(C.q(), vtm_d[t0 + s0:t0 + s0 + 128, n0:n0 + 512], st[:, :], is_output=True)


def attn_phase(P, qT_d, kT_d, krT_d, vtm_d, mix_d, ident_d, nheads=8):
    AD = qT_d.dtype
    ktn = P.sb("ktn", [128, NTOK], AD); ktr = P.sb("ktr", [64, NTOK], AD)
    va = P.sb("va", [128, NKC, 129], AD)
    ident = P.sb("ident", [128, 128]); P.dma("sp", ident[:], ident_d)
    qn = [P.sb("qn", [128, 512], AD) for _ in range(2)]
    qr = [P.sb("qr", [64, 512], AD) for _ in range(2)]
    pts = [P.sb("pt", [128, 512], AD) for _ in range(3)]
    ps_s = [P.ps("ps_s") for _ in range(3)]
    acc = [P.ps("acc") for _ in range(4)]
    ps_t = P.ps("ps_t")
    rc = P.sb("rc", [128, 4])
    ost = [P.sb("ost", [128, 4, 128]) for _ in range(2)]
    obT = [P.sb("obT", [128, 512]) for _ in range(2)]
    P.memset("pool", va[:, :, 128:129], 1.0)
    P.dma("act", ktr[:], krT_d)
    blocks = [(0, 256, 2)] + [(CTX + i * 512, 512, NKC) for i in range(SEQ // 512)]
    bi = 0
    ci = 0
    for h in range(nheads):
        P.dma("sp", ktn[:], kT_d[h])
        P.dma("sp", va[:, :, 0:128], vtm_d[:, h * 128:(h + 1) * 128].rearrange("(c p) d -> p c d", p=128))
        for (q0, TQ, nkc) in blocks:
            bi += 1
            qnb, qrb, ob, obt = qn[bi % 2], qr[bi % 2], ost[bi % 2], obT[bi % 2]
            P.dma("sp", qnb[:, 0:TQ], qT_d[h, 0:128, q0:q0 + TQ])
            P.dma("act", qrb[:, 0:TQ], qT_d[h, 128:192, q0:q0 + TQ])
            nsub = TQ // 128

            def s_mm(kc):
                ps = ps_s[(ci + kc) % 3]
                P.mm(ps[:, 0:TQ], ktn[:, kc * 128:(kc + 1) * 128], qnb[:, 0:TQ], start=True, stop=False)
                P.mm(ps[:, 0:TQ], ktr[:, kc * 128:(kc + 1) * 128], qrb[:, 0:TQ], start=False, stop=True)

            s_mm(0)
            for kc in range(nkc):
                if kc + 1 < nkc:
                    s_mm(kc + 1)
                ps = ps_s[(ci + kc) % 3]
                pt = pts[(ci + kc) % 3]
                P.act(pt[:, 0:TQ], ps[:, 0:TQ], AF.Exp, scale=MLA_SCALE)
                for s in range(nsub):
                    P.mm(acc[s][:, 0:129], pt[:, s * 128:(s + 1) * 128], va[:, kc, :],
                         start=(kc == 0), stop=(kc == nkc - 1))
            ci += nkc
            for s in range(nsub):
                P.recip(rc[:, s:s + 1], acc[s][:, 128:129])
                P.ts("dve", ob[:, s, :], acc[s][:, 0:128], rc[:, s:s + 1], ALU.mult)
            for s in range(nsub):
                P.transpose(ps_t[:, s * 128:(s + 1) * 128], ob[:, s, :], ident[:])
            P.copy("act", obt[:, 0:TQ], ps_t[:, 0:TQ])
            P.dma("sp", mix_d[1024 + h * 128:1024 + (h + 1) * 128, q0:q0 + TQ], obt[:, 0:TQ], is_output=True)


def tab_to_bf16(P, dst_d, src_d):
    R = 4
    src3 = src_d.rearrange("(c p r) d -> c p r d", p=128, r=R)
    dst3 = dst_d.rearrange("(c p r) d -> c p r d", p=128, r=R)
    fb = [P.sb("tbf", [128, R, D]) for _ in range(2)]
    hb = [P.sb("tbh", [128, R, D], BF16) for _ in range(2)]
    for c in range(16384 // (128 * R)):
        f, hh = fb[c % 2], hb[c % 2]
        P.dma(("sp", "act")[c % 2], f[:], src3[c])
        P.copy(("dve", "pool", "act")[c % 3], hh[:], f[:])
        P.dma(("act", "sp")[c % 2], dst3[c], hh[:], is_output=True)


def copy_phase(P, dst_d, src_d, rows, cols, col0=0):
    bufs = [P.sb("cp", [128, 4096]) for _ in range(2)]
    i = 0
    for r0 in range(0, rows, 128):
        for c0 in range(0, cols, 4096):
            n = min(4096, cols - c0)
            b = bufs[i % 2]; i += 1
            P.dma("sp", b[:, 0:n], src_d[r0:r0 + 128, col0 + c0:col0 + c0 + n])
            P.dma("act", dst_d[r0:r0 + 128, c0:c0 + n], b[:, 0:n], is_output=True)


def v16(a):
    return np.ascontiguousarray(a.reshape(-1, 128).T)


def build_mega(depth=DEPTH, dump=False):
    nc = new_nc()
    dt = lambda name, shape, kind="ExternalInput": nc.dram_tensor(name, list(shape), F32, kind=kind).ap()
    it = lambda name, shape: nc.dram_tensor(name, list(shape), F32).ap()
    nmod = min(DEPTH, depth + 1)
    hT0 = dt("hT0", [D, NTOK]); cT = dt("cT", [128, KD, 2]); modb = dt("modb", [128, 384])
    modw = [dt("modw%d" % l, [96, 128, KD, 128]) for l in range(nmod)]
    ident = dt("ident", [128, 128]); ropec = dt("ropec", [64, NTOK]); ropes = dt("ropes", [64, NTOK]); rT = dt("rT", [64, 64])
    cd = {k: dt(k, shp) for k, shp in HC_SHAPES.items()}
    L = []
    for l in range(depth):
        w = {}
        for n, shp in (("lng", [128, KD]), ("lnb", [128, KD]), ("lfg", [128, KD]), ("lfb", [128, KD]),
                       ("wq", [16, 128, KD, 128]), ("keysT", [128, 16, 128]), ("pu", [16384, D]), ("pv", [16384, D]),
                       ("w_o", [16, 128, KD, 128])):
            w[n] = dt("%s_%d" % (n, l), shp)
        if l % 2 == 0:
            for n, shp in (("w_in", [31, 128, KD, 128]), ("qg", [128, 4]), ("kvg", [128, 2]), ("wqn", [8, 128, 4, 128]),
                           ("wqr", [8, 128, 4, 64]), ("wkn", [8, 128, 2, 128]), ("wv", [256, 1024]),
                           ("hw1", [33, 64]), ("hw23", [2, 64, 64]), ("hbb", [64, 3]), ("hfq", [64, 1]), ("hw4", [64, 4096]),
                           ("convw", [3, 3072]), ("convb", [3072]), ("hbias", [2, 1024]),
                           ("convw_f", [128, 24, 3]), ("convb_f", [128, 24]), ("hbias_f", [128, 2, 8])):
                w[n] = dt("%s_%d" % (n, l), shp)
        else:
            for n, shp in (("s5p", [2, 64, 128, 67]), ("dsk", [128, KD]), ("w_g", [16, 128, KD, 128])):
                w[n] = dt("%s_%d" % (n, l), shp)
        L.append(w)
    out = dt("out", [D, SEQ], "ExternalOutput")
    AD = BF16 if BF16_ATTN else F32
    ita = lambda name, shape: nc.dram_tensor(name, list(shape), AD).ap()
    h = it("h", [D, NTOK]); zhy = it("zhy", [3072, NTOK]); qT = ita("qT", [8, 192, NTOK]); kT = ita("kT", [8, 128, NTOK])
    krT = ita("krT", [64, NTOK]); vtm = ita("vtm", [NTOK, 1024]); mix = it("mix", [D, NTOK]); s5in = it("s5in", [D, NTOK])
    modv = it("modv", [DEPTH, 128, 96, 2]); hdnL = it("hdnL", [64, FFT_N]); hdnC = it("hdnC", [64, CTX])
    dumps = {}
    pu16 = [nc.dram_tensor("pu16_%d" % l, [16384, D], BF16).ap() for l in range(depth)] if BF16_TABLES else None
    pv16 = [nc.dram_tensor("pv16_%d" % l, [16384, D], BF16).ap() for l in range(depth)] if BF16_TABLES else None
    stk = ExitStack(); stk.__enter__()
    P = Prog(nc, stk)
    with P.phase():
        copy_phase(P, h, hT0, D, NTOK)
    if BF16_TABLES:
        for l in range(depth):
            with P.phase():
                tab_to_bf16(P, pu16[l], L[l]["pu"])
            with P.phase():
                tab_to_bf16(P, pv16[l], L[l]["pv"])
    with P.phase():
        C = Ctx(P)
        ct = P.sb("ct", [128, KD, 2]); sg = P.sb("sg", [128, KD, 2]); bt = P.sb("bt", [128, 384]); ot = P.sb("ot", [128, 96, 2])
        P.dma("sp", ct[:], cT); P.dma("act", bt[:], modb)
        P.act(sg[:], ct[:], AF.Sigmoid)
        P.tt("dve", ct[:], ct[:], sg[:], ALU.mult)
        for l in range(nmod):
            for cc in range(96):
                ps = dense_chunk(P, C, modw[l][cc], KD, ct, 2)
                P.ts("dve", ot[:, cc, :], ps[:, 0:2], bt[:, l * 96 + cc:l * 96 + cc + 1], ALU.add)
            P.dma("sp", modv[l], ot[:], is_output=True)
    for l in range(depth):
        w = L[l]
        nxt_odd = (l + 1 < DEPTH) and ((l + 1) % 2 == 1)
        if l % 2 == 0:
            with P.phase():
                t1_phase(P, h, modv[l], w["w_in"], w["qg"], w["kvg"], w["wqn"], w["wqr"], w["wkn"], w["wv"], ropec, ropes, rT,
                         zhy, qT, kT, krT, vtm, CTX, SEQ)
            with P.phase():
                hyena_mlp(P, cd["hc_featL"], FFT_N, hdnL, w["hw1"], w["hw23"], w["hbb"], w["hfq"])
            with P.phase():
                hyena_mlp(P, cd["hc_featC"], CTX, hdnC, w["hw1"], w["hw23"], w["hbb"], w["hfq"])
            with P.phase():
                hyena_lat(P, zhy, mix, hdnL, w["hw4"], w["convw"], w["convb"], w["hbias"], cd, CTX)
            with P.phase():
                hyena_ctx(P, zhy, mix, hdnC, w["hw4"], w["convw_f"], w["convb_f"], w["hbias_f"], cd)
            with P.phase():
                attn_phase(P, qT, kT, krT, vtm, mix, ident)
        else:
            with P.phase():
                s5_phase(P, s5in, mix, w["s5p"], ident, CTX, SEQ)
        if dump:
            dumps["mix%d" % l] = dt("dump_mix%d" % l, [D, NTOK], "ExternalOutput")
            with P.phase():
                copy_phase(P, dumps["mix%d" % l], mix, D, NTOK)
        with P.phase():
            post_phase(P, l % 2 == 1, h, mix, modv[l], modv[l + 1] if nxt_odd else None, w["w_o"], w.get("w_g"), w.get("dsk"),
                       w["lng"], w["lnb"], w["lfg"], w["lfb"], w["wq"], w["keysT"], ident,
                       pu16[l] if BF16_TABLES else w["pu"], pv16[l] if BF16_TABLES else w["pv"],
                       s5in if nxt_odd else None, CTX, SEQ, tab_dtype=BF16 if BF16_TABLES else F32)
        if dump:
            dumps["h%d" % l] = dt("dump_h%d" % l, [D, NTOK], "ExternalOutput")
            with P.phase():
                copy_phase(P, dumps["h%d" % l], h, D, NTOK)
    with P.phase():
        copy_phase(P, out, h, D, SEQ, col0=CTX)
    stk.close()
    return nc


def make_inputs(inputs, b, depth=DEPTH):
    g = lambda k: np.asarray(inputs[k], dtype=np.float32)
    nmod = min(DEPTH, depth + 1)
    m = {}
    m["hT0"] = np.ascontiguousarray(np.concatenate([g("ctx")[b], g("x")[b]], 0).T)
    m["cT"] = np.ascontiguousarray(np.stack([g("c")[b], g("c_ctx")], -1).reshape(KD, 128, 2).transpose(1, 0, 2))
    m["modb"] = np.ascontiguousarray(g("mod_b").reshape(DEPTH * 96, 128).T)
    for l in range(nmod):
        m["modw%d" % l] = wl_layout(g("mod_w")[l])
    m["ident"] = np.eye(128, dtype=np.float32)
    cos, sin = rope_tables()
    m["ropec"] = np.ascontiguousarray(np.concatenate([np.ones((CTX, 64), np.float32), cos], 0).T)
    m["ropes"] = np.ascontiguousarray(np.concatenate([np.zeros((CTX, 64), np.float32), sin], 0).T)
    m["rT"] = rope_rT()
    m.update(hyena_consts())
    for l in range(depth):
        i = l // 2
        s = lambda n: "%s_%d" % (n, l)
        m[s("lng")], m[s("lnb")] = v16(g("ln_mix_g")[l]), v16(g("ln_mix_b")[l])
        m[s("lfg")], m[s("lfb")] = v16(g("ln_ffn_g")[l]), v16(g("ln_ffn_b")[l])
        m[s("wq")] = wl_layout(g("peer_w_q")[l])
        m[s("keysT")] = np.ascontiguousarray(g("peer_keys")[l].reshape(16, 128, 128).transpose(2, 0, 1))
        m[s("pu")], m[s("pv")] = g("peer_u")[l], g("peer_v")[l]
        if l % 2 == 0:
            m[s("w_o")] = wl_layout(g("ev_w_o")[i])
            w_in = np.concatenate([g("ev_w_in")[i], np.zeros((D, 64), np.float32)], axis=1)
            m[s("w_in")] = wl_layout(w_in)
            m[s("qg")], m[s("kvg")] = v16(g("mla_q_norm")[i]), v16(g("mla_kv_norm")[i])
            uq = g("mla_w_uq")[i].reshape(512, 8, 192)
            m[s("wqn")] = np.stack([wl_layout(np.ascontiguousarray(uq[:, h, :128]))[0] for h in range(8)])
            m[s("wqr")] = np.stack([np.ascontiguousarray(uq[:, h, 128:].reshape(4, 128, 64).transpose(1, 0, 2)) for h in range(8)])
            ukv = g("mla_w_ukv")[i].reshape(256, 8, 256)
            m[s("wkn")] = np.stack([wl_layout(np.ascontiguousarray(ukv[:, h, :128]))[0] for h in range(8)])
            m[s("wv")] = np.ascontiguousarray(ukv[:, :, 128:].reshape(256, 1024))
            m[s("hw1")] = g("hy_w1")[i]
            m[s("hw23")] = np.ascontiguousarray(np.stack([g("hy_w2")[i], g("hy_w3")[i]]))
            m[s("hbb")] = np.ascontiguousarray(np.stack([g("hy_b1")[i], g("hy_b2")[i], g("hy_b3")[i]], 1))
            m[s("hfq")] = np.ascontiguousarray(g("hy_freq")[i][:, None])
            m[s("hw4")] = g("hy_w4")[i]
            cw, cb, hb = g("ev_conv_w")[i], g("ev_conv_b")[i], g("hy_bias")[i]
            m[s("convw")], m[s("convb")], m[s("hbias")] = cw, cb, hb
            m[s("convw_f")] = np.ascontiguousarray(cw.T.reshape(24, 128, 3).transpose(1, 0, 2))
            m[s("convb_f")] = np.ascontiguousarray(cb.reshape(24, 128).T)
            m[s("hbias_f")] = np.ascontiguousarray(hb.reshape(2, 8, 128).transpose(2, 0, 1))
        else:
            m[s("w_o")] = wl_layout(g("od_w_o")[i])
            m[s("w_g")] = wl_layout(g("od_w_g")[i])
            m[s("dsk")] = v16(g("s5_d")[i])
            m[s("s5p")] = s5_pack_params(g("s5_lam_re")[i], g("s5_lam_im")[i], g("s5_log_dt")[i], g("s5_b_re")[i],
                                         g("s5_b_im")[i], g("s5_c_re")[i], g("s5_c_im")[i])
    return m


def kernel(**inputs):
    nc = build_mega()
    maps = [make_inputs(inputs, b) for b in range(BATCH)]
    res = run_spmd(nc, maps)
    return np.ascontiguousarray(np.stack([res[b]["out"].T for b in range(BATCH)], 0)).astype(np.float32)
```
